# Optimizing a Trainium2 kernel written in Bass

```python
import math
import jax, jax.numpy as jnp
from jax import lax
import numpy as np

D_MODEL = 2048
BATCH = 8
SEQ = 4096
DEPTH = 4

CTX_LEN = 256
GRID_W = 64

DIFF_HEADS = 8
DIFF_HD = 64
DIFF_W = DIFF_HEADS * 2 * DIFF_HD
Q_BLOCK = 128
ROPE_BASE = 10000.0
ROPE_FREQS = DIFF_HD // 4
CONV_W = 1024
CONV_K = 3
RNN_W = 1024
RNN_BLOCKS = 8
RNN_BW = RNN_W // RNN_BLOCKS
RNN_CONV_K = 4
RG_C = 8.0
N_BRANCH = 3
IN_COLS = 3 * DIFF_W + 3 * CONV_W + 2 * RNN_W + N_BRANCH * D_MODEL
FFN_HIDDEN = -(-8 * D_MODEL // (3 * 256)) * 256
N_MOD = 6
EPS = 1e-6

kernel_name = 'hybrid_dit_diffattn_shortconv_rglru'


def rmsnorm(x, g):
    xf = x.astype(jnp.float32)
    y = xf * lax.rsqrt(jnp.mean(xf * xf, axis=-1, keepdims=True) + EPS)
    return (y * g.astype(jnp.float32)).astype(x.dtype)


def modulate(x, shift, scale):
    return x * (1.0 + scale) + shift


def axial_rope_tables(n_tokens):
    rows = n_tokens // GRID_W
    row = jnp.repeat(jnp.arange(rows, dtype=jnp.float32), GRID_W)
    col = jnp.tile(jnp.arange(GRID_W, dtype=jnp.float32), rows)
    inv = ROPE_BASE ** (-jnp.arange(ROPE_FREQS, dtype=jnp.float32) / ROPE_FREQS)
    ang = jnp.stack([row[:, None] * inv, col[:, None] * inv], axis=1)
    return jnp.cos(ang), jnp.sin(ang)


def apply_rope(x, cos, sin):
    xf = x.astype(jnp.float32).reshape(*x.shape[:-1], 2, 2, ROPE_FREQS)
    x1, x2 = xf[..., 0, :], xf[..., 1, :]
    cs, sn = cos[None, :, None, None], sin[None, :, None, None]
    out = jnp.stack([x1 * cs - x2 * sn, x2 * cs + x1 * sn], axis=-2)
    return out.reshape(x.shape).astype(x.dtype)


def diff_attend(q, k, v, lam, lam_init, subln_g):
    s = jnp.einsum('bqhcd,bkhcd->bhcqk', q, k, preferred_element_type=jnp.float32) * (DIFF_HD ** -0.5)
    p = jax.nn.softmax(s, axis=-1)
    w = p[:, :, 0] - lam * p[:, :, 1]
    o = jnp.einsum('bhqk,bkhe->bqhe', w.astype(v.dtype), v)
    o = rmsnorm(o, subln_g) * (1.0 - lam_init)
    return o.reshape(*o.shape[:2], DIFF_W)


def diff_attend_blocked(q, k, v, lam, lam_init, subln_g):
    b, s = q.shape[:2]
    nb = s // Q_BLOCK
    qb = jnp.swapaxes(q.reshape(b, nb, Q_BLOCK, *q.shape[2:]), 0, 1)
    ob = lax.map(lambda qi: diff_attend(qi, k, v, lam, lam_init, subln_g), qb)
    return jnp.swapaxes(ob, 0, 1).reshape(b, s, DIFF_W)


def dwconv(x, w, bias, pad):
    y = lax.conv_general_dilated(x, w[:, None, :].astype(x.dtype), window_strides=(1,), padding=[pad],
                                 dimension_numbers=('NWC', 'WIO', 'NWC'), feature_group_count=x.shape[-1])
    return y + bias


def rglru_coeffs(xr, wa, ba, wx, bx, lam):
    xf = xr.astype(jnp.float32)
    xb = xf.reshape(*xf.shape[:2], RNN_BLOCKS, RNN_BW)
    r = jax.nn.sigmoid(jnp.einsum('bsnj,njk->bsnk', xb, wa.astype(jnp.float32)).reshape(xf.shape) + ba.astype(jnp.float32))
    i = jax.nn.sigmoid(jnp.einsum('bsnj,njk->bsnk', xb, wx.astype(jnp.float32)).reshape(xf.shape) + bx.astype(jnp.float32))
    log_a = -RG_C * r * jax.nn.softplus(-lam.astype(jnp.float32))
    return jnp.exp(log_a), jnp.sqrt(-jnp.expm1(2.0 * log_a)) * (i * xf)


def linear_scan(a, g, reverse, h0=None):
    if h0 is not None:
        first = -1 if reverse else 0
        g = g.at[:, first].add(a[:, first] * h0)

    def combine(left, right):
        a_l, g_l = left
        a_r, g_r = right
        return a_l * a_r, a_r * g_l + g_r

    _, h = lax.associative_scan(combine, (a, g), axis=1, reverse=reverse)
    return h


def split_proj(p):
    o1 = 3 * DIFF_W
    o2 = o1 + 3 * CONV_W
    idx = [DIFF_W, 2 * DIFF_W, o1, o1 + CONV_W, o1 + 2 * CONV_W, o2, o2 + RNN_W, o2 + 2 * RNN_W]
    return jnp.split(p, idx, axis=-1)


def merge(gate_pre, ya, yb, yc, b_merge, w_branch_a, w_branch_b, w_branch_c, w_o):
    g = jax.nn.sigmoid(gate_pre.reshape(*gate_pre.shape[:-1], N_BRANCH, D_MODEL) + b_merge)
    m = g[..., 0, :] * (ya @ w_branch_a) + g[..., 1, :] * (yb @ w_branch_b) + g[..., 2, :] * (yc @ w_branch_c)
    return m @ w_o


def token_mixers(u_lat, u_ctx, cos, sin, layer, need_ctx, w_in, diff_lambda, diff_subln, conv_w, conv_b,
                 rnn_conv_w, rnn_conv_b, rg_wa, rg_ba, rg_wx, rg_bx, rg_lambda, b_merge,
                 w_branch_a, w_branch_b, w_branch_c, w_o):
    b, s, _ = u_lat.shape
    n_ctx = u_ctx.shape[1]
    lam_init = 0.8 - 0.6 * math.exp(-0.3 * layer)
    lq1, lk1, lq2, lk2 = diff_lambda.astype(jnp.float32)
    lam = jnp.exp(jnp.sum(lq1 * lk1)) - jnp.exp(jnp.sum(lq2 * lk2)) + lam_init

    pl = split_proj(u_lat @ w_in)
    pc = split_proj(u_ctx @ w_in)

    q_l = apply_rope(pl[0].reshape(b, s, DIFF_HEADS, 2, DIFF_HD), cos, sin)
    k_l = apply_rope(pl[1].reshape(b, s, DIFF_HEADS, 2, DIFF_HD), cos, sin)
    v_l = pl[2].reshape(b, s, DIFF_HEADS, 2 * DIFF_HD)
    k_c = pc[1].reshape(b, n_ctx, DIFF_HEADS, 2, DIFF_HD)
    v_c = pc[2].reshape(b, n_ctx, DIFF_HEADS, 2 * DIFF_HD)
    k_all = jnp.concatenate([k_c, k_l], axis=1)
    v_all = jnp.concatenate([v_c, v_l], axis=1)
    ya_l = diff_attend_blocked(q_l, k_all, v_all, lam, lam_init, diff_subln)

    yb_l = pl[5] * dwconv(pl[4] * pl[3], conv_w, conv_b, (1, 1))

    xr_l = dwconv(pl[7], rnn_conv_w, rnn_conv_b, (2, 1))
    xr_c = dwconv(pc[7], rnn_conv_w, rnn_conv_b, (2, 1))
    hl_dirs, hc_dirs = [], []
    for d, rev in enumerate((False, True)):
        a_c, g_c = rglru_coeffs(xr_c, rg_wa[d], rg_ba[d], rg_wx[d], rg_bx[d], rg_lambda[d])
        hc = linear_scan(a_c, g_c, rev)
        h_final = hc[:, 0] if rev else hc[:, -1]
        a_l, g_l = rglru_coeffs(xr_l, rg_wa[d], rg_ba[d], rg_wx[d], rg_bx[d], rg_lambda[d])
        hl_dirs.append(linear_scan(a_l, g_l, rev, h_final))
        hc_dirs.append(hc)
    yc_l = jax.nn.gelu(pl[6]) * (hl_dirs[0] + hl_dirs[1]).astype(u_lat.dtype)

    out_l = merge(pl[8], ya_l, yb_l, yc_l, b_merge, w_branch_a, w_branch_b, w_branch_c, w_o)
    if not need_ctx:
        return out_l, None

    q_c = pc[0].reshape(b, n_ctx, DIFF_HEADS, 2, DIFF_HD)
    ya_c = diff_attend(q_c, k_c, v_c, lam, lam_init, diff_subln)
    yb_c = pc[5] * dwconv(pc[4] * pc[3], conv_w, conv_b, (1, 1))
    yc_c = jax.nn.gelu(pc[6]) * (hc_dirs[0] + hc_dirs[1]).astype(u_ctx.dtype)
    out_c = merge(pc[8], ya_c, yb_c, yc_c, b_merge, w_branch_a, w_branch_b, w_branch_c, w_o)
    return out_l, out_c


def swiglu(u, wg, wu, wd):
    return (jax.nn.silu(u @ wg) * (u @ wu)) @ wd


def setup_inputs(seed: int = 0) -> dict:
    key = jax.random.key(seed)
    ks = list(jax.random.split(key, 40))
    counter = [0]

    def nrm(shape, scale):
        k = ks[counter[0]]
        counter[0] += 1
        return jax.random.normal(k, shape, jnp.float32) * scale

    D = D_MODEL
    inp = {}
    inp['x'] = nrm((BATCH, SEQ, D), 1.0)
    inp['c'] = nrm((BATCH, D), 1.0)
    inp['ctx'] = nrm((BATCH, CTX_LEN, D), 1.0)
    inp['c_ctx'] = nrm((D,), 1.0)
    inp['w_ada'] = nrm((DEPTH, D, N_MOD * D), 0.5 * D ** -0.5)
    inp['b_ada'] = nrm((DEPTH, N_MOD, D), 0.02)
    inp['g_pre_mix'] = 1.0 + nrm((DEPTH, D), 0.05)
    inp['g_post_mix'] = 1.0 + nrm((DEPTH, D), 0.05)
    inp['g_pre_ffn'] = 1.0 + nrm((DEPTH, D), 0.05)
    inp['g_post_ffn'] = 1.0 + nrm((DEPTH, D), 0.05)
    inp['w_in'] = nrm((DEPTH, D, IN_COLS), D ** -0.5)
    inp['diff_lambda'] = nrm((DEPTH, 4, DIFF_HD), 0.1)
    inp['diff_subln'] = 1.0 + nrm((DEPTH, 2 * DIFF_HD), 0.05)
    inp['conv_w'] = nrm((DEPTH, CONV_K, CONV_W), CONV_K ** -0.5)
    inp['conv_b'] = nrm((DEPTH, CONV_W), 0.02)
    inp['rnn_conv_w'] = nrm((DEPTH, RNN_CONV_K, RNN_W), RNN_CONV_K ** -0.5)
    inp['rnn_conv_b'] = nrm((DEPTH, RNN_W), 0.02)
    inp['rg_wa'] = nrm((DEPTH, 2, RNN_BLOCKS, RNN_BW, RNN_BW), RNN_BW ** -0.5)
    inp['rg_ba'] = nrm((DEPTH, 2, RNN_W), 0.02)
    inp['rg_wx'] = nrm((DEPTH, 2, RNN_BLOCKS, RNN_BW, RNN_BW), RNN_BW ** -0.5)
    inp['rg_bx'] = nrm((DEPTH, 2, RNN_W), 0.02)
    u = jax.random.uniform(ks[counter[0]], (DEPTH, 2, RNN_W), jnp.float32, 0.9, 0.999)
    counter[0] += 1
    a0 = u ** (1.0 / RG_C)
    inp['rg_lambda'] = jnp.log(a0) - jnp.log1p(-a0)
    inp['b_merge'] = nrm((DEPTH, N_BRANCH, D), 0.02)
    inp['w_branch_a'] = nrm((DEPTH, DIFF_W, D), DIFF_W ** -0.5)
    inp['w_branch_b'] = nrm((DEPTH, CONV_W, D), CONV_W ** -0.5)
    inp['w_branch_c'] = nrm((DEPTH, RNN_W, D), RNN_W ** -0.5)
    inp['w_o'] = nrm((DEPTH, D, D), D ** -0.5)
    inp['w_ffn_gate'] = nrm((DEPTH, D, FFN_HIDDEN), D ** -0.5)
    inp['w_ffn_up'] = nrm((DEPTH, D, FFN_HIDDEN), D ** -0.5)
    inp['w_ffn_down'] = nrm((DEPTH, FFN_HIDDEN, D), FFN_HIDDEN ** -0.5)
    return inp


def reference(x, c, ctx, c_ctx, w_ada, b_ada, g_pre_mix, g_post_mix, g_pre_ffn, g_post_ffn, w_in,
              diff_lambda, diff_subln, conv_w, conv_b, rnn_conv_w, rnn_conv_b, rg_wa, rg_ba, rg_wx, rg_bx,
              rg_lambda, b_merge, w_branch_a, w_branch_b, w_branch_c, w_o, w_ffn_gate, w_ffn_up, w_ffn_down):
    b, s, d = x.shape
    cos, sin = axial_rope_tables(s)
    sc = jax.nn.silu(c)
    scc = jax.nn.silu(c_ctx)
    h_lat, h_ctx = x, ctx
    for l in range(DEPTH):
        need_ctx = l < DEPTH - 1
        ml = (sc @ w_ada[l]).reshape(b, N_MOD, 1, d) + b_ada[l][:, None, :]
        mc = (scc @ w_ada[l]).reshape(N_MOD, 1, d) + b_ada[l][:, None, :]
        u_lat = modulate(rmsnorm(h_lat, g_pre_mix[l]), ml[:, 0], ml[:, 1])
        u_ctx = modulate(rmsnorm(h_ctx, g_pre_mix[l]), mc[0], mc[1])
        mix_l, mix_c = token_mixers(u_lat, u_ctx, cos, sin, l, need_ctx, w_in[l], diff_lambda[l], diff_subln[l],
                                    conv_w[l], conv_b[l], rnn_conv_w[l], rnn_conv_b[l], rg_wa[l], rg_ba[l],
                                    rg_wx[l], rg_bx[l], rg_lambda[l], b_merge[l], w_branch_a[l], w_branch_b[l],
                                    w_branch_c[l], w_o[l])
        h_lat = h_lat + ml[:, 2] * rmsnorm(mix_l, g_post_mix[l])
        u = modulate(rmsnorm(h_lat, g_pre_ffn[l]), ml[:, 3], ml[:, 4])
        h_lat = h_lat + ml[:, 5] * rmsnorm(swiglu(u, w_ffn_gate[l], w_ffn_up[l], w_ffn_down[l]), g_post_ffn[l])
        if need_ctx:
            h_ctx = h_ctx + mc[2] * rmsnorm(mix_c, g_post_mix[l])
            uc = modulate(rmsnorm(h_ctx, g_pre_ffn[l]), mc[3], mc[4])
            h_ctx = h_ctx + mc[5] * rmsnorm(swiglu(uc, w_ffn_gate[l], w_ffn_up[l], w_ffn_down[l]), g_post_ffn[l])
    return h_lat
```

```python
import math
from contextlib import ExitStack
import numpy as np
import concourse.bass as bass
import concourse.mybir as mybir
from concourse.bass_utils import run_bass_kernel_spmd

F32, BF16 = mybir.dt.float32, mybir.dt.bfloat16
ALU = mybir.AluOpType
AF = mybir.ActivationFunctionType
AX = mybir.AxisListType
PE, ACT, DVE, POOL, SP = "tensor", "scalar", "vector", "gpsimd", "sync"

D = 2048
NCK = 16
HEADS = 8
FFH = 5632
NJ = 44
EPS = 1e-6
RG_C = 8.0
NV = 330
WTOT = 78643200
OFF_IN = 0
OFF_BR = OFF_IN + 32 * 1048576
OFF_WO = OFF_BR + 8 * 786432
OFF_GU = OFF_WO + 4 * 1048576
OFF_WD = OFF_GU + 22 * 1048576
assert OFF_WD + 16 * 720896 == WTOT
V_BADA, V_GPM, V_GQM, V_GPF, V_GQF = 0, 96, 112, 128, 144
V_CW, V_CB, V_RW, V_RB, V_BA, V_BX, V_LAM, V_BM, V_SUB = 160, 184, 192, 224, 232, 248, 264, 280, 328
C_Q, C_K, C_CX, C_CC, C_CB, C_RG, C_RX, C_MG = 0, 8, 16, 24, 32, 40, 48, 56
NPT = 104
UNIT = 256


class Buf:
    __slots__ = ("name", "w", "rs", "sem", "cnt")

    def __init__(self, name):
        self.name = name
        self.w = None
        self.rs = []
        self.sem = None
        self.cnt = 0


class Op:
    __slots__ = ("eng", "fn", "deps", "need", "owner", "ndma", "sig", "idx_")


class Prog:
    SEM_LIMIT = 30000

    def __init__(self):
        self.ops = []
        self.last = {}
        self.dma_last = {}

    def _add(self, eng, fn, reads, writes, owner, ndma):
        op = Op()
        op.eng, op.fn, op.owner, op.ndma, op.need, op.sig = eng, fn, owner, ndma, False, None
        deps = {}
        is_dma = owner is not None

        def dep(p):
            if p is None:
                return
            if (not is_dma) and p.owner is None and p.eng == eng and eng == PE:
                return
            deps[id(p)] = p

        for b in reads:
            dep(b.w)
        for b in writes:
            dep(b.w)
            for r in b.rs:
                dep(r)
        best = {}
        out = []
        for p in deps.values():
            if p.owner is None:
                q = best.get(p.eng)
                if q is None or p.idx_ > q.idx_:
                    best[p.eng] = p
            else:
                out.append(p)
        out.extend(best.values())
        for p in out:
            if p.owner is None:
                p.need = True
        op.deps = out
        op.idx_ = len(self.ops)
        for b in reads:
            if not is_dma:
                b.rs = [r for r in b.rs if not (r.owner is None and r.eng == eng)]
            b.rs.append(op)
        for b in writes:
            b.w = op
            b.rs = []
        self.ops.append(op)
        if is_dma:
            self.dma_last[id(owner)] = op
        else:
            self.last[eng] = op
        return op

    def op(self, eng, fn, reads=(), writes=()):
        return self._add(eng, fn, reads, writes, None, 0)

    def seq(self, eng, fns, reads=(), writes=()):
        r = None
        for f in fns:
            r = self._add(eng, f, reads, writes, None, 0)
        return r

    def dma(self, q, fn, reads, writes, owner, ndma=1):
        return self._add(q, fn, reads, writes, owner, ndma)

    def snapshot(self):
        snap = list(self.last.values()) + list(self.dma_last.values())
        for p in snap:
            if p.owner is None:
                p.need = True
        return snap

    def phase_begin(self, bufs):
        snap = self.snapshot()
        for b in bufs:
            b.w = None
            b.rs = list(snap)

    def emit(self, nc):
        sem_names = []
        eng_state = {}
        for op in self.ops:
            if op.owner is not None:
                o = op.owner
                if o.sem is None:
                    o.sem = len(sem_names)
                    sem_names.append("d%d" % o.sem)
                o.cnt += 16 * op.ndma
                op.sig = (o.sem, o.cnt)
            elif op.need:
                st = eng_state.get(op.eng)
                if st is None or st[1] >= self.SEM_LIMIT:
                    st = [len(sem_names), 0]
                    sem_names.append("e%d" % st[0])
                    eng_state[op.eng] = st
                st[1] += 1
                op.sig = (st[0], st[1])
        per = {}
        for op in self.ops:
            per.setdefault(op.eng, []).append(op)
        self.nsem = len(sem_names)
        with ExitStack() as es:
            sems = [es.enter_context(nc.semaphore(n)) for n in sem_names]
            block = es.enter_context(nc.Block())

            def run(engname):
                def body(e):
                    waited = {}
                    for op in per.get(engname, []):
                        for p in op.deps:
                            s, v = p.sig
                            if waited.get(s, 0) < v:
                                e.wait_ge(sems[s], v)
                                waited[s] = v
                        if op.fn is None:
                            continue
                        r = op.fn(e)
                        if op.owner is not None:
                            assert len(r) == op.ndma, (len(r), op.ndma)
                            for ins in r:
                                ins.then_inc(sems[op.sig[0]], 16)
                        elif op.sig is not None:
                            r.then_inc(sems[op.sig[0]], 1)
                return body

            block.tensor(run(PE))
            block.scalar(run(ACT))
            block.vector(run(DVE))
            block.gpsimd(run(POOL))
            block.sync(run(SP))


class Arena:
    def __init__(self, ap, base, limit):
        self.ap, self.base, self.cur, self.limit = ap, base, base, limit

    def reset(self):
        self.cur = self.base

    def f32(self, n):
        a = self.ap[:, self.cur:self.cur + n]
        self.cur += n
        assert self.cur <= self.limit, ("arena overflow", self.cur, self.limit)
        return a

    def bf16(self, n):
        m = (n + 1) // 2
        a = self.ap[:, self.cur:self.cur + m].bitcast(BF16)
        self.cur += m
        assert self.cur <= self.limit, ("arena overflow", self.cur, self.limit)
        return a[:, 0:n]


def build_program(S, CTX, L, LAYER0=0, DEBUG=False):
    NT = CTX + S
    NCH = NT // 128
    NU = NT // UNIT
    TM = 256
    nc = bass.Bass("TRN2", target_bir_lowering=False)
    P = Prog()

    def dram_in(name, shape, dt=F32):
        return nc.dram_tensor(name, list(shape), dt, kind="ExternalInput").ap()

    xb = dram_in("xb", [NT, D])
    cvec = dram_in("cvec", [128, 2 * NCK])
    wada = dram_in("wada", [L, 24, 128, NCK * 512])
    wpack = dram_in("wpack", [L, WTOT // 2048, 2048])
    rgw = dram_in("rgw", [L, 128, 32 * 128])
    vecs_d = dram_in("vecs", [128, L * NV])
    dlam = dram_in("dlam", [L * 256])
    rope = dram_in("rope", [2, 128, S])
    ident_d = dram_in("ident", [128, 128])
    out = nc.dram_tensor("out", [S, D], F32, kind="ExternalOutput").ap()

    IK = "ExternalOutput" if DEBUG else "Internal"
    hT = nc.dram_tensor("hT", [NCK, 128, NT], F32, kind=IK).ap()
    wbf = [nc.dram_tensor("wbf%d" % l, [WTOT // 2048, 2048], BF16, kind="Internal").ap() for l in range(L)]
    pT = nc.dram_tensor("pT", [NPT, 128, NT], BF16, kind=IK).ap()
    vS = nc.dram_tensor("vS", [HEADS, 128, NCH, 128], BF16, kind=IK).ap()
    yT = nc.dram_tensor("yT", [24, 128, NT], BF16, kind=IK).ap()
    wbf_flat = [wbf[l].rearrange("r c -> (r c)") for l in range(L)]

    HT = [Buf("HT%d" % u) for u in range(NU)]
    PT = [[Buf("PT") for u in range(NU)] for c in range(NPT)]
    VS = [[Buf("VS") for u in range(NU)] for h in range(HEADS)]
    YT = [[Buf("YT") for u in range(NU)] for c in range(24)]
    WBF = [Buf("WBF%d" % l) for l in range(L)]
    OUTB = Buf("OUT")

    def units(t0, w):
        return range(t0 // UNIT, (t0 + w + UNIT - 1) // UNIT)

    SB_F32 = 53200
    arena_t = nc.alloc_sbuf_tensor("arena", [128, SB_F32], F32)
    ar = arena_t.ap()
    ps_t = nc.alloc_psum_tensor("ps", [128, 8, 512], F32)
    psa = ps_t.ap()
    PSB = [Buf("PS%d" % i) for i in range(8)]
    BC = {}

    def CB(name):
        if name not in BC:
            BC[name] = Buf(name)
        return BC[name]
    ps = [psa[:, i, :] for i in range(8)]

    perm = Arena(ar, 0, SB_F32)
    WSLOT_E = 8192
    NSLOT = 3
    wslots = [perm.bf16(WSLOT_E) for _ in range(NSLOT)]
    WS = [Buf("W%d" % i) for i in range(NSLOT)]
    ident = perm.f32(128)
    ones32 = perm.f32(128)
    ones16 = perm.bf16(128)
    vecs = perm.f32(L * NV)
    modv = perm.f32(L * 6 * 2 * NCK)
    lamv = perm.f32(L * 4)
    cpv = perm.f32(L * 16)
    CONST = Buf("CONST")
    MODV = Buf("MODV")
    A = Arena(ar, perm.cur, SB_F32)
    slot_i = [0]

    def next_slot():
        s = slot_i[0] % NSLOT
        slot_i[0] += 1
        return s

    bank_i = [0]

    def next_bank(n=7):
        b = bank_i[0] % n
        bank_i[0] += 1
        return b

    def vcol(l, off, n=1):
        return vecs[:, l * NV + off: l * NV + off + n]

    def mod(l, kind, j, c):
        o = ((l * 6 + kind) * 2 + j) * NCK + c
        return modv[:, o:o + 1]

    K_A1, K_B1, K_G2, K_A3, K_B3, K_G5 = range(6)

    def load_w(l, off, nelem_pp, reads_extra=()):
        s = next_slot()
        src = wbf_flat[l][off: off + 128 * nelem_pp].rearrange("(p x) -> p x", p=128)
        dst = wslots[s][:, 0:nelem_pp]
        P.dma(SP, lambda e: [e.dma_start(out=dst, in_=src)], reads=[WBF[l]], writes=[WS[s]], owner=WS[s])
        return s

    def prologue():
        P.dma(SP, lambda e: [e.dma_start(out=ident, in_=ident_d),
                             e.dma_start(out=vecs, in_=vecs_d)],
              reads=[], writes=[CONST], owner=CONST, ndma=2)
        P.op(POOL, lambda e: e.memset(ones32, 1.0), writes=[CONST])
        P.op(POOL, lambda e: e.memset(ones16, 1.0), writes=[CONST])

    def cast_weights(l):
        R = WTOT // 2048
        step = 1280
        n = R // step
        assert n * step == R

        def fn(e):
            return [e.dma_start(out=wbf[l][i * step:(i + 1) * step, :], in_=wpack[l, i * step:(i + 1) * step, :])
                    for i in range(n)]
        P.dma(POOL, fn, reads=[], writes=[WBF[l]], owner=WBF[l], ndma=n)

    def phase0():
        A.reset()
        xin = [A.f32(D) for _ in range(2)]
        XIN = [CB("xin%d" % i) for i in range(2)]
        stg = [A.f32(NCK * 512) for _ in range(2)]
        STG = [CB("stg%d" % i) for i in range(2)]
        P.phase_begin(XIN + STG)
        k = 0
        ti = 0
        for t0 in range(0, NT, 512):
            w = min(512, NT - t0)
            sg = stg[ti % 2].rearrange("p (c t) -> p c t", c=NCK)
            SG = STG[ti % 2]
            for tb in range(w // 128):
                xi, XI = xin[k % 2], XIN[k % 2]
                k += 1
                r0 = t0 + tb * 128
                P.dma(SP, lambda e, xi=xi, r0=r0: [e.dma_start(out=xi, in_=xb[r0:r0 + 128, :])],
                      reads=[], writes=[XI], owner=XI)
                for cg in range(4):
                    b = next_bank()

                    def tr(e, xi=xi, b=b, cg=cg):
                        for i in range(4):
                            c = cg * 4 + i
                            r = e.transpose(ps[b][:, i * 128:(i + 1) * 128], xi[:, c * 128:(c + 1) * 128], ident)
                        return r
                    P.op(PE, tr, reads=[XI, CONST], writes=[PSB[b]])
                    dst = sg[:, cg * 4:(cg + 1) * 4, tb * 128:(tb + 1) * 128]
                    src = ps[b].rearrange("p (c t) -> p c t", c=4)
                    eng = ACT if (cg % 2 == 0) else DVE
                    if eng == ACT:
                        P.op(ACT, lambda e, dst=dst, src=src: e.activation(out=dst, in_=src, func=AF.Copy),
                             reads=[PSB[b]], writes=[SG])
                    else:
                        P.op(DVE, lambda e, dst=dst, src=src: e.tensor_copy(out=dst, in_=src),
                             reads=[PSB[b]], writes=[SG])
            P.dma(POOL, lambda e, sg=sg, t0=t0, w=w: [e.dma_start(
                out=hT[:, :, t0:t0 + w].rearrange("c p t -> p c t"), in_=sg[:, :, 0:w])],
                reads=[SG], writes=[HT[u] for u in units(t0, w)], owner=SG)
            ti += 1

    def adaln():
        A.reset()
        wsl = [A.f32(NCK * 512) for _ in range(2)]
        WSL = [CB("wada%d" % i) for i in range(2)]
        cv = A.f32(2 * NCK)
        xr = A.f32(2 * NCK)
        mt = A.f32(2 * 96)
        dl = A.f32(L * 256)
        tmp = A.f32(64)
        s12 = A.f32(4)
        tv = A.f32(32)
        SM = CB("adasmall")
        P.phase_begin(WSL + [SM])
        P.dma(SP, lambda e: [e.dma_start(out=cv, in_=cvec),
                             e.dma_start(out=dl, in_=dlam.partition_broadcast(128))],
              reads=[], writes=[SM], owner=SM, ndma=2)
        P.op(ACT, lambda e: e.activation(out=cv, in_=cv, func=AF.Silu), reads=[SM], writes=[SM])
        P.op(DVE, lambda e: e.tensor_copy(out=xr.rearrange("p (c j) -> p c j", j=2),
                                          in_=cv.rearrange("p (j c) -> p c j", j=2)), reads=[SM], writes=[SM])
        k = 0
        for l in range(L):
            for nb in range(24):
                s = k % 2
                k += 1
                P.dma(SP, lambda e, s=s, l=l, nb=nb: [e.dma_start(out=wsl[s], in_=wada[l, nb])],
                      reads=[], writes=[WSL[s]], owner=WSL[s])

                def mm(e, s=s, nb=nb):
                    w3 = wsl[s].rearrange("p (k n) -> p k n", k=NCK)
                    for n4 in range(4):
                        n = nb * 4 + n4
                        for kc in range(NCK):
                            r = e.matmul(ps[7][:, n * 2:n * 2 + 2], lhsT=w3[:, kc, n4 * 128:(n4 + 1) * 128],
                                         rhs=xr[:, kc * 2:kc * 2 + 2], start=(kc == 0), stop=(kc == NCK - 1))
                    return r
                P.op(PE, mm, reads=[WSL[s], SM], writes=[PSB[7]])
            pv = ps[7][:, 0:192].rearrange("p (n j) -> p n j", j=2)
            for j in range(2):
                P.op(DVE, lambda e, j=j, l=l: e.tensor_tensor(out=mt[:, j * 96:(j + 1) * 96], in0=pv[:, :, j],
                                                              in1=vcol(l, V_BADA, 96), op=ALU.add),
                     reads=[PSB[7], CONST], writes=[SM])

                def derive(e, j=j, l=l):
                    M = lambda k: mt[:, j * 96 + k * 16: j * 96 + (k + 1) * 16]
                    dst = lambda kind: modv[:, ((l * 6 + kind) * 2 + j) * NCK:((l * 6 + kind) * 2 + j + 1) * NCK]
                    e.scalar_tensor_tensor(out=dst(K_A1), in0=M(1), scalar=1.0, in1=vcol(l, V_GPM, 16),
                                           op0=ALU.add, op1=ALU.mult)
                    e.tensor_copy(out=dst(K_B1), in_=M(0))
                    e.tensor_tensor(out=dst(K_G2), in0=M(2), in1=vcol(l, V_GQM, 16), op=ALU.mult)
                    e.scalar_tensor_tensor(out=dst(K_A3), in0=M(4), scalar=1.0, in1=vcol(l, V_GPF, 16),
                                           op0=ALU.add, op1=ALU.mult)
                    e.tensor_copy(out=dst(K_B3), in_=M(3))
                    return e.tensor_tensor(out=dst(K_G5), in0=M(5), in1=vcol(l, V_GQF, 16), op=ALU.mult)
                P.op(DVE, derive, reads=[SM, CONST], writes=[MODV])
            lam_init = 0.8 - 0.6 * math.exp(-0.3 * (l + LAYER0))

            d4 = dl[:, l * 256:(l + 1) * 256]
            P.seq(DVE, [
                lambda e, d4=d4: e.tensor_tensor(out=tmp, in0=d4[:, 0:64], in1=d4[:, 64:128], op=ALU.mult),
                lambda e: e.tensor_reduce(out=s12[:, 0:1], in_=tmp, axis=AX.X, op=ALU.add),
                lambda e, d4=d4: e.tensor_tensor(out=tmp, in0=d4[:, 128:192], in1=d4[:, 192:256], op=ALU.mult),
                lambda e: e.tensor_reduce(out=s12[:, 1:2], in_=tmp, axis=AX.X, op=ALU.add)],
                reads=[SM], writes=[SM])
            P.op(ACT, lambda e: e.activation(out=s12[:, 2:4], in_=s12[:, 0:2], func=AF.Exp), reads=[SM], writes=[SM])

            P.seq(DVE, [
                lambda e: e.tensor_tensor(out=s12[:, 0:1], in0=s12[:, 3:4], in1=s12[:, 2:3], op=ALU.subtract),
                lambda e, l=l, lam_init=lam_init: e.tensor_scalar(out=lamv[:, l * 4:l * 4 + 1], in0=s12[:, 0:1],
                                                                  scalar1=-lam_init, scalar2=None, op0=ALU.add),
                lambda e, l=l, lam_init=lam_init: e.tensor_scalar(out=lamv[:, l * 4 + 1:l * 4 + 2], in0=vcol(l, V_SUB, 1),
                                                                  scalar1=(1.0 - lam_init), scalar2=None, op0=ALU.mult)],
                reads=[SM, CONST], writes=[SM, MODV])
            P.op(ACT, lambda e, l=l: e.activation(out=tv[:, 0:16], in_=vcol(l, V_LAM, 16), func=AF.Exp, scale=-1.0),
                 reads=[CONST, SM], writes=[SM])
            P.op(ACT, lambda e: e.activation(out=tv[:, 16:32], in_=tv[:, 0:16], func=AF.Ln, bias=1.0, scale=1.0),
                 reads=[SM], writes=[SM])
            P.op(DVE, lambda e, l=l: e.tensor_scalar(out=cpv[:, l * 16:(l + 1) * 16], in0=tv[:, 16:32],
                                                     scalar1=-RG_C, scalar2=None, op0=ALU.mult),
                 reads=[SM], writes=[MODV])

    def rstd_from(psb, b, w, dstbuf, DST, n_feat):
        P.op(ACT, lambda e: e.activation(out=dstbuf[:, :w], in_=ps[b][:, :w], func=AF.Sqrt, scale=1.0 / n_feat,
                                         bias=EPS), reads=[PSB[b]], writes=[DST])
        P.op(DVE, lambda e: e.reciprocal(out=dstbuf[:, :w], in_=dstbuf[:, :w]), reads=[DST], writes=[DST])

    def p1_tiles():
        t = [(0, CTX)] if CTX else []
        for t0 in range(CTX, NT, 512):
            t.append((t0, min(512, NT - t0)))
        return t

    def phase_p1(l):
        A.reset()
        hin = A.f32(NCK * 512)
        hin3 = hin.rearrange("p (c t) -> p c t", c=NCK)
        HIN = [CB("hin%d" % c) for c in range(NCK)]
        uT = A.bf16(NCK * 512)
        uT3 = uT.rearrange("p (c t) -> p c t", c=NCK)
        UT = [CB("uT%d" % c) for c in range(NCK)]
        sq = [A.f32(512) for _ in range(2)]
        SQ = [CB("sq%d" % i) for i in range(2)]
        rstd = A.f32(512)
        RS = CB("rstd")
        cs = [A.f32(1024) for _ in range(2)]
        CS = [CB("cs%d" % i) for i in range(2)]
        stg = [A.bf16(4 * 512) for _ in range(3)]
        STG = [CB("pstg%d" % i) for i in range(3)]
        rt = [A.f32(512) for _ in range(4)]
        RT = [CB("rt%d" % i) for i in range(4)]
        vst = [A.bf16(4 * 512) for _ in range(2)]
        VST = [CB("vst%d" % i) for i in range(2)]
        P.phase_begin(HIN + UT + SQ + [RS] + CS + STG + RT + VST)
        stg_i = [0]
        rt_i = [0]
        vst_i = [0]
        ev_i = [0]
        def p1_tile(ti, t0, w):
            is_ctx = t0 < CTX
            j = 1 if is_ctx else 0
            csb = CSB = None
            P.dma(SP, lambda e, t0=t0, w=w: [e.dma_start(out=hin3[:, :, 0:w],
                                                         in_=hT[:, :, t0:t0 + w].rearrange("c p t -> p c t"))],
                  reads=[HT[u] for u in units(t0, w)], writes=HIN, owner=HIN[0])
            if not is_ctx:
                csb, CSB = cs[ti % 2], CS[ti % 2]
                p0 = t0 - CTX
                P.dma(SP, lambda e, csb=csb, p0=p0, w=w: [
                    e.dma_start(out=csb[:, 0:w], in_=rope[0, :, p0:p0 + w]),
                    e.dma_start(out=csb[:, 512:512 + w], in_=rope[1, :, p0:p0 + w])],
                    reads=[], writes=[CSB], owner=CSB, ndma=2)
            for c in range(NCK):
                sb, SBF = sq[c % 2], SQ[c % 2]
                P.op(ACT, lambda e, sb=sb, c=c, w=w: e.activation(out=sb[:, :w], in_=hin3[:, c, :w], func=AF.Square),
                     reads=[HIN[c]], writes=[SBF])
                P.op(PE, lambda e, sb=sb, c=c, w=w: e.matmul(ps[7][:, :w], lhsT=ones32, rhs=sb[:, :w],
                                                             start=(c == 0), stop=(c == NCK - 1)),
                     reads=[SBF, CONST], writes=[PSB[7]])
            rstd_from(ps, 7, w, rstd, RS, D)
            for c in range(NCK):
                P.op(DVE, lambda e, c=c, w=w: e.tensor_tensor(out=hin3[:, c, :w], in0=hin3[:, c, :w], in1=rstd[:, :w],
                                                              op=ALU.mult), reads=[HIN[c], RS], writes=[HIN[c]])
                P.op(ACT, lambda e, c=c, w=w, j=j: e.activation(out=uT3[:, c, :w], in_=hin3[:, c, :w], func=AF.Identity,
                                                                scale=mod(l, K_A1, j, c), bias=mod(l, K_B1, j, c)),
                     reads=[HIN[c], MODV], writes=[UT[c]])

            def mm_group(s, i, b, w=w):
                def fn(e):
                    w3 = wslots[s].rearrange("p (k n) -> p k n", k=NCK)
                    for kc in range(NCK):
                        r = e.matmul(ps[b][:, :w], lhsT=w3[:, kc, i * 128:(i + 1) * 128], rhs=uT3[:, kc, :w],
                                     start=(kc == 0), stop=(kc == NCK - 1))
                    return r
                P.op(PE, fn, reads=[WS[s]] + UT, writes=[PSB[b]])

            def store_stage(si, ch0, w=w, t0=t0):
                sg = stg[si].rearrange("p (c t) -> p c t", c=4)
                P.dma(POOL, lambda e: [e.dma_start(out=pT[ch0:ch0 + 4, :, t0:t0 + w].rearrange("c p t -> p c t"),
                                                   in_=sg[:, :, 0:w])],
                      reads=[STG[si]], writes=[PT[ch0 + i][u] for i in range(4) for u in units(t0, w)], owner=STG[si])

            def evac(dst, b, DSTB, w=w):
                ev_i[0] += 1
                if ev_i[0] % 2 == 0:
                    P.op(ACT, lambda e: e.activation(out=dst, in_=ps[b][:, :w], func=AF.Copy),
                         reads=[PSB[b]], writes=[DSTB])
                else:
                    P.op(DVE, lambda e: e.tensor_copy(out=dst, in_=ps[b][:, :w]), reads=[PSB[b]], writes=[DSTB])

            for kind in range(2):
                for g in range(2):
                    blk = kind * 4 + g * 2
                    s_pl = load_w(l, OFF_IN + blk * 1048576, 8192)
                    s_sw = None
                    if not is_ctx:
                        s_sw = load_w(l, OFF_IN + (blk + 1) * 1048576, 8192)
                    si = stg_i[0] % 3
                    stg_i[0] += 1
                    sg = stg[si].rearrange("p (c t) -> p c t", c=4)
                    for i in range(4):
                        b1 = next_bank()
                        mm_group(s_pl, i, b1)
                        if is_ctx:
                            evac(sg[:, i, :w], b1, STG[si])
                        else:
                            b2 = next_bank()
                            mm_group(s_sw, i, b2)
                            r1 = rt_i[0] % 4
                            r2 = (rt_i[0] + 1) % 4
                            rt_i[0] += 2
                            P.op(DVE, lambda e, r1=r1, b1=b1, csb=csb: e.tensor_tensor(
                                out=rt[r1][:, :w], in0=ps[b1][:, :w], in1=csb[:, 0:w], op=ALU.mult),
                                reads=[PSB[b1], CSB], writes=[RT[r1]])
                            P.op(DVE, lambda e, r2=r2, b2=b2, csb=csb: e.tensor_tensor(
                                out=rt[r2][:, :w], in0=ps[b2][:, :w], in1=csb[:, 512:512 + w], op=ALU.mult),
                                reads=[PSB[b2], CSB], writes=[RT[r2]])
                            P.op(POOL, lambda e, r1=r1, r2=r2, sg=sg, i=i: e.tensor_tensor(
                                out=sg[:, i, :w], in0=rt[r1][:, :w], in1=rt[r2][:, :w], op=ALU.add),
                                reads=[RT[r1], RT[r2]], writes=[STG[si]])
                    store_stage(si, (C_Q if kind == 0 else C_K) + g * 4)
            for vb in range(2):
                s = load_w(l, OFF_IN + (8 + vb) * 1048576, 8192)
                vi = vst_i[0] % 2
                vst_i[0] += 1
                vt = vst[vi].rearrange("p (b n) -> p b n", b=4)
                ntb = w // 128
                for tb in range(ntb):
                    b = next_bank()

                    def fn(e, s=s, tb=tb, b=b):
                        w3 = wslots[s].rearrange("p (k n) -> p k n", k=NCK)
                        for kc in range(NCK):
                            r = e.matmul(ps[b][:, :], lhsT=uT3[:, kc, tb * 128:(tb + 1) * 128], rhs=w3[:, kc, :],
                                         start=(kc == 0), stop=(kc == NCK - 1))
                        return r
                    P.op(PE, fn, reads=[WS[s]] + UT, writes=[PSB[b]])
                    ev_i[0] += 1
                    if ev_i[0] % 2 == 0:
                        P.op(ACT, lambda e, vt=vt, tb=tb, b=b: e.activation(out=vt[:, tb, :], in_=ps[b][:, :], func=AF.Copy),
                             reads=[PSB[b]], writes=[VST[vi]])
                    else:
                        P.op(DVE, lambda e, vt=vt, tb=tb, b=b: e.tensor_copy(out=vt[:, tb, :], in_=ps[b][:, :]),
                             reads=[PSB[b]], writes=[VST[vi]])
                ch0 = t0 // 128
                P.dma(POOL, lambda e, vt=vt, vb=vb, ch0=ch0, ntb=ntb: [e.dma_start(
                    out=vS[vb * 4 + hh, :, ch0:ch0 + ntb, :],
                    in_=vt[:, 0:ntb, hh * 128:(hh + 1) * 128]) for hh in range(4)],
                    reads=[VST[vi]], writes=[VS[vb * 4 + h][u] for h in range(4) for u in units(t0, w)],
                    owner=VST[vi], ndma=4)
            for pb in range(22):
                s = load_w(l, OFF_IN + (10 + pb) * 1048576, 8192)
                si = stg_i[0] % 3
                stg_i[0] += 1
                sg = stg[si].rearrange("p (c t) -> p c t", c=4)
                for i in range(4):
                    b = next_bank()
                    mm_group(s, i, b)
                    evac(sg[:, i, :w], b, STG[si])
                store_stage(si, C_CX + pb * 4)

        for ti, (t0, w) in enumerate(p1_tiles()):
            p1_tile(ti, t0, w)

    def segs():
        s = []
        if CTX:
            s.append((0, CTX))
        s.append((CTX, NT))
        return s

    def phase_conv(l):
        A.reset()
        xin = [A.bf16(3 * NT) for _ in range(2)]
        XIN = [CB("cin%d" % i) for i in range(2)]
        Z = [A.f32(NT) for _ in range(2)]
        Y = [A.f32(NT) for _ in range(2)]
        ZB = [CB("Z%d" % i) for i in range(2)]
        YB = [CB("Y%d" % i) for i in range(2)]
        ob = [A.bf16(NT) for _ in range(2)]
        OB = [CB("cob%d" % i) for i in range(2)]
        P.phase_begin(XIN + ZB + YB + OB)
        allu = range(NU)
        for c in range(8):
            i = c % 2
            x3 = xin[i].rearrange("p (k t) -> p k t", k=3)
            P.dma(SP, lambda e, x3=x3, c=c: [e.dma_start(out=x3[:, 0, :], in_=pT[C_CX + c]),
                                             e.dma_start(out=x3[:, 1, :], in_=pT[C_CC + c]),
                                             e.dma_start(out=x3[:, 2, :], in_=pT[C_CB + c])],
                  reads=[PT[C_CX + c][u] for u in allu] + [PT[C_CC + c][u] for u in allu] + [PT[C_CB + c][u] for u in allu],
                  writes=[XIN[i]], owner=XIN[i], ndma=3)
            z, y = Z[i], Y[i]
            P.op(POOL, lambda e, z=z, x3=x3: e.tensor_tensor(out=z, in0=x3[:, 1, :], in1=x3[:, 0, :], op=ALU.mult),
                 reads=[XIN[i]], writes=[ZB[i]])
            P.op(ACT, lambda e, z=z, y=y, c=c: e.activation(out=y, in_=z, func=AF.Identity,
                                                            scale=vcol(l, V_CW + 8 + c), bias=vcol(l, V_CB + c)),
                 reads=[ZB[i], CONST], writes=[YB[i]])

            fns = []
            for (a, b) in segs():
                fns.append(lambda e, z=z, y=y, c=c, a=a, b=b: e.scalar_tensor_tensor(
                    out=y[:, a + 1:b], in0=z[:, a:b - 1], scalar=vcol(l, V_CW + c), in1=y[:, a + 1:b],
                    op0=ALU.mult, op1=ALU.add))
                fns.append(lambda e, z=z, y=y, c=c, a=a, b=b: e.scalar_tensor_tensor(
                    out=y[:, a:b - 1], in0=z[:, a + 1:b], scalar=vcol(l, V_CW + 16 + c), in1=y[:, a:b - 1],
                    op0=ALU.mult, op1=ALU.add))
            P.seq(DVE, fns, reads=[ZB[i], YB[i], CONST], writes=[YB[i]])
            P.op(POOL, lambda e, y=y, x3=x3, o=ob[i]: e.tensor_tensor(out=o, in0=x3[:, 2, :], in1=y, op=ALU.mult),
                 reads=[YB[i], XIN[i]], writes=[OB[i]])
            P.dma(POOL, lambda e, o=ob[i], c=c: [e.dma_start(out=yT[8 + c], in_=o)],
                  reads=[OB[i]], writes=[YT[8 + c][u] for u in allu], owner=OB[i])

    def phase_rnn(l):
        A.reset()
        xg = [A.bf16(2 * NT) for _ in range(2)]
        XG = [CB("xg%d" % i) for i in range(2)]
        rg = A.f32(32 * 128)
        RGW = CB("rgw")
        XR, AA, GG, TT, HF = [A.f32(NT) for _ in range(5)]
        BXR, BA, BG, BT, BH = [CB(n) for n in ("XR", "AA", "GG", "TT", "HF")]
        ob0 = A.bf16(NT)
        ob = [ob0, ob0]
        OB0 = CB("rob")
        OB = [OB0, OB0]
        P.phase_begin(XG + [RGW, BXR, BA, BG, BT, BH, OB0])
        rg3 = rg.rearrange("p (i k) -> p i k", i=32)
        P.dma(SP, lambda e: [e.dma_start(out=rg, in_=rgw[l])], reads=[], writes=[RGW], owner=RGW)
        allu = range(NU)
        tiles = p1_tiles()
        for n in range(8):
            i = n % 2
            x3 = xg[i].rearrange("p (k t) -> p k t", k=2)
            P.dma(SP, lambda e, x3=x3, n=n: [e.dma_start(out=x3[:, 0, :], in_=pT[C_RG + n]),
                                             e.dma_start(out=x3[:, 1, :], in_=pT[C_RX + n])],
                  reads=[PT[C_RG + n][u] for u in allu] + [PT[C_RX + n][u] for u in allu],
                  writes=[XG[i]], owner=XG[i], ndma=2)
            xx = x3[:, 1, :]
            gt = x3[:, 0, :]
            P.op(ACT, lambda e, xx=xx, n=n: e.activation(out=XR, in_=xx, func=AF.Identity,
                                                         scale=vcol(l, V_RW + 16 + n), bias=vcol(l, V_RB + n)),
                 reads=[XG[i], CONST], writes=[BXR])

            fns = []
            for (a, b) in segs():
                fns.append(lambda e, xx=xx, n=n, a=a, b=b: e.scalar_tensor_tensor(
                    out=XR[:, a + 2:b], in0=xx[:, a:b - 2], scalar=vcol(l, V_RW + n), in1=XR[:, a + 2:b],
                    op0=ALU.mult, op1=ALU.add))
                fns.append(lambda e, xx=xx, n=n, a=a, b=b: e.scalar_tensor_tensor(
                    out=XR[:, a + 1:b], in0=xx[:, a:b - 1], scalar=vcol(l, V_RW + 8 + n), in1=XR[:, a + 1:b],
                    op0=ALU.mult, op1=ALU.add))
                fns.append(lambda e, xx=xx, n=n, a=a, b=b: e.scalar_tensor_tensor(
                    out=XR[:, a:b - 1], in0=xx[:, a + 1:b], scalar=vcol(l, V_RW + 24 + n), in1=XR[:, a:b - 1],
                    op0=ALU.mult, op1=ALU.add))
            P.seq(DVE, fns, reads=[XG[i], BXR, CONST], writes=[BXR])
            for d in range(2):
                for (t0, w) in tiles:
                    for gi, (dst, DSTB, voff) in enumerate(((AA, BA, V_BA), (GG, BG, V_BX))):
                        b = next_bank()
                        widx = gi * 16 + d * 8 + n
                        P.op(PE, lambda e, b=b, widx=widx, t0=t0, w=w: e.matmul(
                            ps[b][:, :w], lhsT=rg3[:, widx, :], rhs=XR[:, t0:t0 + w], start=True, stop=True),
                            reads=[RGW, BXR], writes=[PSB[b]])
                        P.op(ACT, lambda e, b=b, dst=dst, t0=t0, w=w, voff=voff, d=d, n=n: e.activation(
                            out=dst[:, t0:t0 + w], in_=ps[b][:, :w], func=AF.Sigmoid, bias=vcol(l, voff + d * 8 + n),
                            scale=1.0), reads=[PSB[b], CONST], writes=[DSTB])
                cp = cpv[:, l * 16 + d * 8 + n: l * 16 + d * 8 + n + 1]
                P.op(ACT, lambda e, cp=cp: e.activation(out=AA, in_=AA, func=AF.Exp, scale=cp),
                     reads=[BA, MODV], writes=[BA])

                P.op(POOL, lambda e: e.tensor_tensor(out=GG, in0=GG, in1=XR, op=ALU.mult), reads=[BG, BXR], writes=[BG])
                P.seq(POOL, [
                    lambda e: e.tensor_tensor(out=TT, in0=AA, in1=AA, op=ALU.mult),
                    lambda e: e.tensor_scalar(out=TT, in0=TT, scalar1=-1.0, scalar2=1.0, op0=ALU.mult, op1=ALU.add),
                    lambda e: e.tensor_scalar(out=TT, in0=TT, scalar1=1e-20, scalar2=None, op0=ALU.max)],
                    reads=[BA, BT], writes=[BT])
                P.op(ACT, lambda e: e.activation(out=TT, in_=TT, func=AF.Sqrt), reads=[BT], writes=[BT])
                P.op(POOL, lambda e: e.tensor_tensor(out=GG, in0=GG, in1=TT, op=ALU.mult), reads=[BG, BT], writes=[BG])
                if d == 0:
                    fns = []
                    if CTX:
                        fns.append(lambda e: e.tensor_tensor_scan(out=HF[:, 0:CTX], data0=AA[:, 0:CTX], data1=GG[:, 0:CTX],
                                                                  initial=0.0, op0=ALU.mult, op1=ALU.add))
                    init = HF[:, CTX - 1:CTX] if CTX else 0.0
                    fns.append(lambda e, init=init: e.tensor_tensor_scan(
                        out=HF[:, CTX:NT], data0=AA[:, CTX:NT], data1=GG[:, CTX:NT], initial=init,
                        op0=ALU.mult, op1=ALU.add))
                    P.seq(DVE, fns, reads=[BA, BG, BH], writes=[BH])
                else:
                    fns = []
                    if CTX:
                        fns.append(lambda e: e.tensor_tensor_scan(
                            out=TT[:, 0:CTX][:, ::-1], data0=AA[:, 0:CTX][:, ::-1], data1=GG[:, 0:CTX][:, ::-1],
                            initial=0.0, op0=ALU.mult, op1=ALU.add))
                    init = TT[:, 0:1] if CTX else 0.0
                    fns.append(lambda e, init=init: e.tensor_tensor_scan(
                        out=TT[:, CTX:NT][:, ::-1], data0=AA[:, CTX:NT][:, ::-1], data1=GG[:, CTX:NT][:, ::-1],
                        initial=init, op0=ALU.mult, op1=ALU.add))
                    P.seq(DVE, fns, reads=[BA, BG, BT], writes=[BT])
            P.op(POOL, lambda e: e.tensor_tensor(out=HF, in0=HF, in1=TT, op=ALU.add), reads=[BH, BT], writes=[BH])
            P.op(ACT, lambda e, gt=gt: e.activation(out=XR, in_=gt, func=AF.Gelu_apprx_tanh),
                 reads=[XG[i], BXR], writes=[BXR])
            P.op(POOL, lambda e, o=ob[i]: e.tensor_tensor(out=o, in0=XR, in1=HF, op=ALU.mult),
                 reads=[BXR, BH], writes=[OB[i]])
            P.dma(POOL, lambda e, o=ob[i], n=n: [e.dma_start(out=yT[16 + n], in_=o)],
                  reads=[OB[i]], writes=[YT[16 + n][u] for u in allu], owner=OB[i])

    def phase_attn(l, need_ctx):
        A.reset()
        kq = [A.bf16(3 * NT) for _ in range(2)]
        KQ = [CB("kqv%d" % i) for i in range(2)]
        ya = [A.bf16(NT) for _ in range(2)]
        YA = [CB("ya%d" % i) for i in range(2)]
        pe = [A.bf16(512) for _ in range(4)]
        PEX = [CB("pex%d" % i) for i in range(4)]
        rc = [A.f32(512) for _ in range(2)]
        oo = [A.f32(512) for _ in range(2)]
        osum, osq, orr = A.f32(512), A.f32(512), A.f32(512)
        EP = CB("ep")
        P.phase_begin(KQ + YA + PEX + [EP])
        allu = range(NU)
        neglam = lamv[:, l * 4:l * 4 + 1]
        gs = lamv[:, l * 4 + 1:l * 4 + 2]
        qtiles = []
        if CTX and need_ctx:
            qtiles.append((0, CTX, CTX // 128))
        for t0 in range(CTX, NT, 512):
            qtiles.append((t0, min(512, NT - t0), NCH))
        pe_i = [0]
        def head_body(h):
            i = h % 2
            k3 = kq[i].rearrange("p (k t) -> p k t", k=3)
            kS, qS = k3[:, 0, :], k3[:, 1, :]
            vv = k3[:, 2, :].rearrange("p (c e) -> p c e", e=128)
            P.dma(SP, lambda e, kS=kS, qS=qS, k3=k3, h=h: [
                e.dma_start(out=kS, in_=pT[C_K + h]), e.dma_start(out=qS, in_=pT[C_Q + h]),
                e.dma_start(out=k3[:, 2, :], in_=vS[h].rearrange("p c e -> p (c e)"))],
                reads=[PT[C_K + h][u] for u in allu] + [PT[C_Q + h][u] for u in allu] + [VS[h][u] for u in allu],
                writes=[KQ[i]], owner=KQ[i], ndma=3)
            def qtile_body(t0, w, nk):
                def S_op(j, c, t0=t0, w=w):
                    b = (2 * j + c) % 4
                    P.op(PE, lambda e: e.matmul(ps[b][:, :w], lhsT=kS[c * 64:(c + 1) * 64, j * 128:(j + 1) * 128],
                                                rhs=qS[c * 64:(c + 1) * 64, t0:t0 + w], start=True, stop=True),
                         reads=[KQ[i]], writes=[PSB[b]])
                    pi = pe_i[0] % 4
                    pe_i[0] += 1
                    P.op(ACT, lambda e: e.activation(out=pe[pi][:, :w], in_=ps[b][:, :w], func=AF.Exp, scale=0.125),
                         reads=[PSB[b]], writes=[PEX[pi]])
                    return pi

                def PV_op(j, c, pi, nk=nk, w=w):
                    def fn(e):
                        e.matmul(ps[4 + c][:, :w], lhsT=vv[:, j, :], rhs=pe[pi][:, :w], start=(j == 0), stop=(j == nk - 1))
                        return e.matmul(ps[6 + c][:, :w], lhsT=ones16, rhs=pe[pi][:, :w], start=(j == 0),
                                        stop=(j == nk - 1))
                    P.op(PE, fn, reads=[KQ[i], PEX[pi], CONST], writes=[PSB[4 + c], PSB[6 + c]])
                cur = [S_op(0, 0), S_op(0, 1)]
                for j in range(nk):
                    nxt = None
                    if j + 1 < nk:
                        nxt = [S_op(j + 1, 0), S_op(j + 1, 1)]
                    PV_op(j, 0, cur[0])
                    PV_op(j, 1, cur[1])
                    cur = nxt
                for c in range(2):
                    P.op(DVE, lambda e, c=c, w=w: e.reciprocal(out=rc[c][:, :w], in_=ps[6 + c][:, :w]),
                         reads=[PSB[6 + c]], writes=[EP])
                    P.op(DVE, lambda e, c=c, w=w: e.tensor_tensor(out=oo[c][:, :w], in0=ps[4 + c][:, :w],
                                                                  in1=rc[c][:, :w], op=ALU.mult),
                         reads=[PSB[4 + c], EP], writes=[EP])
                P.op(DVE, lambda e, w=w: e.scalar_tensor_tensor(out=osum[:, :w], in0=oo[1][:, :w], scalar=neglam,
                                                                in1=oo[0][:, :w], op0=ALU.mult, op1=ALU.add),
                     reads=[EP, MODV], writes=[EP])
                P.op(POOL, lambda e, w=w: e.tensor_tensor(out=osq[:, :w], in0=osum[:, :w], in1=osum[:, :w], op=ALU.mult),
                     reads=[EP], writes=[EP])
                P.op(PE, lambda e, w=w: e.matmul(ps[0][:, :w], lhsT=ones32, rhs=osq[:, :w], start=True, stop=True),
                     reads=[EP, CONST], writes=[PSB[0]])
                rstd_from(ps, 0, w, orr, EP, 128)
                P.op(DVE, lambda e, w=w, t0=t0, i=i: e.scalar_tensor_tensor(
                    out=ya[i][:, t0:t0 + w], in0=osum[:, :w], scalar=gs, in1=orr[:, :w], op0=ALU.mult, op1=ALU.mult),
                    reads=[EP, MODV], writes=[YA[i]])
            for (t0, w, nk) in qtiles:
                qtile_body(t0, w, nk)
            q0 = 0 if (need_ctx or not CTX) else CTX
            P.dma(POOL, lambda e, i=i, h=h, q0=q0: [e.dma_start(out=yT[h][:, q0:NT], in_=ya[i][:, q0:NT])],
                  reads=[YA[i]], writes=[YT[h][u] for u in units(q0, NT - q0)], owner=YA[i])

        for h in range(HEADS):
            head_body(h)

    def phase_mf(l, need_ctx, last):
        A.reset()
        W = TM
        yS = [A.bf16(24 * W) for _ in range(2)]
        YS = [CB("yS%d" % i) for i in range(2)]
        gS = [A.bf16(3 * W) for _ in range(3)]
        GS = [CB("gS%d" % i) for i in range(3)]
        hin = A.f32(NCK * W)
        HIN = CB("mhin")
        mT = A.bf16(NCK * W)
        MT = [CB("mT%d" % c) for c in range(NCK)]
        X1 = A.f32(NCK * W)
        X1B = [CB("X1_%d" % c) for c in range(NCK)]
        u2 = A.bf16(NCK * W)
        U2 = [CB("u2_%d" % c) for c in range(NCK)]
        aT = A.bf16(NJ * W)
        AT = [CB("aT%d" % c) for c in range(NJ)]
        X2 = A.f32(NCK * W)
        X2B = [CB("X2_%d" % c) for c in range(NCK)]
        rs = A.f32(W)
        RS = CB("mrs")
        sq = [A.f32(W) for _ in range(2)]
        SQ = [CB("msq%d" % i) for i in range(2)]
        gsig = [A.f32(W) for _ in range(6)]
        GSG = [CB("gsig%d" % i) for i in range(6)]
        tt = [A.f32(W) for _ in range(6)]
        TTB = [CB("tt%d" % i) for i in range(6)]
        sgb = [A.f32(W) for _ in range(2)]
        SGB = [CB("sg%d" % i) for i in range(2)]
        ost = [A.f32(D) for _ in range(2)] if last else []
        OST = [CB("ost%d" % i) for i in range(2)] if last else []
        P.phase_begin(YS + GS + [HIN] + MT + X1B + U2 + AT + X2B + [RS] + SQ + GSG + TTB + SGB + OST)
        hin3 = hin.rearrange("p (c t) -> p c t", c=NCK)
        mT3 = mT.rearrange("p (c t) -> p c t", c=NCK)
        X13 = X1.rearrange("p (c t) -> p c t", c=NCK)
        X23 = X2.rearrange("p (c t) -> p c t", c=NCK)
        u23 = u2.rearrange("p (c t) -> p c t", c=NCK)
        aT3 = aT.rearrange("p (c t) -> p c t", c=NJ)
        tiles = []
        if CTX and need_ctx:
            for t0 in range(0, CTX, W):
                tiles.append((t0, min(W, CTX - t0)))
        for t0 in range(CTX, NT, W):
            tiles.append((t0, min(W, NT - t0)))
        cnt = {"g": 0, "gs": 0, "tt": 0, "sq": 0, "sg": 0, "ost": 0}

        def sumsq_chain(src3, SRCB, w):
            for c in range(NCK):
                k = cnt["sq"] % 2
                cnt["sq"] += 1
                P.op(POOL, lambda e, c=c, k=k: e.tensor_tensor(out=sq[k][:, :w], in0=src3[:, c, :w], in1=src3[:, c, :w],
                                                               op=ALU.mult), reads=[SRCB[c]], writes=[SQ[k]])
                P.op(PE, lambda e, c=c, k=k: e.matmul(ps[7][:, :w], lhsT=ones32, rhs=sq[k][:, :w], start=(c == 0),
                                                      stop=(c == NCK - 1)), reads=[SQ[k], CONST], writes=[PSB[7]])
            rstd_from(ps, 7, w, rs, RS, D)

        def mf_tile(ti, t0, w):
            j = 1 if t0 < CTX else 0
            yi = ti % 2
            y3 = yS[yi].rearrange("p (c t) -> p c t", c=24)
            us = list(units(t0, w))
            P.dma(SP, lambda e, y3=y3, t0=t0, w=w: [e.dma_start(out=y3[:, :, 0:w],
                                                                in_=yT[:, :, t0:t0 + w].rearrange("c p t -> p c t"))],
                  reads=[YT[c][u] for c in range(24) for u in us], writes=[YS[yi]], owner=YS[yi])
            P.dma(SP, lambda e, t0=t0, w=w: [e.dma_start(out=hin3[:, :, 0:w],
                                                         in_=hT[:, :, t0:t0 + w].rearrange("c p t -> p c t"))],
                  reads=[HT[u] for u in us], writes=[HIN], owner=HIN)
            for q in range(8):
                s = load_w(l, OFF_BR + q * 786432, 6144)
                w4 = wslots[s][:, 0:6144].rearrange("p (r k n) -> p r k n", r=3, k=8)
                for i2 in range(2):
                    n = q * 2 + i2
                    gi = cnt["g"] % 3
                    cnt["g"] += 1
                    g3 = gS[gi].rearrange("p (r t) -> p r t", r=3)
                    P.dma(SP, lambda e, g3=g3, n=n, t0=t0, w=w: [e.dma_start(
                        out=g3[:, :, 0:w],
                        in_=pT[C_MG + n:C_MG + n + 33:16, :, t0:t0 + w].rearrange("r p t -> p r t"))],
                        reads=[PT[C_MG + r * 16 + n][u] for r in range(3) for u in us], writes=[GS[gi]], owner=GS[gi])
                    tks = []
                    for r in range(3):
                        b = next_bank()

                        def fn(e, r=r, b=b, i2=i2, w4=w4):
                            for kc in range(8):
                                x = e.matmul(ps[b][:, :w], lhsT=w4[:, r, kc, i2 * 128:(i2 + 1) * 128],
                                             rhs=y3[:, r * 8 + kc, :w], start=(kc == 0), stop=(kc == 7))
                            return x
                        P.op(PE, fn, reads=[WS[s], YS[yi]], writes=[PSB[b]])
                        k = cnt["gs"] % 6
                        cnt["gs"] += 1
                        P.op(ACT, lambda e, k=k, g3=g3, r=r, n=n: e.activation(
                            out=gsig[k][:, :w], in_=g3[:, r, :w], func=AF.Sigmoid,
                            bias=vcol(l, V_BM + r * 16 + n), scale=1.0), reads=[GS[gi], CONST], writes=[GSG[k]])
                        tk = cnt["tt"] % 6
                        cnt["tt"] += 1
                        P.op(DVE, lambda e, tk=tk, k=k, b=b: e.tensor_tensor(out=tt[tk][:, :w], in0=ps[b][:, :w],
                                                                             in1=gsig[k][:, :w], op=ALU.mult),
                             reads=[PSB[b], GSG[k]], writes=[TTB[tk]])
                        tks.append(tk)

                    P.op(POOL, lambda e, tks=tks: e.tensor_tensor(out=tt[tks[0]][:, :w], in0=tt[tks[0]][:, :w],
                                                                 in1=tt[tks[1]][:, :w], op=ALU.add),
                         reads=[TTB[tks[0]], TTB[tks[1]]], writes=[TTB[tks[0]]])
                    P.op(POOL, lambda e, tks=tks, n=n: e.tensor_tensor(out=mT3[:, n, :w], in0=tt[tks[0]][:, :w],
                                                                      in1=tt[tks[2]][:, :w], op=ALU.add),
                         reads=[TTB[tks[0]], TTB[tks[2]]], writes=[MT[n]])
            for g in range(4):
                s = load_w(l, OFF_WO + g * 1048576, 8192)
                w3 = wslots[s].rearrange("p (k n) -> p k n", k=NCK)
                for i4 in range(4):
                    n = g * 4 + i4
                    b = next_bank()

                    def fn(e, b=b, i4=i4, w3=w3):
                        for kc in range(NCK):
                            x = e.matmul(ps[b][:, :w], lhsT=w3[:, kc, i4 * 128:(i4 + 1) * 128], rhs=mT3[:, kc, :w],
                                         start=(kc == 0), stop=(kc == NCK - 1))
                        return x
                    P.op(PE, fn, reads=[WS[s]] + MT, writes=[PSB[b]])
                    P.op(ACT, lambda e, b=b, n=n: e.activation(out=X13[:, n, :w], in_=ps[b][:, :w], func=AF.Copy),
                         reads=[PSB[b]], writes=[X1B[n]])
            sumsq_chain(X13, X1B, w)
            for n in range(NCK):
                k = cnt["tt"] % 6
                cnt["tt"] += 1
                P.op(DVE, lambda e, n=n, k=k, j=j: e.scalar_tensor_tensor(
                    out=tt[k][:, :w], in0=X13[:, n, :w], scalar=mod(l, K_G2, j, n), in1=rs[:, :w],
                    op0=ALU.mult, op1=ALU.mult), reads=[X1B[n], RS, MODV], writes=[TTB[k]])
                P.op(POOL, lambda e, n=n, k=k: e.tensor_tensor(out=X13[:, n, :w], in0=hin3[:, n, :w], in1=tt[k][:, :w],
                                                               op=ALU.add), reads=[HIN, TTB[k]], writes=[X1B[n]])
            sumsq_chain(X13, X1B, w)
            for c in range(NCK):
                k = cnt["tt"] % 6
                cnt["tt"] += 1
                P.op(DVE, lambda e, c=c, k=k: e.tensor_tensor(out=tt[k][:, :w], in0=X13[:, c, :w], in1=rs[:, :w],
                                                              op=ALU.mult), reads=[X1B[c], RS], writes=[TTB[k]])
                P.op(ACT, lambda e, c=c, k=k, j=j: e.activation(out=u23[:, c, :w], in_=tt[k][:, :w], func=AF.Identity,
                                                                scale=mod(l, K_A3, j, c), bias=mod(l, K_B3, j, c)),
                     reads=[TTB[k], MODV], writes=[U2[c]])
            for jb in range(22):
                s = load_w(l, OFF_GU + jb * 1048576, 8192)
                w5 = wslots[s].rearrange("p (a g k n) -> p a g k n", a=2, g=2, k=NCK)
                for jj in range(2):
                    jx = jb * 2 + jj
                    bg = next_bank()
                    bu = next_bank()
                    for (bb, gu) in ((bg, 0), (bu, 1)):
                        def fn(e, bb=bb, gu=gu, jj=jj, w5=w5):
                            for kc in range(NCK):
                                x = e.matmul(ps[bb][:, :w], lhsT=w5[:, jj, gu, kc, :], rhs=u23[:, kc, :w],
                                             start=(kc == 0), stop=(kc == NCK - 1))
                            return x
                        P.op(PE, fn, reads=[WS[s]] + U2, writes=[PSB[bb]])
                    k = cnt["sg"] % 2
                    cnt["sg"] += 1
                    P.op(ACT, lambda e, k=k, bg=bg: e.activation(out=sgb[k][:, :w], in_=ps[bg][:, :w], func=AF.Silu),
                         reads=[PSB[bg]], writes=[SGB[k]])
                    P.op(DVE, lambda e, k=k, bu=bu, jx=jx: e.tensor_tensor(out=aT3[:, jx, :w], in0=ps[bu][:, :w],
                                                                           in1=sgb[k][:, :w], op=ALU.mult),
                         reads=[PSB[bu], SGB[k]], writes=[AT[jx]])
            for n in range(NCK):
                s = load_w(l, OFF_WD + n * 720896, 5632)
                w3 = wslots[s][:, 0:5632].rearrange("p (k n) -> p k n", k=NJ)
                b = next_bank()

                def fn(e, b=b, w3=w3):
                    for kc in range(NJ):
                        x = e.matmul(ps[b][:, :w], lhsT=w3[:, kc, :], rhs=aT3[:, kc, :w], start=(kc == 0),
                                     stop=(kc == NJ - 1))
                    return x
                P.op(PE, fn, reads=[WS[s]] + AT, writes=[PSB[b]])
                P.op(ACT, lambda e, b=b, n=n: e.activation(out=X23[:, n, :w], in_=ps[b][:, :w], func=AF.Copy),
                     reads=[PSB[b]], writes=[X2B[n]])
            sumsq_chain(X23, X2B, w)
            for n in range(NCK):
                k = cnt["tt"] % 6
                cnt["tt"] += 1
                P.op(DVE, lambda e, n=n, k=k, j=j: e.scalar_tensor_tensor(
                    out=tt[k][:, :w], in0=X23[:, n, :w], scalar=mod(l, K_G5, j, n), in1=rs[:, :w],
                    op0=ALU.mult, op1=ALU.mult), reads=[X2B[n], RS, MODV], writes=[TTB[k]])
                P.op(POOL, lambda e, n=n, k=k: e.tensor_tensor(out=X23[:, n, :w], in0=X13[:, n, :w], in1=tt[k][:, :w],
                                                               op=ALU.add), reads=[X1B[n], TTB[k]], writes=[X2B[n]])
            if not last:
                P.dma(POOL, lambda e, t0=t0, w=w: [e.dma_start(out=hT[:, :, t0:t0 + w].rearrange("c p t -> p c t"),
                                                               in_=X23[:, :, 0:w])],
                      reads=X2B, writes=[HT[u] for u in us], owner=X2B[0])
            else:
                for tb in range(w // 128):
                    oi = cnt["ost"] % 2
                    cnt["ost"] += 1
                    for cg in range(4):
                        b = next_bank()

                        def tr(e, b=b, cg=cg, tb=tb):
                            for i4 in range(4):
                                c = cg * 4 + i4
                                x = e.transpose(ps[b][:, i4 * 128:(i4 + 1) * 128], X23[:, c, tb * 128:(tb + 1) * 128],
                                                ident)
                            return x
                        P.op(PE, tr, reads=[X2B[cg * 4 + i4] for i4 in range(4)] + [CONST], writes=[PSB[b]])
                        if cg % 2 == 0:
                            P.op(ACT, lambda e, b=b, cg=cg, oi=oi: e.activation(
                                out=ost[oi][:, cg * 512:(cg + 1) * 512], in_=ps[b][:, :], func=AF.Copy),
                                reads=[PSB[b]], writes=[OST[oi]])
                        else:
                            P.op(DVE, lambda e, b=b, cg=cg, oi=oi: e.tensor_copy(
                                out=ost[oi][:, cg * 512:(cg + 1) * 512], in_=ps[b][:, :]),
                                reads=[PSB[b]], writes=[OST[oi]])
                    r0 = t0 - CTX + tb * 128
                    P.dma(POOL, lambda e, oi=oi, r0=r0: [e.dma_start(out=out[r0:r0 + 128, :], in_=ost[oi])],
                          reads=[OST[oi]], writes=[OUTB], owner=OST[oi])

        for ti, (t0, w) in enumerate(tiles):
            mf_tile(ti, t0, w)

    prologue()
    cast_weights(0)
    phase0()
    adaln()
    for l in range(L):
        last = (l == L - 1)
        need_ctx = not last
        phase_p1(l)
        if l + 1 < L:
            cast_weights(l + 1)
        phase_conv(l)
        phase_rnn(l)
        phase_attn(l, need_ctx)
        phase_mf(l, need_ctx, last)
    snap = P.snapshot()
    fin = P.op(SP, None)
    fin.deps = snap
    P.emit(nc)
    return nc, P


def _vec16(v):
    return np.ascontiguousarray(v.reshape(-1, 128).T)


def pack_layer_weights(w_in, wa, wb, wc, wo, wg, wu, wd):
    out = np.empty(WTOT, np.float32)
    idx = np.arange(1024).reshape(8, 2, 2, 2, 16)
    sw = idx[:, :, :, ::-1, :].reshape(-1)

    def blk(cols):
        m = w_in[:, cols]
        return m.reshape(16, 128, 512).transpose(1, 0, 2).reshape(-1)
    o = OFF_IN
    blocks = []
    for base in (0, 1024):
        for g in range(2):
            c = np.arange(g * 512, (g + 1) * 512)
            blocks.append(base + c)
            blocks.append(base + sw[c])
    for vb in range(2):
        blocks.append(2048 + np.arange(vb * 512, (vb + 1) * 512))
    for pb in range(22):
        blocks.append(3072 + np.arange(pb * 512, (pb + 1) * 512))
    assert len(blocks) == 32
    for cols in blocks:
        out[o:o + 1048576] = blk(cols)
        o += 1048576
    assert o == OFF_BR
    for q in range(8):
        parts = []
        for wr in (wa, wb, wc):
            m = wr[:, q * 256:(q + 1) * 256].reshape(8, 128, 256).transpose(1, 0, 2)
            parts.append(m)
        out[o:o + 786432] = np.stack(parts, axis=1).reshape(-1)
        o += 786432
    assert o == OFF_WO
    for g in range(4):
        out[o:o + 1048576] = wo[:, g * 512:(g + 1) * 512].reshape(16, 128, 512).transpose(1, 0, 2).reshape(-1)
        o += 1048576
    assert o == OFF_GU
    for jb in range(22):
        parts = []
        for jj in range(2):
            jx = jb * 2 + jj
            gg = wg[:, jx * 128:(jx + 1) * 128].reshape(16, 128, 128).transpose(1, 0, 2)
            uu = wu[:, jx * 128:(jx + 1) * 128].reshape(16, 128, 128).transpose(1, 0, 2)
            parts.append(np.stack([gg, uu], axis=1))
        out[o:o + 1048576] = np.stack(parts, axis=1).reshape(-1)
        o += 1048576
    assert o == OFF_WD
    for n in range(16):
        out[o:o + 720896] = wd[:, n * 128:(n + 1) * 128].reshape(44, 128, 128).transpose(1, 0, 2).reshape(-1)
        o += 720896
    assert o == WTOT
    return out.reshape(WTOT // 2048, 2048)


def pack_vecs(l, inp):
    v = np.zeros((128, NV), np.float32)
    v[:, V_BADA:V_BADA + 96] = np.concatenate([_vec16(inp["b_ada"][l, k]) for k in range(6)], axis=1)
    v[:, V_GPM:V_GPM + 16] = _vec16(inp["g_pre_mix"][l])
    v[:, V_GQM:V_GQM + 16] = _vec16(inp["g_post_mix"][l])
    v[:, V_GPF:V_GPF + 16] = _vec16(inp["g_pre_ffn"][l])
    v[:, V_GQF:V_GQF + 16] = _vec16(inp["g_post_ffn"][l])
    v[:, V_CW:V_CW + 24] = np.concatenate([_vec16(inp["conv_w"][l, k]) for k in range(3)], axis=1)
    v[:, V_CB:V_CB + 8] = _vec16(inp["conv_b"][l])
    v[:, V_RW:V_RW + 32] = np.concatenate([_vec16(inp["rnn_conv_w"][l, k]) for k in range(4)], axis=1)
    v[:, V_RB:V_RB + 8] = _vec16(inp["rnn_conv_b"][l])
    v[:, V_BA:V_BA + 16] = np.concatenate([_vec16(inp["rg_ba"][l, d]) for d in range(2)], axis=1)
    v[:, V_BX:V_BX + 16] = np.concatenate([_vec16(inp["rg_bx"][l, d]) for d in range(2)], axis=1)
    v[:, V_LAM:V_LAM + 16] = np.concatenate([_vec16(inp["rg_lambda"][l, d]) for d in range(2)], axis=1)
    v[:, V_BM:V_BM + 48] = np.concatenate([_vec16(inp["b_merge"][l, r]) for r in range(3)], axis=1)
    v[:, V_SUB] = inp["diff_subln"][l]
    return v


def rope_tables(S):
    t = np.arange(S)
    row = (t // 64).astype(np.float32)
    col = (t % 64).astype(np.float32)
    inv = (np.float32(10000.0) ** (-np.arange(16, dtype=np.float32) / np.float32(16))).astype(np.float32)
    tab = np.zeros((2, 128, S), np.float32)
    for p in range(128):
        r = p % 64
        axis, half, f = r // 32, (r % 32) // 16, r % 16
        pos = row if axis == 0 else col
        ang = (pos * inv[f]).astype(np.float32)
        tab[0, p] = np.cos(ang.astype(np.float64))
        tab[1, p] = np.sin(ang.astype(np.float64)) * (-1.0 if half == 0 else 1.0)
    return tab


def prepare_shared(inp, L):
    sh = {}
    sh["wada"] = np.ascontiguousarray(
        inp["w_ada"][:L].reshape(L, 16, 128, 24, 512).transpose(0, 3, 2, 1, 4)).reshape(L, 24, 128, 16 * 512)
    sh["wpack"] = np.stack([pack_layer_weights(inp["w_in"][l], inp["w_branch_a"][l], inp["w_branch_b"][l],
                                               inp["w_branch_c"][l], inp["w_o"][l], inp["w_ffn_gate"][l],
                                               inp["w_ffn_up"][l], inp["w_ffn_down"][l]) for l in range(L)])
    rg = np.stack([inp["rg_wa"][:L], inp["rg_wx"][:L]], axis=1)
    sh["rgw"] = np.ascontiguousarray(rg.transpose(0, 4, 1, 2, 3, 5)).reshape(L, 128, 32 * 128)
    sh["vecs"] = np.ascontiguousarray(np.stack([pack_vecs(l, inp) for l in range(L)], axis=1)).reshape(128, L * NV)
    sh["dlam"] = np.ascontiguousarray(inp["diff_lambda"][:L]).reshape(-1)
    sh["ident"] = np.eye(128, dtype=np.float32)
    return sh


_CACHE = {}


def run(inp, S, CTX, L, B, trace=False):
    inp = {k: np.asarray(v) for k, v in inp.items()}
    key = (S, CTX, L)
    if key not in _CACHE:
        _CACHE[key] = build_program(S, CTX, L)
    nc, _ = _CACHE[key]
    sh = prepare_shared(inp, L)
    sh["rope"] = rope_tables(S)
    in_maps = []
    for b in range(B):
        m = dict(sh)
        m["xb"] = np.ascontiguousarray(np.concatenate([inp["ctx"][b], inp["x"][b]], axis=0))
        m["cvec"] = np.ascontiguousarray(np.concatenate([_vec16(inp["c"][b]), _vec16(inp["c_ctx"])], axis=1))
        in_maps.append(m)
    res = run_bass_kernel_spmd(nc, in_maps, core_ids=list(range(B)), trace=trace)
    outp = np.stack([np.asarray(r["out"]) for r in res.results], axis=0)
    return outp.astype(np.float32), res


def kernel(**inputs):
    outp, _ = run(inputs, 4096, 256, 4, 8)
    return outp
```

```python
import math
from contextlib import ExitStack
import numpy as np
import concourse.bass as bass
import concourse.mybir as mybir
from concourse.bass_utils import run_bass_kernel_spmd

F32, BF16 = mybir.dt.float32, mybir.dt.bfloat16
ALU = mybir.AluOpType
AF = mybir.ActivationFunctionType
AX = mybir.AxisListType
PE, ACT, DVE, POOL, SP = "tensor", "scalar", "vector", "gpsimd", "sync"

D = 2048
NCK = 16
HEADS = 8
FFH = 5632
NJ = 44
EPS = 1e-6
RG_C = 8.0
NV = 330
WTOT = 78643200
OFF_IN = 0
OFF_BR = OFF_IN + 32 * 1048576
OFF_WO = OFF_BR + 8 * 786432
OFF_GU = OFF_WO + 4 * 1048576
OFF_WD = OFF_GU + 22 * 1048576
assert OFF_WD + 16 * 720896 == WTOT
V_BADA, V_GPM, V_GQM, V_GPF, V_GQF = 0, 96, 112, 128, 144
V_CW, V_CB, V_RW, V_RB, V_BA, V_BX, V_LAM, V_BM, V_SUB = 160, 184, 192, 224, 232, 248, 264, 280, 328
C_Q, C_K, C_CX, C_CC, C_CB, C_RG, C_RX, C_MG = 0, 8, 16, 24, 32, 40, 48, 56
NPT = 104
UNIT = 256


class Buf:
    __slots__ = ("name", "w", "rs", "sem", "cnt")

    def __init__(self, name):
        self.name = name
        self.w = None
        self.rs = []
        self.sem = None
        self.cnt = 0


class Op:
    __slots__ = ("eng", "fn", "deps", "need", "owner", "ndma", "sig", "idx_")


class Prog:
    SEM_LIMIT = 30000

    def __init__(self):
        self.ops = []
        self.last = {}
        self.dma_last = {}

    def _add(self, eng, fn, reads, writes, owner, ndma):
        op = Op()
        op.eng, op.fn, op.owner, op.ndma, op.need, op.sig = eng, fn, owner, ndma, False, None
        deps = {}
        is_dma = owner is not None

        def dep(p):
            if p is None:
                return
            if (not is_dma) and p.owner is None and p.eng == eng and eng == PE:
                return
            deps[id(p)] = p

        for b in reads:
            dep(b.w)
        for b in writes:
            dep(b.w)
            for r in b.rs:
                dep(r)
        best = {}
        out = []
        for p in deps.values():
            if p.owner is None:
                q = best.get(p.eng)
                if q is None or p.idx_ > q.idx_:
                    best[p.eng] = p
            else:
                out.append(p)
        out.extend(best.values())
        for p in out:
            if p.owner is None:
                p.need = True
        op.deps = out
        op.idx_ = len(self.ops)
        for b in reads:
            if not is_dma:
                b.rs = [r for r in b.rs if not (r.owner is None and r.eng == eng)]
            b.rs.append(op)
        for b in writes:
            b.w = op
            b.rs = []
        self.ops.append(op)
        if is_dma:
            self.dma_last[id(owner)] = op
        else:
            self.last[eng] = op
        return op

    def op(self, eng, fn, reads=(), writes=()):
        return self._add(eng, fn, reads, writes, None, 0)

    def seq(self, eng, fns, reads=(), writes=()):
        r = None
        for f in fns:
            r = self._add(eng, f, reads, writes, None, 0)
        return r

    def dma(self, q, fn, reads, writes, owner, ndma=1):
        return self._add(q, fn, reads, writes, owner, ndma)

    def snapshot(self):
        snap = list(self.last.values()) + list(self.dma_last.values())
        for p in snap:
            if p.owner is None:
                p.need = True
        return snap

    def phase_begin(self, bufs):
        snap = self.snapshot()
        for b in bufs:
            b.w = None
            b.rs = list(snap)

    def emit(self, nc):
        sem_names = []
        eng_state = {}
        for op in self.ops:
            if op.owner is not None:
                o = op.owner
                if o.sem is None:
                    o.sem = len(sem_names)
                    sem_names.append("d%d" % o.sem)
                o.cnt += 16 * op.ndma
                op.sig = (o.sem, o.cnt)
            elif op.need:
                st = eng_state.get(op.eng)
                if st is None or st[1] >= self.SEM_LIMIT:
                    st = [len(sem_names), 0]
                    sem_names.append("e%d" % st[0])
                    eng_state[op.eng] = st
                st[1] += 1
                op.sig = (st[0], st[1])
        per = {}
        for op in self.ops:
            per.setdefault(op.eng, []).append(op)
        self.nsem = len(sem_names)
        with ExitStack() as es:
            sems = [es.enter_context(nc.semaphore(n)) for n in sem_names]
            block = es.enter_context(nc.Block())

            def run(engname):
                def body(e):
                    waited = {}
                    for op in per.get(engname, []):
                        for p in op.deps:
                            s, v = p.sig
                            if waited.get(s, 0) < v:
                                e.wait_ge(sems[s], v)
                                waited[s] = v
                        if op.fn is None:
                            continue
                        r = op.fn(e)
                        if op.owner is not None:
                            assert len(r) == op.ndma, (len(r), op.ndma)
                            for ins in r:
                                ins.then_inc(sems[op.sig[0]], 16)
                        elif op.sig is not None:
                            r.then_inc(sems[op.sig[0]], 1)
                return body

            block.tensor(run(PE))
            block.scalar(run(ACT))
            block.vector(run(DVE))
            block.gpsimd(run(POOL))
            block.sync(run(SP))


class Arena:
    def __init__(self, ap, base, limit):
        self.ap, self.base, self.cur, self.limit = ap, base, base, limit

    def reset(self):
        self.cur = self.base

    def f32(self, n):
        a = self.ap[:, self.cur:self.cur + n]
        self.cur += n
        assert self.cur <= self.limit, ("arena overflow", self.cur, self.limit)
        return a

    def bf16(self, n):
        m = (n + 1) // 2
        a = self.ap[:, self.cur:self.cur + m].bitcast(BF16)
        self.cur += m
        assert self.cur <= self.limit, ("arena overflow", self.cur, self.limit)
        return a[:, 0:n]


def build_program(S, CTX, L, LAYER0=0, DEBUG=False):
    NT = CTX + S
    NCH = NT // 128
    NU = NT // UNIT
    nc = bass.Bass("TRN2", target_bir_lowering=False)
    P = Prog()

    def dram_in(name, shape, dt=F32):
        return nc.dram_tensor(name, list(shape), dt, kind="ExternalInput").ap()

    xb = dram_in("xb", [NT, D])
    cvec = dram_in("cvec", [128, 2 * NCK])
    wada = dram_in("wada", [L, 24, 128, NCK * 512])
    wpack = dram_in("wpack", [L, WTOT // 2048, 2048])
    rgw = dram_in("rgw", [L, 128, 32 * 128])
    vecs_d = dram_in("vecs", [128, L * NV])
    dlam = dram_in("dlam", [L * 256])
    rope = dram_in("rope", [2, 128, S])
    ident_d = dram_in("ident", [128, 128])
    out = nc.dram_tensor("out", [S, D], F32, kind="ExternalOutput").ap()

    IK = "ExternalOutput" if DEBUG else "Internal"
    hT = nc.dram_tensor("hT", [NCK, 128, NT], F32, kind=IK).ap()
    wbf = [nc.dram_tensor("wbf%d" % l, [WTOT // 2048, 2048], BF16, kind="Internal").ap() for l in range(L)]
    pT = nc.dram_tensor("pT", [NPT, 128, NT], BF16, kind=IK).ap()
    vS = nc.dram_tensor("vS", [HEADS, 128, NCH, 128], BF16, kind=IK).ap()
    yT = nc.dram_tensor("yT", [24, 128, NT], BF16, kind=IK).ap()
    wbf_flat = [wbf[l].rearrange("r c -> (r c)") for l in range(L)]

    HT = [Buf("HT%d" % u) for u in range(NU)]
    PT = [[Buf("PT") for u in range(NU)] for c in range(NPT)]
    VS = [[Buf("VS") for u in range(NU)] for h in range(HEADS)]
    YT = [[Buf("YT") for u in range(NU)] for c in range(24)]
    WBF = [Buf("WBF%d" % l) for l in range(L)]
    OUTB = Buf("OUT")

    def units(t0, w):
        return range(t0 // UNIT, (t0 + w + UNIT - 1) // UNIT)

    SB_F32 = 53200
    arena_t = nc.alloc_sbuf_tensor("arena", [128, SB_F32], F32)
    ar = arena_t.ap()
    ps_t = nc.alloc_psum_tensor("ps", [128, 8, 512], F32)
    psa = ps_t.ap()
    PSB = [Buf("PS%d" % i) for i in range(8)]
    BC = {}

    def CB(name):
        if name not in BC:
            BC[name] = Buf(name)
        return BC[name]
    ps = [psa[:, i, :] for i in range(8)]

    perm = Arena(ar, 0, SB_F32)
    WSLOT_E = 8192
    NSLOT = 3
    wslots = [perm.bf16(WSLOT_E) for _ in range(NSLOT)]
    WS = [Buf("W%d" % i) for i in range(NSLOT)]
    ident = perm.f32(128)
    ones32 = perm.f32(128)
    ones16 = perm.bf16(128)
    vecs = perm.f32(L * NV)
    modv = perm.f32(L * 6 * 2 * NCK)
    lamv = perm.f32(L * 4)
    cpv = perm.f32(L * 16)
    cp2v = perm.f32(L * 16)
    CONST = Buf("CONST")
    MODV = Buf("MODV")
    A = Arena(ar, perm.cur, SB_F32)
    slot_i = [0]

    def next_slot():
        s = slot_i[0] % NSLOT
        slot_i[0] += 1
        return s

    bank_i = [0]

    def next_bank(n=7):
        b = bank_i[0] % n
        bank_i[0] += 1
        return b

    def vcol(l, off, n=1):
        return vecs[:, l * NV + off: l * NV + off + n]

    def mod(l, kind, j, c):
        o = ((l * 6 + kind) * 2 + j) * NCK + c
        return modv[:, o:o + 1]

    K_A1, K_B1, K_G2, K_A3, K_B3, K_G5 = range(6)

    def load_w(l, off, nelem_pp, reads_extra=()):
        s = next_slot()
        src = wbf_flat[l][off: off + 128 * nelem_pp].rearrange("(p x) -> p x", p=128)
        dst = wslots[s][:, 0:nelem_pp]
        P.dma(SP, lambda e: [e.dma_start(out=dst, in_=src)], reads=[WBF[l]], writes=[WS[s]], owner=WS[s])
        return s

    def prologue():
        P.dma(SP, lambda e: [e.dma_start(out=ident, in_=ident_d),
                             e.dma_start(out=vecs, in_=vecs_d)],
              reads=[], writes=[CONST], owner=CONST, ndma=2)
        P.op(POOL, lambda e: e.memset(ones32, 1.0), writes=[CONST])
        P.op(POOL, lambda e: e.memset(ones16, 1.0), writes=[CONST])

    def cast_weights(l):
        R = WTOT // 2048
        step = 1280
        n = R // step
        assert n * step == R

        def fn(e):
            return [e.dma_start(out=wbf[l][i * step:(i + 1) * step, :], in_=wpack[l, i * step:(i + 1) * step, :])
                    for i in range(n)]
        P.dma(POOL, fn, reads=[], writes=[WBF[l]], owner=WBF[l], ndma=n)

    def phase0():
        A.reset()
        xin = [A.f32(D) for _ in range(2)]
        XIN = [CB("xin%d" % i) for i in range(2)]
        stg = [A.f32(NCK * 512) for _ in range(2)]
        STG = [CB("stg%d" % i) for i in range(2)]
        P.phase_begin(XIN + STG)
        k = 0
        ti = 0
        for t0 in range(0, NT, 512):
            w = min(512, NT - t0)
            sg = stg[ti % 2].rearrange("p (c t) -> p c t", c=NCK)
            SG = STG[ti % 2]
            for tb in range(w // 128):
                xi, XI = xin[k % 2], XIN[k % 2]
                k += 1
                r0 = t0 + tb * 128
                P.dma(SP, lambda e, xi=xi, r0=r0: [e.dma_start(out=xi, in_=xb[r0:r0 + 128, :])],
                      reads=[], writes=[XI], owner=XI)
                for cg in range(4):
                    b = next_bank()

                    def tr(e, xi=xi, b=b, cg=cg):
                        for i in range(4):
                            c = cg * 4 + i
                            r = e.transpose(ps[b][:, i * 128:(i + 1) * 128], xi[:, c * 128:(c + 1) * 128], ident)
                        return r
                    P.op(PE, tr, reads=[XI, CONST], writes=[PSB[b]])
                    dst = sg[:, cg * 4:(cg + 1) * 4, tb * 128:(tb + 1) * 128]
                    src = ps[b].rearrange("p (c t) -> p c t", c=4)
                    eng = ACT if (cg % 2 == 0) else DVE
                    if eng == ACT:
                        P.op(ACT, lambda e, dst=dst, src=src: e.activation(out=dst, in_=src, func=AF.Copy),
                             reads=[PSB[b]], writes=[SG])
                    else:
                        P.op(DVE, lambda e, dst=dst, src=src: e.tensor_copy(out=dst, in_=src),
                             reads=[PSB[b]], writes=[SG])
            P.dma(POOL, lambda e, sg=sg, t0=t0, w=w: [e.dma_start(
                out=hT[:, :, t0:t0 + w].rearrange("c p t -> p c t"), in_=sg[:, :, 0:w])],
                reads=[SG], writes=[HT[u] for u in units(t0, w)], owner=SG)
            ti += 1

    def adaln():
        A.reset()
        wsl = [A.f32(NCK * 512) for _ in range(2)]
        WSL = [CB("wada%d" % i) for i in range(2)]
        cv = A.f32(2 * NCK)
        xr = A.f32(2 * NCK)
        mt = A.f32(2 * 96)
        dl = A.f32(L * 256)
        tmp = A.f32(64)
        s12 = A.f32(4)
        tv = A.f32(32)
        SM = CB("adasmall")
        P.phase_begin(WSL + [SM])
        P.dma(SP, lambda e: [e.dma_start(out=cv, in_=cvec),
                             e.dma_start(out=dl, in_=dlam.partition_broadcast(128))],
              reads=[], writes=[SM], owner=SM, ndma=2)
        P.op(ACT, lambda e: e.activation(out=cv, in_=cv, func=AF.Silu), reads=[SM], writes=[SM])
        P.op(DVE, lambda e: e.tensor_copy(out=xr.rearrange("p (c j) -> p c j", j=2),
                                          in_=cv.rearrange("p (j c) -> p c j", j=2)), reads=[SM], writes=[SM])
        k = 0
        for l in range(L):
            for nb in range(24):
                s = k % 2
                k += 1
                P.dma(SP, lambda e, s=s, l=l, nb=nb: [e.dma_start(out=wsl[s], in_=wada[l, nb])],
                      reads=[], writes=[WSL[s]], owner=WSL[s])

                def mm(e, s=s, nb=nb):
                    w3 = wsl[s].rearrange("p (k n) -> p k n", k=NCK)
                    for n4 in range(4):
                        n = nb * 4 + n4
                        for kc in range(NCK):
                            r = e.matmul(ps[7][:, n * 2:n * 2 + 2], lhsT=w3[:, kc, n4 * 128:(n4 + 1) * 128],
                                         rhs=xr[:, kc * 2:kc * 2 + 2], start=(kc == 0), stop=(kc == NCK - 1))
                    return r
                P.op(PE, mm, reads=[WSL[s], SM], writes=[PSB[7]])
            pv = ps[7][:, 0:192].rearrange("p (n j) -> p n j", j=2)
            for j in range(2):
                P.op(DVE, lambda e, j=j, l=l: e.tensor_tensor(out=mt[:, j * 96:(j + 1) * 96], in0=pv[:, :, j],
                                                              in1=vcol(l, V_BADA, 96), op=ALU.add),
                     reads=[PSB[7], CONST], writes=[SM])

                def derive(e, j=j, l=l):
                    M = lambda k: mt[:, j * 96 + k * 16: j * 96 + (k + 1) * 16]
                    dst = lambda kind: modv[:, ((l * 6 + kind) * 2 + j) * NCK:((l * 6 + kind) * 2 + j + 1) * NCK]
                    e.scalar_tensor_tensor(out=dst(K_A1), in0=M(1), scalar=1.0, in1=vcol(l, V_GPM, 16),
                                           op0=ALU.add, op1=ALU.mult)
                    e.tensor_copy(out=dst(K_B1), in_=M(0))
                    e.tensor_tensor(out=dst(K_G2), in0=M(2), in1=vcol(l, V_GQM, 16), op=ALU.mult)
                    e.scalar_tensor_tensor(out=dst(K_A3), in0=M(4), scalar=1.0, in1=vcol(l, V_GPF, 16),
                                           op0=ALU.add, op1=ALU.mult)
                    e.tensor_copy(out=dst(K_B3), in_=M(3))
                    return e.tensor_tensor(out=dst(K_G5), in0=M(5), in1=vcol(l, V_GQF, 16), op=ALU.mult)
                P.op(DVE, derive, reads=[SM, CONST], writes=[MODV])
            lam_init = 0.8 - 0.6 * math.exp(-0.3 * (l + LAYER0))

            d4 = dl[:, l * 256:(l + 1) * 256]
            P.seq(DVE, [
                lambda e, d4=d4: e.tensor_tensor(out=tmp, in0=d4[:, 0:64], in1=d4[:, 64:128], op=ALU.mult),
                lambda e: e.tensor_reduce(out=s12[:, 0:1], in_=tmp, axis=AX.X, op=ALU.add),
                lambda e, d4=d4: e.tensor_tensor(out=tmp, in0=d4[:, 128:192], in1=d4[:, 192:256], op=ALU.mult),
                lambda e: e.tensor_reduce(out=s12[:, 1:2], in_=tmp, axis=AX.X, op=ALU.add)],
                reads=[SM], writes=[SM])
            P.op(ACT, lambda e: e.activation(out=s12[:, 2:4], in_=s12[:, 0:2], func=AF.Exp), reads=[SM], writes=[SM])

            P.seq(DVE, [
                lambda e: e.tensor_tensor(out=s12[:, 0:1], in0=s12[:, 3:4], in1=s12[:, 2:3], op=ALU.subtract),
                lambda e, l=l, lam_init=lam_init: e.tensor_scalar(out=lamv[:, l * 4:l * 4 + 1], in0=s12[:, 0:1],
                                                                  scalar1=-lam_init, scalar2=None, op0=ALU.add),
                lambda e, l=l, lam_init=lam_init: e.tensor_scalar(out=lamv[:, l * 4 + 1:l * 4 + 2], in0=vcol(l, V_SUB, 1),
                                                                  scalar1=(1.0 - lam_init), scalar2=None, op0=ALU.mult)],
                reads=[SM, CONST], writes=[SM, MODV])
            P.op(ACT, lambda e, l=l: e.activation(out=tv[:, 0:16], in_=vcol(l, V_LAM, 16), func=AF.Exp, scale=-1.0),
                 reads=[CONST, SM], writes=[SM])
            P.op(ACT, lambda e: e.activation(out=tv[:, 16:32], in_=tv[:, 0:16], func=AF.Ln, bias=1.0, scale=1.0),
                 reads=[SM], writes=[SM])
            P.op(DVE, lambda e, l=l: e.tensor_scalar(out=cpv[:, l * 16:(l + 1) * 16], in0=tv[:, 16:32],
                                                     scalar1=-RG_C, scalar2=None, op0=ALU.mult),
                 reads=[SM], writes=[MODV])
            P.op(DVE, lambda e, l=l: e.tensor_scalar(out=cp2v[:, l * 16:(l + 1) * 16], in0=tv[:, 16:32],
                                                     scalar1=-2.0 * RG_C, scalar2=None, op0=ALU.mult),
                 reads=[SM], writes=[MODV])

    def rstd_from(psb, b, w, dstbuf, DST, n_feat):
        P.op(ACT, lambda e: e.activation(out=dstbuf[:, :w], in_=ps[b][:, :w], func=AF.Sqrt, scale=1.0 / n_feat,
                                         bias=EPS), reads=[PSB[b]], writes=[DST])
        P.op(DVE, lambda e: e.reciprocal(out=dstbuf[:, :w], in_=dstbuf[:, :w]), reads=[DST], writes=[DST])

    def p1_tiles():
        t = [(0, CTX)] if CTX else []
        for t0 in range(CTX, NT, 512):
            t.append((t0, min(512, NT - t0)))
        return t

    def phase_p1(l):
        A.reset()
        hin = A.f32(NCK * 512)
        hin3 = hin.rearrange("p (c t) -> p c t", c=NCK)
        HIN = [CB("hin%d" % c) for c in range(NCK)]
        uT = A.bf16(NCK * 512)
        uT3 = uT.rearrange("p (c t) -> p c t", c=NCK)
        UT = [CB("uT%d" % c) for c in range(NCK)]
        sq = [A.f32(512) for _ in range(2)]
        SQ = [CB("sq%d" % i) for i in range(2)]
        rstd = A.f32(512)
        RS = CB("rstd")
        cs = [A.f32(1024) for _ in range(2)]
        CS = [CB("cs%d" % i) for i in range(2)]
        stg = [A.bf16(4 * 512) for _ in range(3)]
        STG = [CB("pstg%d" % i) for i in range(3)]
        rt = [A.f32(512) for _ in range(4)]
        RT = [CB("rt%d" % i) for i in range(4)]
        vst = [A.bf16(4 * 512) for _ in range(2)]
        VST = [CB("vst%d" % i) for i in range(2)]
        P.phase_begin(HIN + UT + SQ + [RS] + CS + STG + RT + VST)
        stg_i = [0]
        rt_i = [0]
        vst_i = [0]
        ev_i = [0]
        def p1_tile(ti, t0, w):
            is_ctx = t0 < CTX
            j = 1 if is_ctx else 0
            csb = CSB = None
            P.dma(SP, lambda e, t0=t0, w=w: [e.dma_start(out=hin3[:, :, 0:w],
                                                         in_=hT[:, :, t0:t0 + w].rearrange("c p t -> p c t"))],
                  reads=[HT[u] for u in units(t0, w)], writes=HIN, owner=HIN[0])
            if not is_ctx:
                csb, CSB = cs[ti % 2], CS[ti % 2]
                p0 = t0 - CTX
                P.dma(SP, lambda e, csb=csb, p0=p0, w=w: [
                    e.dma_start(out=csb[:, 0:w], in_=rope[0, :, p0:p0 + w]),
                    e.dma_start(out=csb[:, 512:512 + w], in_=rope[1, :, p0:p0 + w])],
                    reads=[], writes=[CSB], owner=CSB, ndma=2)
            for c in range(NCK):
                sb, SBF = sq[c % 2], SQ[c % 2]
                P.op(ACT, lambda e, sb=sb, c=c, w=w: e.activation(out=sb[:, :w], in_=hin3[:, c, :w], func=AF.Square),
                     reads=[HIN[c]], writes=[SBF])
                P.op(PE, lambda e, sb=sb, c=c, w=w: e.matmul(ps[7][:, :w], lhsT=ones32, rhs=sb[:, :w],
                                                             start=(c == 0), stop=(c == NCK - 1)),
                     reads=[SBF, CONST], writes=[PSB[7]])
            rstd_from(ps, 7, w, rstd, RS, D)
            for c in range(NCK):
                P.op(DVE, lambda e, c=c, w=w: e.tensor_tensor(out=hin3[:, c, :w], in0=hin3[:, c, :w], in1=rstd[:, :w],
                                                              op=ALU.mult), reads=[HIN[c], RS], writes=[HIN[c]])
                P.op(ACT, lambda e, c=c, w=w, j=j: e.activation(out=uT3[:, c, :w], in_=hin3[:, c, :w], func=AF.Identity,
                                                                scale=mod(l, K_A1, j, c), bias=mod(l, K_B1, j, c)),
                     reads=[HIN[c], MODV], writes=[UT[c]])

            def mm_group(s, i, b, w=w):
                def fn(e):
                    w3 = wslots[s].rearrange("p (k n) -> p k n", k=NCK)
                    for kc in range(NCK):
                        r = e.matmul(ps[b][:, :w], lhsT=w3[:, kc, i * 128:(i + 1) * 128], rhs=uT3[:, kc, :w],
                                     start=(kc == 0), stop=(kc == NCK - 1))
                    return r
                P.op(PE, fn, reads=[WS[s]] + UT, writes=[PSB[b]])

            def store_stage(si, ch0, w=w, t0=t0):
                sg = stg[si].rearrange("p (c t) -> p c t", c=4)
                P.dma(POOL, lambda e: [e.dma_start(out=pT[ch0:ch0 + 4, :, t0:t0 + w].rearrange("c p t -> p c t"),
                                                   in_=sg[:, :, 0:w])],
                      reads=[STG[si]], writes=[PT[ch0 + i][u] for i in range(4) for u in units(t0, w)], owner=STG[si])

            def evac(dst, b, DSTB, w=w):
                ev_i[0] += 1
                if ev_i[0] % 2 == 0:
                    P.op(ACT, lambda e: e.activation(out=dst, in_=ps[b][:, :w], func=AF.Copy),
                         reads=[PSB[b]], writes=[DSTB])
                else:
                    P.op(DVE, lambda e: e.tensor_copy(out=dst, in_=ps[b][:, :w]), reads=[PSB[b]], writes=[DSTB])

            for kind in range(2):
                for g in range(2):
                    blk = kind * 4 + g * 2
                    s_pl = load_w(l, OFF_IN + blk * 1048576, 8192)
                    s_sw = None
                    if not is_ctx:
                        s_sw = load_w(l, OFF_IN + (blk + 1) * 1048576, 8192)
                    si = stg_i[0] % 3
                    stg_i[0] += 1
                    sg = stg[si].rearrange("p (c t) -> p c t", c=4)
                    for i in range(4):
                        b1 = next_bank()
                        mm_group(s_pl, i, b1)
                        if is_ctx:
                            evac(sg[:, i, :w], b1, STG[si])
                        else:
                            b2 = next_bank()
                            mm_group(s_sw, i, b2)
                            r1 = rt_i[0] % 4
                            r2 = (rt_i[0] + 1) % 4
                            rt_i[0] += 2
                            P.op(DVE, lambda e, r1=r1, b1=b1, csb=csb: e.tensor_tensor(
                                out=rt[r1][:, :w], in0=ps[b1][:, :w], in1=csb[:, 0:w], op=ALU.mult),
                                reads=[PSB[b1], CSB], writes=[RT[r1]])
                            P.op(DVE, lambda e, r2=r2, b2=b2, csb=csb: e.tensor_tensor(
                                out=rt[r2][:, :w], in0=ps[b2][:, :w], in1=csb[:, 512:512 + w], op=ALU.mult),
                                reads=[PSB[b2], CSB], writes=[RT[r2]])
                            P.op(POOL, lambda e, r1=r1, r2=r2, sg=sg, i=i: e.tensor_tensor(
                                out=sg[:, i, :w], in0=rt[r1][:, :w], in1=rt[r2][:, :w], op=ALU.add),
                                reads=[RT[r1], RT[r2]], writes=[STG[si]])
                    store_stage(si, (C_Q if kind == 0 else C_K) + g * 4)
            for vb in range(2):
                s = load_w(l, OFF_IN + (8 + vb) * 1048576, 8192)
                vi = vst_i[0] % 2
                vst_i[0] += 1
                vt = vst[vi].rearrange("p (b n) -> p b n", b=4)
                ntb = w // 128
                for tb in range(ntb):
                    b = next_bank()

                    def fn(e, s=s, tb=tb, b=b):
                        w3 = wslots[s].rearrange("p (k n) -> p k n", k=NCK)
                        for kc in range(NCK):
                            r = e.matmul(ps[b][:, :], lhsT=uT3[:, kc, tb * 128:(tb + 1) * 128], rhs=w3[:, kc, :],
                                         start=(kc == 0), stop=(kc == NCK - 1))
                        return r
                    P.op(PE, fn, reads=[WS[s]] + UT, writes=[PSB[b]])
                    ev_i[0] += 1
                    if ev_i[0] % 2 == 0:
                        P.op(ACT, lambda e, vt=vt, tb=tb, b=b: e.activation(out=vt[:, tb, :], in_=ps[b][:, :], func=AF.Copy),
                             reads=[PSB[b]], writes=[VST[vi]])
                    else:
                        P.op(DVE, lambda e, vt=vt, tb=tb, b=b: e.tensor_copy(out=vt[:, tb, :], in_=ps[b][:, :]),
                             reads=[PSB[b]], writes=[VST[vi]])
                ch0 = t0 // 128
                P.dma(POOL, lambda e, vt=vt, vb=vb, ch0=ch0, ntb=ntb: [e.dma_start(
                    out=vS[vb * 4 + hh, :, ch0:ch0 + ntb, :],
                    in_=vt[:, 0:ntb, hh * 128:(hh + 1) * 128]) for hh in range(4)],
                    reads=[VST[vi]], writes=[VS[vb * 4 + h][u] for h in range(4) for u in units(t0, w)],
                    owner=VST[vi], ndma=4)
            for pb in range(22):
                s = load_w(l, OFF_IN + (10 + pb) * 1048576, 8192)
                si = stg_i[0] % 3
                stg_i[0] += 1
                sg = stg[si].rearrange("p (c t) -> p c t", c=4)
                for i in range(4):
                    b = next_bank()
                    mm_group(s, i, b)
                    evac(sg[:, i, :w], b, STG[si])
                store_stage(si, C_CX + pb * 4)

        for ti, (t0, w) in enumerate(p1_tiles()):
            p1_tile(ti, t0, w)

    def segs():
        s = []
        if CTX:
            s.append((0, CTX))
        s.append((CTX, NT))
        return s

    def phase_conv(l):
        A.reset()
        xin = [A.bf16(3 * NT) for _ in range(2)]
        XIN = [CB("cin%d" % i) for i in range(2)]
        Z = [A.f32(NT) for _ in range(2)]
        Y = [A.f32(NT) for _ in range(2)]
        ZB = [CB("Z%d" % i) for i in range(2)]
        YB = [CB("Y%d" % i) for i in range(2)]
        ob = [A.bf16(NT) for _ in range(2)]
        OB = [CB("cob%d" % i) for i in range(2)]
        P.phase_begin(XIN + ZB + YB + OB)
        allu = range(NU)
        for c in range(8):
            i = c % 2
            x3 = xin[i].rearrange("p (k t) -> p k t", k=3)
            P.dma(SP, lambda e, x3=x3, c=c: [e.dma_start(out=x3[:, 0, :], in_=pT[C_CX + c]),
                                             e.dma_start(out=x3[:, 1, :], in_=pT[C_CC + c]),
                                             e.dma_start(out=x3[:, 2, :], in_=pT[C_CB + c])],
                  reads=[PT[C_CX + c][u] for u in allu] + [PT[C_CC + c][u] for u in allu] + [PT[C_CB + c][u] for u in allu],
                  writes=[XIN[i]], owner=XIN[i], ndma=3)
            z, y = Z[i], Y[i]
            P.op(DVE, lambda e, z=z, x3=x3: e.tensor_tensor(out=z, in0=x3[:, 1, :], in1=x3[:, 0, :], op=ALU.mult),
                 reads=[XIN[i]], writes=[ZB[i]])
            P.op(ACT, lambda e, z=z, y=y, c=c: e.activation(out=y, in_=z, func=AF.Identity,
                                                            scale=vcol(l, V_CW + 8 + c), bias=vcol(l, V_CB + c)),
                 reads=[ZB[i], CONST], writes=[YB[i]])

            fns = []
            for (a, b) in segs():
                fns.append(lambda e, z=z, y=y, c=c, a=a, b=b: e.scalar_tensor_tensor(
                    out=y[:, a + 1:b], in0=z[:, a:b - 1], scalar=vcol(l, V_CW + c), in1=y[:, a + 1:b],
                    op0=ALU.mult, op1=ALU.add))
                fns.append(lambda e, z=z, y=y, c=c, a=a, b=b: e.scalar_tensor_tensor(
                    out=y[:, a:b - 1], in0=z[:, a + 1:b], scalar=vcol(l, V_CW + 16 + c), in1=y[:, a:b - 1],
                    op0=ALU.mult, op1=ALU.add))
            P.seq(DVE, fns, reads=[ZB[i], YB[i], CONST], writes=[YB[i]])
            P.op(DVE, lambda e, y=y, x3=x3, o=ob[i]: e.tensor_tensor(out=o, in0=x3[:, 2, :], in1=y, op=ALU.mult),
                 reads=[YB[i], XIN[i]], writes=[OB[i]])
            P.dma(POOL, lambda e, o=ob[i], c=c: [e.dma_start(out=yT[8 + c], in_=o)],
                  reads=[OB[i]], writes=[YT[8 + c][u] for u in allu], owner=OB[i])

    def phase_rnn(l):
        A.reset()
        xg = [A.bf16(2 * NT) for _ in range(2)]
        XG = [CB("xg%d" % i) for i in range(2)]
        rg = A.f32(32 * 128)
        RGW = CB("rgw")
        XR, AA, GG, TT, HF = [A.f32(NT) for _ in range(5)]
        BXR, BA, BG, BT, BH = [CB(n) for n in ("XR", "AA", "GG", "TT", "HF")]
        ob0 = A.bf16(NT)
        ob = [ob0, ob0]
        OB0 = CB("rob")
        OB = [OB0, OB0]
        P.phase_begin(XG + [RGW, BXR, BA, BG, BT, BH, OB0])
        rg3 = rg.rearrange("p (i k) -> p i k", i=32)
        P.dma(SP, lambda e: [e.dma_start(out=rg, in_=rgw[l])], reads=[], writes=[RGW], owner=RGW)
        allu = range(NU)
        tiles = p1_tiles()
        for n in range(8):
            i = n % 2
            x3 = xg[i].rearrange("p (k t) -> p k t", k=2)
            P.dma(SP, lambda e, x3=x3, n=n: [e.dma_start(out=x3[:, 0, :], in_=pT[C_RG + n]),
                                             e.dma_start(out=x3[:, 1, :], in_=pT[C_RX + n])],
                  reads=[PT[C_RG + n][u] for u in allu] + [PT[C_RX + n][u] for u in allu],
                  writes=[XG[i]], owner=XG[i], ndma=2)
            xx = x3[:, 1, :]
            gt = x3[:, 0, :]
            P.op(ACT, lambda e, xx=xx, n=n: e.activation(out=XR, in_=xx, func=AF.Identity,
                                                         scale=vcol(l, V_RW + 16 + n), bias=vcol(l, V_RB + n)),
                 reads=[XG[i], CONST], writes=[BXR])

            fns = []
            for (a, b) in segs():
                fns.append(lambda e, xx=xx, n=n, a=a, b=b: e.scalar_tensor_tensor(
                    out=XR[:, a + 2:b], in0=xx[:, a:b - 2], scalar=vcol(l, V_RW + n), in1=XR[:, a + 2:b],
                    op0=ALU.mult, op1=ALU.add))
                fns.append(lambda e, xx=xx, n=n, a=a, b=b: e.scalar_tensor_tensor(
                    out=XR[:, a + 1:b], in0=xx[:, a:b - 1], scalar=vcol(l, V_RW + 8 + n), in1=XR[:, a + 1:b],
                    op0=ALU.mult, op1=ALU.add))
                fns.append(lambda e, xx=xx, n=n, a=a, b=b: e.scalar_tensor_tensor(
                    out=XR[:, a:b - 1], in0=xx[:, a + 1:b], scalar=vcol(l, V_RW + 24 + n), in1=XR[:, a:b - 1],
                    op0=ALU.mult, op1=ALU.add))
            P.seq(DVE, fns, reads=[XG[i], BXR, CONST], writes=[BXR])
            for d in range(2):
                for (t0, w) in tiles:
                    for gi, (dst, DSTB, voff) in enumerate(((AA, BA, V_BA), (GG, BG, V_BX))):
                        b = next_bank()
                        widx = gi * 16 + d * 8 + n
                        P.op(PE, lambda e, b=b, widx=widx, t0=t0, w=w: e.matmul(
                            ps[b][:, :w], lhsT=rg3[:, widx, :], rhs=XR[:, t0:t0 + w], start=True, stop=True),
                            reads=[RGW, BXR], writes=[PSB[b]])
                        P.op(ACT, lambda e, b=b, dst=dst, t0=t0, w=w, voff=voff, d=d, n=n: e.activation(
                            out=dst[:, t0:t0 + w], in_=ps[b][:, :w], func=AF.Sigmoid, bias=vcol(l, voff + d * 8 + n),
                            scale=1.0), reads=[PSB[b], CONST], writes=[DSTB])
                cp = cpv[:, l * 16 + d * 8 + n: l * 16 + d * 8 + n + 1]
                cp2 = cp2v[:, l * 16 + d * 8 + n: l * 16 + d * 8 + n + 1]
                P.op(ACT, lambda e, cp2=cp2: e.activation(out=TT, in_=AA, func=AF.Exp, scale=cp2),
                     reads=[BA, MODV], writes=[BT])
                P.op(ACT, lambda e, cp=cp: e.activation(out=AA, in_=AA, func=AF.Exp, scale=cp),
                     reads=[BA, MODV], writes=[BA])
                P.op(DVE, lambda e: e.tensor_scalar(out=TT, in0=TT, scalar1=-1.0, scalar2=1.0, op0=ALU.mult, op1=ALU.add),
                     reads=[BT], writes=[BT])
                P.op(ACT, lambda e: e.activation(out=TT, in_=TT, func=AF.Sqrt, bias=1e-30, scale=1.0),
                     reads=[BT], writes=[BT])
                P.op(DVE, lambda e: e.tensor_tensor(out=GG, in0=GG, in1=XR, op=ALU.mult), reads=[BG, BXR], writes=[BG])
                P.op(DVE, lambda e: e.tensor_tensor(out=GG, in0=GG, in1=TT, op=ALU.mult), reads=[BG, BT], writes=[BG])
                if d == 0:
                    fns = []
                    if CTX:
                        fns.append(lambda e: e.tensor_tensor_scan(out=HF[:, 0:CTX], data0=AA[:, 0:CTX], data1=GG[:, 0:CTX],
                                                                  initial=0.0, op0=ALU.mult, op1=ALU.add))
                    init = HF[:, CTX - 1:CTX] if CTX else 0.0
                    fns.append(lambda e, init=init: e.tensor_tensor_scan(
                        out=HF[:, CTX:NT], data0=AA[:, CTX:NT], data1=GG[:, CTX:NT], initial=init,
                        op0=ALU.mult, op1=ALU.add))
                    P.seq(DVE, fns, reads=[BA, BG, BH], writes=[BH])
                else:
                    fns = []
                    if CTX:
                        fns.append(lambda e: e.tensor_tensor_scan(
                            out=TT[:, 0:CTX][:, ::-1], data0=AA[:, 0:CTX][:, ::-1], data1=GG[:, 0:CTX][:, ::-1],
                            initial=0.0, op0=ALU.mult, op1=ALU.add))
                    init = TT[:, 0:1] if CTX else 0.0
                    fns.append(lambda e, init=init: e.tensor_tensor_scan(
                        out=TT[:, CTX:NT][:, ::-1], data0=AA[:, CTX:NT][:, ::-1], data1=GG[:, CTX:NT][:, ::-1],
                        initial=init, op0=ALU.mult, op1=ALU.add))
                    P.seq(DVE, fns, reads=[BA, BG, BT], writes=[BT])
            P.op(DVE, lambda e: e.tensor_tensor(out=HF, in0=HF, in1=TT, op=ALU.add), reads=[BH, BT], writes=[BH])
            P.op(ACT, lambda e, gt=gt: e.activation(out=XR, in_=gt, func=AF.Gelu_apprx_tanh),
                 reads=[XG[i], BXR], writes=[BXR])
            P.op(DVE, lambda e, o=ob[i]: e.tensor_tensor(out=o, in0=XR, in1=HF, op=ALU.mult),
                 reads=[BXR, BH], writes=[OB[i]])
            P.dma(POOL, lambda e, o=ob[i], n=n: [e.dma_start(out=yT[16 + n], in_=o)],
                  reads=[OB[i]], writes=[YT[16 + n][u] for u in allu], owner=OB[i])

    def phase_attn(l, need_ctx):
        A.reset()
        kq = [A.bf16(3 * NT) for _ in range(2)]
        KQ = [CB("kqv%d" % i) for i in range(2)]
        ya = [A.bf16(NT) for _ in range(2)]
        YA = [CB("ya%d" % i) for i in range(2)]
        pe = [A.bf16(512) for _ in range(4)]
        PEX = [CB("pex%d" % i) for i in range(4)]
        rc = [A.f32(512) for _ in range(2)]
        oo = [A.f32(512) for _ in range(2)]
        osum, osq, orr = A.f32(512), A.f32(512), A.f32(512)
        EP = CB("ep")
        P.phase_begin(KQ + YA + PEX + [EP])
        allu = range(NU)
        neglam = lamv[:, l * 4:l * 4 + 1]
        gs = lamv[:, l * 4 + 1:l * 4 + 2]
        qtiles = []
        if CTX and need_ctx:
            qtiles.append((0, CTX, CTX // 128))
        for t0 in range(CTX, NT, 512):
            qtiles.append((t0, min(512, NT - t0), NCH))
        pe_i = [0]
        def head_body(h):
            i = h % 2
            k3 = kq[i].rearrange("p (k t) -> p k t", k=3)
            kS, qS = k3[:, 0, :], k3[:, 1, :]
            vv = k3[:, 2, :].rearrange("p (c e) -> p c e", e=128)
            P.dma(SP, lambda e, kS=kS, qS=qS, k3=k3, h=h: [
                e.dma_start(out=kS, in_=pT[C_K + h]), e.dma_start(out=qS, in_=pT[C_Q + h]),
                e.dma_start(out=k3[:, 2, :], in_=vS[h].rearrange("p c e -> p (c e)"))],
                reads=[PT[C_K + h][u] for u in allu] + [PT[C_Q + h][u] for u in allu] + [VS[h][u] for u in allu],
                writes=[KQ[i]], owner=KQ[i], ndma=3)
            def qtile_body(t0, w, nk):
                def S_op(j, c, t0=t0, w=w):
                    b = (2 * j + c) % 4
                    P.op(PE, lambda e: e.matmul(ps[b][:, :w], lhsT=kS[c * 64:(c + 1) * 64, j * 128:(j + 1) * 128],
                                                rhs=qS[c * 64:(c + 1) * 64, t0:t0 + w], start=True, stop=True),
                         reads=[KQ[i]], writes=[PSB[b]])
                    pi = pe_i[0] % 4
                    pe_i[0] += 1
                    P.op(ACT, lambda e: e.activation(out=pe[pi][:, :w], in_=ps[b][:, :w], func=AF.Exp, scale=0.125),
                         reads=[PSB[b]], writes=[PEX[pi]])
                    return pi

                def PV_op(j, c, pi, nk=nk, w=w):
                    def fn(e):
                        e.matmul(ps[4 + c][:, :w], lhsT=vv[:, j, :], rhs=pe[pi][:, :w], start=(j == 0), stop=(j == nk - 1))
                        return e.matmul(ps[6 + c][:, :w], lhsT=ones16, rhs=pe[pi][:, :w], start=(j == 0),
                                        stop=(j == nk - 1))
                    P.op(PE, fn, reads=[KQ[i], PEX[pi], CONST], writes=[PSB[4 + c], PSB[6 + c]])
                cur = [S_op(0, 0), S_op(0, 1)]
                for j in range(nk):
                    nxt = None
                    if j + 1 < nk:
                        nxt = [S_op(j + 1, 0), S_op(j + 1, 1)]
                    PV_op(j, 0, cur[0])
                    PV_op(j, 1, cur[1])
                    cur = nxt
                for c in range(2):
                    P.op(DVE, lambda e, c=c, w=w: e.reciprocal(out=rc[c][:, :w], in_=ps[6 + c][:, :w]),
                         reads=[PSB[6 + c]], writes=[EP])
                    P.op(DVE, lambda e, c=c, w=w: e.tensor_tensor(out=oo[c][:, :w], in0=ps[4 + c][:, :w],
                                                                  in1=rc[c][:, :w], op=ALU.mult),
                         reads=[PSB[4 + c], EP], writes=[EP])
                P.op(DVE, lambda e, w=w: e.scalar_tensor_tensor(out=osum[:, :w], in0=oo[1][:, :w], scalar=neglam,
                                                                in1=oo[0][:, :w], op0=ALU.mult, op1=ALU.add),
                     reads=[EP, MODV], writes=[EP])
                P.op(POOL, lambda e, w=w: e.tensor_tensor(out=osq[:, :w], in0=osum[:, :w], in1=osum[:, :w], op=ALU.mult),
                     reads=[EP], writes=[EP])
                P.op(PE, lambda e, w=w: e.matmul(ps[0][:, :w], lhsT=ones32, rhs=osq[:, :w], start=True, stop=True),
                     reads=[EP, CONST], writes=[PSB[0]])
                rstd_from(ps, 0, w, orr, EP, 128)
                P.op(DVE, lambda e, w=w, t0=t0, i=i: e.scalar_tensor_tensor(
                    out=ya[i][:, t0:t0 + w], in0=osum[:, :w], scalar=gs, in1=orr[:, :w], op0=ALU.mult, op1=ALU.mult),
                    reads=[EP, MODV], writes=[YA[i]])
            for (t0, w, nk) in qtiles:
                qtile_body(t0, w, nk)
            q0 = 0 if (need_ctx or not CTX) else CTX
            P.dma(POOL, lambda e, i=i, h=h, q0=q0: [e.dma_start(out=yT[h][:, q0:NT], in_=ya[i][:, q0:NT])],
                  reads=[YA[i]], writes=[YT[h][u] for u in units(q0, NT - q0)], owner=YA[i])

        for h in range(HEADS):
            head_body(h)

    def phase_mf(l, need_ctx, last):
        A.reset()
        W = 512
        aT = A.bf16(NJ * W)
        AT = [CB("aT%d" % c) for c in range(NJ)]
        gS = [A.bf16(3 * W) for _ in range(3)]
        GS = [CB("gS%d" % i) for i in range(3)]
        hc = [A.f32(W) for _ in range(4)]
        HC = [CB("hc%d" % i) for i in range(4)]
        mT = A.bf16(NCK * W)
        MT = [CB("mT%d" % c) for c in range(NCK)]
        X = A.f32(NCK * W)
        XB = [CB("X_%d" % c) for c in range(NCK)]
        rs = A.f32(W)
        RS = CB("mrs")
        sq = [A.f32(W) for _ in range(2)]
        SQ = [CB("msq%d" % i) for i in range(2)]
        NG = 4
        gsig = [A.f32(W) for _ in range(NG)]
        GSG = [CB("gsig%d" % i) for i in range(NG)]
        tt = [A.f32(W) for _ in range(6)]
        TTB = [CB("tt%d" % i) for i in range(6)]
        sgb = [A.f32(W) for _ in range(2)]
        SGB = [CB("sg%d" % i) for i in range(2)]
        ost = [A.f32(D)] if last else []
        OST = [CB("ost0")] if last else []
        P.phase_begin(AT + GS + HC + MT + XB + [RS] + SQ + GSG + TTB + SGB + OST)
        mT3 = mT.rearrange("p (c t) -> p c t", c=NCK)
        X3 = X.rearrange("p (c t) -> p c t", c=NCK)
        aT3 = aT.rearrange("p (c t) -> p c t", c=NJ)
        tiles = []
        if CTX and need_ctx:
            for t0 in range(0, CTX, W):
                tiles.append((t0, min(W, CTX - t0)))
        for t0 in range(CTX, NT, W):
            tiles.append((t0, min(W, NT - t0)))
        cnt = {"g": 0, "gs": 0, "tt": 0, "sq": 0, "sg": 0, "hc": 0}

        def sumsq_chain(w):
            for c in range(NCK):
                k = cnt["sq"] % 2
                cnt["sq"] += 1
                P.op(POOL, lambda e, c=c, k=k: e.tensor_tensor(out=sq[k][:, :w], in0=X3[:, c, :w], in1=X3[:, c, :w],
                                                               op=ALU.mult), reads=[XB[c]], writes=[SQ[k]])
                P.op(PE, lambda e, c=c, k=k: e.matmul(ps[7][:, :w], lhsT=ones32, rhs=sq[k][:, :w], start=(c == 0),
                                                      stop=(c == NCK - 1)), reads=[SQ[k], CONST], writes=[PSB[7]])
            rstd_from(ps, 7, w, rs, RS, D)

        def residual(kind, j, t0, w, us):
            for n in range(NCK):
                hk = cnt["hc"] % 4
                cnt["hc"] += 1
                P.dma(SP, lambda e, hk=hk, n=n: [e.dma_start(out=hc[hk][:, :w], in_=hT[n, :, t0:t0 + w])],
                      reads=[HT[u] for u in us], writes=[HC[hk]], owner=HC[hk])
                k = cnt["tt"] % 6
                cnt["tt"] += 1
                P.op(DVE, lambda e, n=n, k=k: e.scalar_tensor_tensor(
                    out=tt[k][:, :w], in0=X3[:, n, :w], scalar=mod(l, kind, j, n), in1=rs[:, :w],
                    op0=ALU.mult, op1=ALU.mult), reads=[XB[n], RS, MODV], writes=[TTB[k]])
                P.op(POOL, lambda e, n=n, k=k, hk=hk: e.tensor_tensor(out=X3[:, n, :w], in0=hc[hk][:, :w],
                                                                      in1=tt[k][:, :w], op=ALU.add),
                     reads=[HC[hk], TTB[k]], writes=[XB[n]])

        def store_h(t0, w, us):
            P.dma(POOL, lambda e: [e.dma_start(out=hT[:, :, t0:t0 + w].rearrange("c p t -> p c t"),
                                               in_=X3[:, :, 0:w])],
                  reads=XB, writes=[HT[u] for u in us], owner=XB[0])

        def mf_tile(ti, t0, w):
            j = 1 if t0 < CTX else 0
            us = list(units(t0, w))
            P.dma(SP, lambda e: [e.dma_start(out=aT3[:, 0:24, 0:w],
                                             in_=yT[:, :, t0:t0 + w].rearrange("c p t -> p c t"))],
                  reads=[YT[c][u] for c in range(24) for u in us], writes=AT[0:24], owner=AT[0])
            for q in range(8):
                s = load_w(l, OFF_BR + q * 786432, 6144)
                w4 = wslots[s][:, 0:6144].rearrange("p (r k n) -> p r k n", r=3, k=8)
                for i2 in range(2):
                    n = q * 2 + i2
                    gi = cnt["g"] % 3
                    cnt["g"] += 1
                    g3 = gS[gi].rearrange("p (r t) -> p r t", r=3)
                    P.dma(SP, lambda e, g3=g3, n=n: [e.dma_start(
                        out=g3[:, :, 0:w],
                        in_=pT[C_MG + n:C_MG + n + 33:16, :, t0:t0 + w].rearrange("r p t -> p r t"))],
                        reads=[PT[C_MG + r * 16 + n][u] for r in range(3) for u in us], writes=[GS[gi]], owner=GS[gi])
                    tks = []
                    for r in range(3):
                        b = next_bank()

                        def fn(e, r=r, b=b, i2=i2, w4=w4):
                            for kc in range(8):
                                x = e.matmul(ps[b][:, :w], lhsT=w4[:, r, kc, i2 * 128:(i2 + 1) * 128],
                                             rhs=aT3[:, r * 8 + kc, :w], start=(kc == 0), stop=(kc == 7))
                            return x
                        P.op(PE, fn, reads=[WS[s]] + AT[r * 8:(r + 1) * 8], writes=[PSB[b]])
                        k = cnt["gs"] % NG
                        cnt["gs"] += 1
                        P.op(ACT, lambda e, k=k, g3=g3, r=r, n=n: e.activation(
                            out=gsig[k][:, :w], in_=g3[:, r, :w], func=AF.Sigmoid,
                            bias=vcol(l, V_BM + r * 16 + n), scale=1.0), reads=[GS[gi], CONST], writes=[GSG[k]])
                        tk = cnt["tt"] % 6
                        cnt["tt"] += 1
                        P.op(DVE, lambda e, tk=tk, k=k, b=b: e.tensor_tensor(out=tt[tk][:, :w], in0=ps[b][:, :w],
                                                                             in1=gsig[k][:, :w], op=ALU.mult),
                             reads=[PSB[b], GSG[k]], writes=[TTB[tk]])
                        tks.append(tk)
                    P.op(POOL, lambda e, tks=tks: e.tensor_tensor(out=tt[tks[0]][:, :w], in0=tt[tks[0]][:, :w],
                                                                 in1=tt[tks[1]][:, :w], op=ALU.add),
                         reads=[TTB[tks[0]], TTB[tks[1]]], writes=[TTB[tks[0]]])
                    P.op(POOL, lambda e, tks=tks, n=n: e.tensor_tensor(out=mT3[:, n, :w], in0=tt[tks[0]][:, :w],
                                                                      in1=tt[tks[2]][:, :w], op=ALU.add),
                         reads=[TTB[tks[0]], TTB[tks[2]]], writes=[MT[n]])
            for g in range(4):
                s = load_w(l, OFF_WO + g * 1048576, 8192)
                w3 = wslots[s].rearrange("p (k n) -> p k n", k=NCK)
                for i4 in range(4):
                    n = g * 4 + i4
                    b = next_bank()

                    def fn(e, b=b, i4=i4, w3=w3):
                        for kc in range(NCK):
                            x = e.matmul(ps[b][:, :w], lhsT=w3[:, kc, i4 * 128:(i4 + 1) * 128], rhs=mT3[:, kc, :w],
                                         start=(kc == 0), stop=(kc == NCK - 1))
                        return x
                    P.op(PE, fn, reads=[WS[s]] + MT, writes=[PSB[b]])
                    P.op(ACT, lambda e, b=b, n=n: e.activation(out=X3[:, n, :w], in_=ps[b][:, :w], func=AF.Copy),
                         reads=[PSB[b]], writes=[XB[n]])
            sumsq_chain(w)
            residual(K_G2, j, t0, w, us)
            store_h(t0, w, us)
            sumsq_chain(w)
            for c in range(NCK):
                k = cnt["tt"] % 6
                cnt["tt"] += 1
                P.op(DVE, lambda e, c=c, k=k: e.tensor_tensor(out=tt[k][:, :w], in0=X3[:, c, :w], in1=rs[:, :w],
                                                              op=ALU.mult), reads=[XB[c], RS], writes=[TTB[k]])
                P.op(ACT, lambda e, c=c, k=k: e.activation(out=mT3[:, c, :w], in_=tt[k][:, :w], func=AF.Identity,
                                                           scale=mod(l, K_A3, j, c), bias=mod(l, K_B3, j, c)),
                     reads=[TTB[k], MODV], writes=[MT[c]])
            for jb in range(22):
                s = load_w(l, OFF_GU + jb * 1048576, 8192)
                w5 = wslots[s].rearrange("p (a g k n) -> p a g k n", a=2, g=2, k=NCK)
                for jj in range(2):
                    jx = jb * 2 + jj
                    bg = next_bank()
                    bu = next_bank()
                    for (bb, gu) in ((bg, 0), (bu, 1)):
                        def fn(e, bb=bb, gu=gu, jj=jj, w5=w5):
                            for kc in range(NCK):
                                x = e.matmul(ps[bb][:, :w], lhsT=w5[:, jj, gu, kc, :], rhs=mT3[:, kc, :w],
                                             start=(kc == 0), stop=(kc == NCK - 1))
                            return x
                        P.op(PE, fn, reads=[WS[s]] + MT, writes=[PSB[bb]])
                    k = cnt["sg"] % 2
                    cnt["sg"] += 1
                    P.op(ACT, lambda e, k=k, bg=bg: e.activation(out=sgb[k][:, :w], in_=ps[bg][:, :w], func=AF.Silu),
                         reads=[PSB[bg]], writes=[SGB[k]])
                    P.op(DVE, lambda e, k=k, bu=bu, jx=jx: e.tensor_tensor(out=aT3[:, jx, :w], in0=ps[bu][:, :w],
                                                                           in1=sgb[k][:, :w], op=ALU.mult),
                         reads=[PSB[bu], SGB[k]], writes=[AT[jx]])
            for n in range(NCK):
                s = load_w(l, OFF_WD + n * 720896, 5632)
                w3 = wslots[s][:, 0:5632].rearrange("p (k n) -> p k n", k=NJ)
                b = next_bank()

                def fn(e, b=b, w3=w3):
                    for kc in range(NJ):
                        x = e.matmul(ps[b][:, :w], lhsT=w3[:, kc, :], rhs=aT3[:, kc, :w], start=(kc == 0),
                                     stop=(kc == NJ - 1))
                    return x
                P.op(PE, fn, reads=[WS[s]] + AT, writes=[PSB[b]])
                P.op(ACT, lambda e, b=b, n=n: e.activation(out=X3[:, n, :w], in_=ps[b][:, :w], func=AF.Copy),
                     reads=[PSB[b]], writes=[XB[n]])
            sumsq_chain(w)
            residual(K_G5, j, t0, w, us)
            if not last:
                store_h(t0, w, us)
            else:
                for tb in range(w // 128):
                    for cg in range(4):
                        b = next_bank()

                        def tr(e, b=b, cg=cg, tb=tb):
                            for i4 in range(4):
                                c = cg * 4 + i4
                                x = e.transpose(ps[b][:, i4 * 128:(i4 + 1) * 128], X3[:, c, tb * 128:(tb + 1) * 128],
                                                ident)
                            return x
                        P.op(PE, tr, reads=[XB[cg * 4 + i4] for i4 in range(4)] + [CONST], writes=[PSB[b]])
                        if cg % 2 == 0:
                            P.op(ACT, lambda e, b=b, cg=cg: e.activation(
                                out=ost[0][:, cg * 512:(cg + 1) * 512], in_=ps[b][:, :], func=AF.Copy),
                                reads=[PSB[b]], writes=[OST[0]])
                        else:
                            P.op(DVE, lambda e, b=b, cg=cg: e.tensor_copy(
                                out=ost[0][:, cg * 512:(cg + 1) * 512], in_=ps[b][:, :]),
                                reads=[PSB[b]], writes=[OST[0]])
                    r0 = t0 - CTX + tb * 128
                    P.dma(POOL, lambda e, r0=r0: [e.dma_start(out=out[r0:r0 + 128, :], in_=ost[0])],
                          reads=[OST[0]], writes=[OUTB], owner=OST[0])

        for ti, (t0, w) in enumerate(tiles):
            mf_tile(ti, t0, w)

    prologue()
    cast_weights(0)
    phase0()
    adaln()
    for l in range(L):
        last = (l == L - 1)
        need_ctx = not last
        phase_p1(l)
        if l + 1 < L:
            cast_weights(l + 1)
        phase_conv(l)
        phase_rnn(l)
        phase_attn(l, need_ctx)
        phase_mf(l, need_ctx, last)
    snap = P.snapshot()
    fin = P.op(SP, None)
    fin.deps = snap
    P.emit(nc)
    return nc, P


def _vec16(v):
    return np.ascontiguousarray(v.reshape(-1, 128).T)


def pack_layer_weights(w_in, wa, wb, wc, wo, wg, wu, wd):
    out = np.empty(WTOT, np.float32)
    idx = np.arange(1024).reshape(8, 2, 2, 2, 16)
    sw = idx[:, :, :, ::-1, :].reshape(-1)

    def blk(cols):
        m = w_in[:, cols]
        return m.reshape(16, 128, 512).transpose(1, 0, 2).reshape(-1)
    o = OFF_IN
    blocks = []
    for base in (0, 1024):
        for g in range(2):
            c = np.arange(g * 512, (g + 1) * 512)
            blocks.append(base + c)
            blocks.append(base + sw[c])
    for vb in range(2):
        blocks.append(2048 + np.arange(vb * 512, (vb + 1) * 512))
    for pb in range(22):
        blocks.append(3072 + np.arange(pb * 512, (pb + 1) * 512))
    assert len(blocks) == 32
    for cols in blocks:
        out[o:o + 1048576] = blk(cols)
        o += 1048576
    assert o == OFF_BR
    for q in range(8):
        parts = []
        for wr in (wa, wb, wc):
            m = wr[:, q * 256:(q + 1) * 256].reshape(8, 128, 256).transpose(1, 0, 2)
            parts.append(m)
        out[o:o + 786432] = np.stack(parts, axis=1).reshape(-1)
        o += 786432
    assert o == OFF_WO
    for g in range(4):
        out[o:o + 1048576] = wo[:, g * 512:(g + 1) * 512].reshape(16, 128, 512).transpose(1, 0, 2).reshape(-1)
        o += 1048576
    assert o == OFF_GU
    for jb in range(22):
        parts = []
        for jj in range(2):
            jx = jb * 2 + jj
            gg = wg[:, jx * 128:(jx + 1) * 128].reshape(16, 128, 128).transpose(1, 0, 2)
            uu = wu[:, jx * 128:(jx + 1) * 128].reshape(16, 128, 128).transpose(1, 0, 2)
            parts.append(np.stack([gg, uu], axis=1))
        out[o:o + 1048576] = np.stack(parts, axis=1).reshape(-1)
        o += 1048576
    assert o == OFF_WD
    for n in range(16):
        out[o:o + 720896] = wd[:, n * 128:(n + 1) * 128].reshape(44, 128, 128).transpose(1, 0, 2).reshape(-1)
        o += 720896
    assert o == WTOT
    return out.reshape(WTOT // 2048, 2048)


def pack_vecs(l, inp):
    v = np.zeros((128, NV), np.float32)
    v[:, V_BADA:V_BADA + 96] = np.concatenate([_vec16(inp["b_ada"][l, k]) for k in range(6)], axis=1)
    v[:, V_GPM:V_GPM + 16] = _vec16(inp["g_pre_mix"][l])
    v[:, V_GQM:V_GQM + 16] = _vec16(inp["g_post_mix"][l])
    v[:, V_GPF:V_GPF + 16] = _vec16(inp["g_pre_ffn"][l])
    v[:, V_GQF:V_GQF + 16] = _vec16(inp["g_post_ffn"][l])
    v[:, V_CW:V_CW + 24] = np.concatenate([_vec16(inp["conv_w"][l, k]) for k in range(3)], axis=1)
    v[:, V_CB:V_CB + 8] = _vec16(inp["conv_b"][l])
    v[:, V_RW:V_RW + 32] = np.concatenate([_vec16(inp["rnn_conv_w"][l, k]) for k in range(4)], axis=1)
    v[:, V_RB:V_RB + 8] = _vec16(inp["rnn_conv_b"][l])
    v[:, V_BA:V_BA + 16] = np.concatenate([_vec16(inp["rg_ba"][l, d]) for d in range(2)], axis=1)
    v[:, V_BX:V_BX + 16] = np.concatenate([_vec16(inp["rg_bx"][l, d]) for d in range(2)], axis=1)
    v[:, V_LAM:V_LAM + 16] = np.concatenate([_vec16(inp["rg_lambda"][l, d]) for d in range(2)], axis=1)
    v[:, V_BM:V_BM + 48] = np.concatenate([_vec16(inp["b_merge"][l, r]) for r in range(3)], axis=1)
    v[:, V_SUB] = inp["diff_subln"][l]
    return v


def rope_tables(S):
    t = np.arange(S)
    row = (t // 64).astype(np.float32)
    col = (t % 64).astype(np.float32)
    inv = (np.float32(10000.0) ** (-np.arange(16, dtype=np.float32) / np.float32(16))).astype(np.float32)
    tab = np.zeros((2, 128, S), np.float32)
    for p in range(128):
        r = p % 64
        axis, half, f = r // 32, (r % 32) // 16, r % 16
        pos = row if axis == 0 else col
        ang = (pos * inv[f]).astype(np.float32)
        tab[0, p] = np.cos(ang.astype(np.float64))
        tab[1, p] = np.sin(ang.astype(np.float64)) * (-1.0 if half == 0 else 1.0)
    return tab


def prepare_shared(inp, L):
    sh = {}
    sh["wada"] = np.ascontiguousarray(
        inp["w_ada"][:L].reshape(L, 16, 128, 24, 512).transpose(0, 3, 2, 1, 4)).reshape(L, 24, 128, 16 * 512)
    sh["wpack"] = np.stack([pack_layer_weights(inp["w_in"][l], inp["w_branch_a"][l], inp["w_branch_b"][l],
                                               inp["w_branch_c"][l], inp["w_o"][l], inp["w_ffn_gate"][l],
                                               inp["w_ffn_up"][l], inp["w_ffn_down"][l]) for l in range(L)])
    rg = np.stack([inp["rg_wa"][:L], inp["rg_wx"][:L]], axis=1)
    sh["rgw"] = np.ascontiguousarray(rg.transpose(0, 4, 1, 2, 3, 5)).reshape(L, 128, 32 * 128)
    sh["vecs"] = np.ascontiguousarray(np.stack([pack_vecs(l, inp) for l in range(L)], axis=1)).reshape(128, L * NV)
    sh["dlam"] = np.ascontiguousarray(inp["diff_lambda"][:L]).reshape(-1)
    sh["ident"] = np.eye(128, dtype=np.float32)
    return sh


_CACHE = {}


def run(inp, S, CTX, L, B, trace=False):
    inp = {k: np.asarray(v) for k, v in inp.items()}
    key = (S, CTX, L)
    if key not in _CACHE:
        _CACHE[key] = build_program(S, CTX, L)
    nc, _ = _CACHE[key]
    sh = prepare_shared(inp, L)
    sh["rope"] = rope_tables(S)
    in_maps = []
    for b in range(B):
        m = dict(sh)
        m["xb"] = np.ascontiguousarray(np.concatenate([inp["ctx"][b], inp["x"][b]], axis=0))
        m["cvec"] = np.ascontiguousarray(np.concatenate([_vec16(inp["c"][b]), _vec16(inp["c_ctx"])], axis=1))
        in_maps.append(m)
    res = run_bass_kernel_spmd(nc, in_maps, core_ids=list(range(B)), trace=trace)
    outp = np.stack([np.asarray(r["out"]) for r in res.results], axis=0)
    return outp.astype(np.float32), res


def kernel(**inputs):
    outp, _ = run(inputs, 4096, 256, 4, 8)
    return outp
```

```python
import math
from contextlib import ExitStack
import numpy as np
import concourse.bass as bass
import concourse.mybir as mybir
from concourse.bass_utils import run_bass_kernel_spmd

F32, BF16 = mybir.dt.float32, mybir.dt.bfloat16
ALU = mybir.AluOpType
AF = mybir.ActivationFunctionType
AX = mybir.AxisListType
PE, ACT, DVE, POOL, SP = "tensor", "scalar", "vector", "gpsimd", "sync"

D = 2048
NCK = 16
HEADS = 8
FFH = 5632
NJ = 44
EPS = 1e-6
RG_C = 8.0
NV = 330
WTOT = 78643200
OFF_IN = 0
OFF_BR = OFF_IN + 32 * 1048576
OFF_WO = OFF_BR + 8 * 786432
OFF_GU = OFF_WO + 4 * 1048576
OFF_WD = OFF_GU + 22 * 1048576
assert OFF_WD + 16 * 720896 == WTOT
V_BADA, V_GPM, V_GQM, V_GPF, V_GQF = 0, 96, 112, 128, 144
V_CW, V_CB, V_RW, V_RB, V_BA, V_BX, V_LAM, V_BM, V_SUB = 160, 184, 192, 224, 232, 248, 264, 280, 328
C_Q, C_K, C_CX, C_CC, C_CB, C_RG, C_RX, C_MG = 0, 8, 16, 24, 32, 40, 48, 56
NPT = 104
UNIT = 256


class Buf:
    __slots__ = ("name", "w", "rs", "sem", "cnt")

    def __init__(self, name):
        self.name = name
        self.w = None
        self.rs = []
        self.sem = None
        self.cnt = 0


class Op:
    __slots__ = ("eng", "fn", "deps", "need", "owner", "ndma", "sig", "idx_")


class Prog:
    SEM_LIMIT = 30000

    def __init__(self):
        self.ops = []
        self.last = {}
        self.dma_last = {}

    def _add(self, eng, fn, reads, writes, owner, ndma):
        op = Op()
        op.eng, op.fn, op.owner, op.ndma, op.need, op.sig = eng, fn, owner, ndma, False, None
        deps = {}
        is_dma = owner is not None

        def dep(p):
            if p is None:
                return
            if (not is_dma) and p.owner is None and p.eng == eng and eng == PE:
                return
            deps[id(p)] = p

        for b in reads:
            dep(b.w)
        for b in writes:
            dep(b.w)
            for r in b.rs:
                dep(r)
        best = {}
        out = []
        for p in deps.values():
            if p.owner is None:
                q = best.get(p.eng)
                if q is None or p.idx_ > q.idx_:
                    best[p.eng] = p
            else:
                out.append(p)
        out.extend(best.values())
        for p in out:
            if p.owner is None:
                p.need = True
        op.deps = out
        op.idx_ = len(self.ops)
        for b in reads:
            if not is_dma:
                b.rs = [r for r in b.rs if not (r.owner is None and r.eng == eng)]
            b.rs.append(op)
        for b in writes:
            b.w = op
            b.rs = []
        self.ops.append(op)
        if is_dma:
            self.dma_last[id(owner)] = op
        else:
            self.last[eng] = op
        return op

    def op(self, eng, fn, reads=(), writes=()):
        return self._add(eng, fn, reads, writes, None, 0)

    def seq(self, eng, fns, reads=(), writes=()):
        r = None
        for f in fns:
            r = self._add(eng, f, reads, writes, None, 0)
        return r

    def dma(self, q, fn, reads, writes, owner, ndma=1):
        return self._add(q, fn, reads, writes, owner, ndma)

    def snapshot(self):
        snap = list(self.last.values()) + list(self.dma_last.values())
        for p in snap:
            if p.owner is None:
                p.need = True
        return snap

    def phase_begin(self, bufs):
        snap = self.snapshot()
        for b in bufs:
            b.w = None
            b.rs = list(snap)

    def emit(self, nc):
        sem_names = []
        eng_state = {}
        for op in self.ops:
            if op.owner is not None:
                o = op.owner
                if o.sem is None:
                    o.sem = len(sem_names)
                    sem_names.append("d%d" % o.sem)
                o.cnt += 16 * op.ndma
                op.sig = (o.sem, o.cnt)
            elif op.need:
                st = eng_state.get(op.eng)
                if st is None or st[1] >= self.SEM_LIMIT:
                    st = [len(sem_names), 0]
                    sem_names.append("e%d" % st[0])
                    eng_state[op.eng] = st
                st[1] += 1
                op.sig = (st[0], st[1])
        per = {}
        for op in self.ops:
            per.setdefault(op.eng, []).append(op)
        self.nsem = len(sem_names)
        with ExitStack() as es:
            sems = [es.enter_context(nc.semaphore(n)) for n in sem_names]
            block = es.enter_context(nc.Block())

            def run(engname):
                def body(e):
                    waited = {}
                    for op in per.get(engname, []):
                        for p in op.deps:
                            s, v = p.sig
                            if waited.get(s, 0) < v:
                                e.wait_ge(sems[s], v)
                                waited[s] = v
                        if op.fn is None:
                            continue
                        r = op.fn(e)
                        if op.owner is not None:
                            assert len(r) == op.ndma, (len(r), op.ndma)
                            for ins in r:
                                ins.then_inc(sems[op.sig[0]], 16)
                        elif op.sig is not None:
                            r.then_inc(sems[op.sig[0]], 1)
                return body

            block.tensor(run(PE))
            block.scalar(run(ACT))
            block.vector(run(DVE))
            block.gpsimd(run(POOL))
            block.sync(run(SP))


class Arena:
    def __init__(self, ap, base, limit):
        self.ap, self.base, self.cur, self.limit = ap, base, base, limit

    def reset(self):
        self.cur = self.base

    def f32(self, n):
        a = self.ap[:, self.cur:self.cur + n]
        self.cur += n
        assert self.cur <= self.limit, ("arena overflow", self.cur, self.limit)
        return a

    def bf16(self, n):
        m = (n + 1) // 2
        a = self.ap[:, self.cur:self.cur + m].bitcast(BF16)
        self.cur += m
        assert self.cur <= self.limit, ("arena overflow", self.cur, self.limit)
        return a[:, 0:n]


def build_program(S, CTX, L, LAYER0=0, DEBUG=False):
    NT = CTX + S
    NCH = NT // 128
    NU = NT // UNIT
    nc = bass.Bass("TRN2", target_bir_lowering=False)
    P = Prog()

    def dram_in(name, shape, dt=F32):
        return nc.dram_tensor(name, list(shape), dt, kind="ExternalInput").ap()

    xb = dram_in("xb", [NT, D])
    cvec = dram_in("cvec", [128, 2 * NCK])
    wada = dram_in("wada", [L, 48, 128, NCK * 256])
    wpack = dram_in("wpack", [L, WTOT // 2048, 2048])
    rgw = dram_in("rgw", [L, 128, 32 * 128])
    vecs_d = dram_in("vecs", [128, L * NV])
    dlam = dram_in("dlam", [L * 256])
    rope = dram_in("rope", [2, 128, S])
    ident_d = dram_in("ident", [128, 128])
    out = nc.dram_tensor("out", [S, D], F32, kind="ExternalOutput").ap()

    IK = "ExternalOutput" if DEBUG else "Internal"
    hT = nc.dram_tensor("hT", [NCK, 128, NT], F32, kind=IK).ap()
    wbf = [nc.dram_tensor("wbf%d" % l, [WTOT // 2048, 2048], BF16, kind="Internal").ap() for l in range(L)]
    pT = nc.dram_tensor("pT", [NPT, 128, NT], BF16, kind=IK).ap()
    vS = nc.dram_tensor("vS", [HEADS, 128, NCH, 128], BF16, kind=IK).ap()
    yT = nc.dram_tensor("yT", [24, 128, NT], BF16, kind=IK).ap()
    wbf_flat = [wbf[l].rearrange("r c -> (r c)") for l in range(L)]

    HT = [Buf("HT%d" % u) for u in range(NU)]
    PT = [[Buf("PT") for u in range(NU)] for c in range(NPT)]
    VS = [[Buf("VS") for u in range(NU)] for h in range(HEADS)]
    YT = [[Buf("YT") for u in range(NU)] for c in range(24)]
    WBF = [Buf("WBF%d" % l) for l in range(L)]
    OUTB = Buf("OUT")

    def units(t0, w):
        return range(t0 // UNIT, (t0 + w + UNIT - 1) // UNIT)

    SB_F32 = 53200
    arena_t = nc.alloc_sbuf_tensor("arena", [128, SB_F32], F32)
    ar = arena_t.ap()
    ps_t = nc.alloc_psum_tensor("ps", [128, 8, 512], F32)
    psa = ps_t.ap()
    PSB = [Buf("PS%d" % i) for i in range(8)]
    BC = {}

    def CB(name):
        if name not in BC:
            BC[name] = Buf(name)
        return BC[name]
    ps = [psa[:, i, :] for i in range(8)]

    perm = Arena(ar, 0, SB_F32)
    WSLOT_E = 8192
    NSLOT = 3
    wslots = [perm.bf16(WSLOT_E) for _ in range(NSLOT)]
    WS = [Buf("W%d" % i) for i in range(NSLOT)]
    ident = perm.f32(128)
    ones32 = perm.f32(128)
    ones16 = perm.bf16(128)
    vecs = perm.f32(L * NV)
    modv = perm.f32(L * 6 * 2 * NCK)
    lamv = perm.f32(L * 4)
    cpv = perm.f32(L * 16)
    cp2v = perm.f32(L * 16)
    a_cv = perm.f32(2 * NCK)
    a_xr = perm.f32(2 * NCK)
    a_mt = perm.f32(2 * 96)
    a_tmp = perm.f32(64)
    a_s12 = perm.f32(4)
    a_tv = perm.f32(32)
    SM = Buf("adasmall")
    CONST = Buf("CONST")
    MODV = Buf("MODV")
    A = Arena(ar, perm.cur, SB_F32)
    slot_i = [0]

    def next_slot():
        s = slot_i[0] % NSLOT
        slot_i[0] += 1
        return s

    bank_i = [0]

    def next_bank(n=7):
        b = bank_i[0] % n
        bank_i[0] += 1
        return b

    def vcol(l, off, n=1):
        return vecs[:, l * NV + off: l * NV + off + n]

    def mod(l, kind, j, c):
        o = ((l * 6 + kind) * 2 + j) * NCK + c
        return modv[:, o:o + 1]

    K_A1, K_B1, K_G2, K_A3, K_B3, K_G5 = range(6)

    def load_w(l, off, nelem_pp, reads_extra=()):
        s = next_slot()
        src = wbf_flat[l][off: off + 128 * nelem_pp].rearrange("(p x) -> p x", p=128)
        dst = wslots[s][:, 0:nelem_pp]
        P.dma(SP, lambda e: [e.dma_start(out=dst, in_=src)], reads=[WBF[l]], writes=[WS[s]], owner=WS[s])
        return s

    def prologue():
        P.dma(SP, lambda e: [e.dma_start(out=ident, in_=ident_d),
                             e.dma_start(out=vecs, in_=vecs_d)],
              reads=[], writes=[CONST], owner=CONST, ndma=2)
        P.op(POOL, lambda e: e.memset(ones32, 1.0), writes=[CONST])
        P.op(POOL, lambda e: e.memset(ones16, 1.0), writes=[CONST])

    def cast_weights(l):
        R = WTOT // 2048
        step = 1280
        n = R // step
        assert n * step == R

        def fn(e):
            return [e.dma_start(out=wbf[l][i * step:(i + 1) * step, :], in_=wpack[l, i * step:(i + 1) * step, :])
                    for i in range(n)]
        P.dma(POOL, fn, reads=[], writes=[WBF[l]], owner=WBF[l], ndma=n)

    def phase0():
        A.reset()
        xin = [A.f32(D) for _ in range(2)]
        XIN = [CB("xin%d" % i) for i in range(2)]
        stg = [A.f32(NCK * 512) for _ in range(2)]
        STG = [CB("stg%d" % i) for i in range(2)]
        P.phase_begin(XIN + STG)
        k = 0
        ti = 0
        for t0 in range(0, NT, 512):
            w = min(512, NT - t0)
            sg = stg[ti % 2].rearrange("p (c t) -> p c t", c=NCK)
            SG = STG[ti % 2]
            for tb in range(w // 128):
                xi, XI = xin[k % 2], XIN[k % 2]
                k += 1
                r0 = t0 + tb * 128
                P.dma(SP, lambda e, xi=xi, r0=r0: [e.dma_start(out=xi, in_=xb[r0:r0 + 128, :])],
                      reads=[], writes=[XI], owner=XI)
                for cg in range(4):
                    b = next_bank()

                    def tr(e, xi=xi, b=b, cg=cg):
                        for i in range(4):
                            c = cg * 4 + i
                            r = e.transpose(ps[b][:, i * 128:(i + 1) * 128], xi[:, c * 128:(c + 1) * 128], ident)
                        return r
                    P.op(PE, tr, reads=[XI, CONST], writes=[PSB[b]])
                    dst = sg[:, cg * 4:(cg + 1) * 4, tb * 128:(tb + 1) * 128]
                    src = ps[b].rearrange("p (c t) -> p c t", c=4)
                    eng = ACT if (cg % 2 == 0) else DVE
                    if eng == ACT:
                        P.op(ACT, lambda e, dst=dst, src=src: e.activation(out=dst, in_=src, func=AF.Copy),
                             reads=[PSB[b]], writes=[SG])
                    else:
                        P.op(DVE, lambda e, dst=dst, src=src: e.tensor_copy(out=dst, in_=src),
                             reads=[PSB[b]], writes=[SG])
            P.dma(POOL, lambda e, sg=sg, t0=t0, w=w: [e.dma_start(
                out=hT[:, :, t0:t0 + w].rearrange("c p t -> p c t"), in_=sg[:, :, 0:w])],
                reads=[SG], writes=[HT[u] for u in units(t0, w)], owner=SG)
            ti += 1


    def ada_setup():
        A.reset()
        dl = A.f32(L * 256)
        DLB = CB("dlb")
        P.phase_begin([DLB])
        cv, xr, tmp, s12, tv = a_cv, a_xr, a_tmp, a_s12, a_tv
        P.dma(SP, lambda e: [e.dma_start(out=cv, in_=cvec)], reads=[], writes=[SM], owner=SM)
        P.dma(SP, lambda e: [e.dma_start(out=dl, in_=dlam.partition_broadcast(128))], reads=[], writes=[DLB], owner=DLB)
        P.op(ACT, lambda e: e.activation(out=cv, in_=cv, func=AF.Silu), reads=[SM], writes=[SM])
        P.op(DVE, lambda e: e.tensor_copy(out=xr.rearrange("p (c j) -> p c j", j=2),
                                          in_=cv.rearrange("p (j c) -> p c j", j=2)), reads=[SM], writes=[SM])
        for l in range(L):
            lam_init = 0.8 - 0.6 * math.exp(-0.3 * (l + LAYER0))
            d4 = dl[:, l * 256:(l + 1) * 256]
            P.seq(DVE, [
                lambda e, d4=d4: e.tensor_tensor(out=tmp, in0=d4[:, 0:64], in1=d4[:, 64:128], op=ALU.mult),
                lambda e: e.tensor_reduce(out=s12[:, 0:1], in_=tmp, axis=AX.X, op=ALU.add),
                lambda e, d4=d4: e.tensor_tensor(out=tmp, in0=d4[:, 128:192], in1=d4[:, 192:256], op=ALU.mult),
                lambda e: e.tensor_reduce(out=s12[:, 1:2], in_=tmp, axis=AX.X, op=ALU.add)],
                reads=[SM, DLB], writes=[SM])
            P.op(ACT, lambda e: e.activation(out=s12[:, 2:4], in_=s12[:, 0:2], func=AF.Exp), reads=[SM], writes=[SM])
            P.seq(DVE, [
                lambda e: e.tensor_tensor(out=s12[:, 0:1], in0=s12[:, 3:4], in1=s12[:, 2:3], op=ALU.subtract),
                lambda e, l=l, lam_init=lam_init: e.tensor_scalar(out=lamv[:, l * 4:l * 4 + 1], in0=s12[:, 0:1],
                                                                  scalar1=-lam_init, scalar2=None, op0=ALU.add),
                lambda e, l=l, lam_init=lam_init: e.tensor_scalar(out=lamv[:, l * 4 + 1:l * 4 + 2], in0=vcol(l, V_SUB, 1),
                                                                  scalar1=(1.0 - lam_init), scalar2=None, op0=ALU.mult)],
                reads=[SM, CONST], writes=[SM, MODV])
            P.op(ACT, lambda e, l=l: e.activation(out=tv[:, 0:16], in_=vcol(l, V_LAM, 16), func=AF.Exp, scale=-1.0),
                 reads=[CONST, SM], writes=[SM])
            P.op(ACT, lambda e: e.activation(out=tv[:, 16:32], in_=tv[:, 0:16], func=AF.Ln, bias=1.0, scale=1.0),
                 reads=[SM], writes=[SM])
            P.op(DVE, lambda e, l=l: e.tensor_scalar(out=cpv[:, l * 16:(l + 1) * 16], in0=tv[:, 16:32],
                                                     scalar1=-RG_C, scalar2=None, op0=ALU.mult),
                 reads=[SM], writes=[MODV])
            P.op(DVE, lambda e, l=l: e.tensor_scalar(out=cp2v[:, l * 16:(l + 1) * 16], in0=tv[:, 16:32],
                                                     scalar1=-2.0 * RG_C, scalar2=None, op0=ALU.mult),
                 reads=[SM], writes=[MODV])

    def ada_layer_gen(l):
        xr, mt = a_xr, a_mt
        for nb in range(48):
            s = next_slot()
            wv = ar[:, s * 4096:(s + 1) * 4096]
            P.dma(SP, lambda e, wv=wv, nb=nb: [e.dma_start(out=wv, in_=wada[l, nb])],
                  reads=[], writes=[WS[s]], owner=WS[s])

            def mm(e, wv=wv, nb=nb):
                w3 = wv.rearrange("p (k n) -> p k n", k=NCK)
                for n2 in range(2):
                    n = nb * 2 + n2
                    for kc in range(NCK):
                        r = e.matmul(ps[7][:, n * 2:n * 2 + 2], lhsT=w3[:, kc, n2 * 128:(n2 + 1) * 128],
                                     rhs=xr[:, kc * 2:kc * 2 + 2], start=(kc == 0), stop=(kc == NCK - 1))
                return r
            P.op(PE, mm, reads=[WS[s], SM], writes=[PSB[7]])
            yield
        pv = ps[7][:, 0:192].rearrange("p (n j) -> p n j", j=2)
        for j in range(2):
            P.op(DVE, lambda e, j=j: e.tensor_tensor(out=mt[:, j * 96:(j + 1) * 96], in0=pv[:, :, j],
                                                     in1=vcol(l, V_BADA, 96), op=ALU.add),
                 reads=[PSB[7], CONST, SM], writes=[SM])

            def derive(e, j=j):
                M = lambda k: mt[:, j * 96 + k * 16: j * 96 + (k + 1) * 16]
                dst = lambda kind: modv[:, ((l * 6 + kind) * 2 + j) * NCK:((l * 6 + kind) * 2 + j + 1) * NCK]
                e.scalar_tensor_tensor(out=dst(K_A1), in0=M(1), scalar=1.0, in1=vcol(l, V_GPM, 16),
                                       op0=ALU.add, op1=ALU.mult)
                e.tensor_copy(out=dst(K_B1), in_=M(0))
                e.tensor_tensor(out=dst(K_G2), in0=M(2), in1=vcol(l, V_GQM, 16), op=ALU.mult)
                e.scalar_tensor_tensor(out=dst(K_A3), in0=M(4), scalar=1.0, in1=vcol(l, V_GPF, 16),
                                       op0=ALU.add, op1=ALU.mult)
                e.tensor_copy(out=dst(K_B3), in_=M(3))
                return e.tensor_tensor(out=dst(K_G5), in0=M(5), in1=vcol(l, V_GQF, 16), op=ALU.mult)
            P.op(DVE, derive, reads=[SM, CONST], writes=[MODV, SM])
        yield

    ada_gens = {}

    def ada_step(l, n):
        g = ada_gens.get(l)
        if g is None:
            return
        for _ in range(n):
            try:
                next(g)
            except StopIteration:
                ada_gens[l] = None
                return

    def ada_finish(l):
        ada_step(l, 1000)

    def rstd_from(psb, b, w, dstbuf, DST, n_feat):
        P.op(ACT, lambda e: e.activation(out=dstbuf[:, :w], in_=ps[b][:, :w], func=AF.Sqrt, scale=1.0 / n_feat,
                                         bias=EPS), reads=[PSB[b]], writes=[DST])
        P.op(DVE, lambda e: e.reciprocal(out=dstbuf[:, :w], in_=dstbuf[:, :w]), reads=[DST], writes=[DST])

    def p1_tiles():
        t = [(0, CTX)] if CTX else []
        for t0 in range(CTX, NT, 512):
            t.append((t0, min(512, NT - t0)))
        return t

    def phase_p1(l):
        A.reset()
        hin = A.f32(NCK * 512)
        hin3 = hin.rearrange("p (c t) -> p c t", c=NCK)
        HIN = [CB("hin%d" % c) for c in range(NCK)]
        uT = A.bf16(NCK * 512)
        uT3 = uT.rearrange("p (c t) -> p c t", c=NCK)
        UT = [CB("uT%d" % c) for c in range(NCK)]
        sq = [A.f32(512) for _ in range(2)]
        SQ = [CB("sq%d" % i) for i in range(2)]
        rstd = A.f32(512)
        RS = CB("rstd")
        cs = [A.f32(1024) for _ in range(2)]
        CS = [CB("cs%d" % i) for i in range(2)]
        stg = [A.bf16(4 * 512) for _ in range(3)]
        STG = [CB("pstg%d" % i) for i in range(3)]
        rt = [A.f32(512) for _ in range(4)]
        RT = [CB("rt%d" % i) for i in range(4)]
        vst = [A.bf16(4 * 512) for _ in range(2)]
        VST = [CB("vst%d" % i) for i in range(2)]
        P.phase_begin(HIN + UT + SQ + [RS] + CS + STG + RT + VST)
        stg_i = [0]
        rt_i = [0]
        vst_i = [0]
        ev_i = [0]
        def p1_tile(ti, t0, w):
            is_ctx = t0 < CTX
            j = 1 if is_ctx else 0
            csb = CSB = None
            P.dma(SP, lambda e, t0=t0, w=w: [e.dma_start(out=hin3[:, :, 0:w],
                                                         in_=hT[:, :, t0:t0 + w].rearrange("c p t -> p c t"))],
                  reads=[HT[u] for u in units(t0, w)], writes=HIN, owner=HIN[0])
            if not is_ctx:
                csb, CSB = cs[ti % 2], CS[ti % 2]
                p0 = t0 - CTX
                P.dma(SP, lambda e, csb=csb, p0=p0, w=w: [
                    e.dma_start(out=csb[:, 0:w], in_=rope[0, :, p0:p0 + w]),
                    e.dma_start(out=csb[:, 512:512 + w], in_=rope[1, :, p0:p0 + w])],
                    reads=[], writes=[CSB], owner=CSB, ndma=2)
            for c in range(NCK):
                sb, SBF = sq[c % 2], SQ[c % 2]
                P.op(ACT, lambda e, sb=sb, c=c, w=w: e.activation(out=sb[:, :w], in_=hin3[:, c, :w], func=AF.Square),
                     reads=[HIN[c]], writes=[SBF])
                P.op(PE, lambda e, sb=sb, c=c, w=w: e.matmul(ps[7][:, :w], lhsT=ones32, rhs=sb[:, :w],
                                                             start=(c == 0), stop=(c == NCK - 1)),
                     reads=[SBF, CONST], writes=[PSB[7]])
            rstd_from(ps, 7, w, rstd, RS, D)
            for c in range(NCK):
                P.op(DVE, lambda e, c=c, w=w: e.tensor_tensor(out=hin3[:, c, :w], in0=hin3[:, c, :w], in1=rstd[:, :w],
                                                              op=ALU.mult), reads=[HIN[c], RS], writes=[HIN[c]])
                P.op(ACT, lambda e, c=c, w=w, j=j: e.activation(out=uT3[:, c, :w], in_=hin3[:, c, :w], func=AF.Identity,
                                                                scale=mod(l, K_A1, j, c), bias=mod(l, K_B1, j, c)),
                     reads=[HIN[c], MODV], writes=[UT[c]])

            def mm_group(s, i, b, w=w):
                def fn(e):
                    w3 = wslots[s].rearrange("p (k n) -> p k n", k=NCK)
                    for kc in range(NCK):
                        r = e.matmul(ps[b][:, :w], lhsT=w3[:, kc, i * 128:(i + 1) * 128], rhs=uT3[:, kc, :w],
                                     start=(kc == 0), stop=(kc == NCK - 1))
                    return r
                P.op(PE, fn, reads=[WS[s]] + UT, writes=[PSB[b]])

            def store_stage(si, ch0, w=w, t0=t0):
                sg = stg[si].rearrange("p (c t) -> p c t", c=4)
                P.dma(POOL, lambda e: [e.dma_start(out=pT[ch0:ch0 + 4, :, t0:t0 + w].rearrange("c p t -> p c t"),
                                                   in_=sg[:, :, 0:w])],
                      reads=[STG[si]], writes=[PT[ch0 + i][u] for i in range(4) for u in units(t0, w)], owner=STG[si])

            def evac(dst, b, DSTB, w=w):
                ev_i[0] += 1
                if ev_i[0] % 2 == 0:
                    P.op(ACT, lambda e: e.activation(out=dst, in_=ps[b][:, :w], func=AF.Copy),
                         reads=[PSB[b]], writes=[DSTB])
                else:
                    P.op(DVE, lambda e: e.tensor_copy(out=dst, in_=ps[b][:, :w]), reads=[PSB[b]], writes=[DSTB])

            for kind in range(2):
                for g in range(2):
                    blk = kind * 4 + g * 2
                    s_pl = load_w(l, OFF_IN + blk * 1048576, 8192)
                    s_sw = None
                    if not is_ctx:
                        s_sw = load_w(l, OFF_IN + (blk + 1) * 1048576, 8192)
                    si = stg_i[0] % 3
                    stg_i[0] += 1
                    sg = stg[si].rearrange("p (c t) -> p c t", c=4)
                    for i in range(4):
                        b1 = next_bank()
                        mm_group(s_pl, i, b1)
                        if is_ctx:
                            evac(sg[:, i, :w], b1, STG[si])
                        else:
                            b2 = next_bank()
                            mm_group(s_sw, i, b2)
                            r1 = rt_i[0] % 4
                            r2 = (rt_i[0] + 1) % 4
                            rt_i[0] += 2
                            P.op(DVE, lambda e, r1=r1, b1=b1, csb=csb: e.tensor_tensor(
                                out=rt[r1][:, :w], in0=ps[b1][:, :w], in1=csb[:, 0:w], op=ALU.mult),
                                reads=[PSB[b1], CSB], writes=[RT[r1]])
                            P.op(DVE, lambda e, r2=r2, b2=b2, csb=csb: e.tensor_tensor(
                                out=rt[r2][:, :w], in0=ps[b2][:, :w], in1=csb[:, 512:512 + w], op=ALU.mult),
                                reads=[PSB[b2], CSB], writes=[RT[r2]])
                            P.op(POOL, lambda e, r1=r1, r2=r2, sg=sg, i=i: e.tensor_tensor(
                                out=sg[:, i, :w], in0=rt[r1][:, :w], in1=rt[r2][:, :w], op=ALU.add),
                                reads=[RT[r1], RT[r2]], writes=[STG[si]])
                    store_stage(si, (C_Q if kind == 0 else C_K) + g * 4)
            for vb in range(2):
                s = load_w(l, OFF_IN + (8 + vb) * 1048576, 8192)
                vi = vst_i[0] % 2
                vst_i[0] += 1
                vt = vst[vi].rearrange("p (b n) -> p b n", b=4)
                ntb = w // 128
                for tb in range(ntb):
                    b = next_bank()

                    def fn(e, s=s, tb=tb, b=b):
                        w3 = wslots[s].rearrange("p (k n) -> p k n", k=NCK)
                        for kc in range(NCK):
                            r = e.matmul(ps[b][:, :], lhsT=uT3[:, kc, tb * 128:(tb + 1) * 128], rhs=w3[:, kc, :],
                                         start=(kc == 0), stop=(kc == NCK - 1))
                        return r
                    P.op(PE, fn, reads=[WS[s]] + UT, writes=[PSB[b]])
                    ev_i[0] += 1
                    if ev_i[0] % 2 == 0:
                        P.op(ACT, lambda e, vt=vt, tb=tb, b=b: e.activation(out=vt[:, tb, :], in_=ps[b][:, :], func=AF.Copy),
                             reads=[PSB[b]], writes=[VST[vi]])
                    else:
                        P.op(DVE, lambda e, vt=vt, tb=tb, b=b: e.tensor_copy(out=vt[:, tb, :], in_=ps[b][:, :]),
                             reads=[PSB[b]], writes=[VST[vi]])
                ch0 = t0 // 128
                P.dma(POOL, lambda e, vt=vt, vb=vb, ch0=ch0, ntb=ntb: [e.dma_start(
                    out=vS[vb * 4 + hh, :, ch0:ch0 + ntb, :],
                    in_=vt[:, 0:ntb, hh * 128:(hh + 1) * 128]) for hh in range(4)],
                    reads=[VST[vi]], writes=[VS[vb * 4 + h][u] for h in range(4) for u in units(t0, w)],
                    owner=VST[vi], ndma=4)
            for pb in range(22):
                s = load_w(l, OFF_IN + (10 + pb) * 1048576, 8192)
                si = stg_i[0] % 3
                stg_i[0] += 1
                sg = stg[si].rearrange("p (c t) -> p c t", c=4)
                for i in range(4):
                    b = next_bank()
                    mm_group(s, i, b)
                    evac(sg[:, i, :w], b, STG[si])
                store_stage(si, C_CX + pb * 4)

        for ti, (t0, w) in enumerate(p1_tiles()):
            p1_tile(ti, t0, w)

    def segs():
        s = []
        if CTX:
            s.append((0, CTX))
        s.append((CTX, NT))
        return s

    def phase_conv(l):
        A.reset()
        xin = [A.bf16(3 * NT) for _ in range(2)]
        XIN = [CB("cin%d" % i) for i in range(2)]
        Z = [A.f32(NT) for _ in range(2)]
        Y = [A.f32(NT) for _ in range(2)]
        ZB = [CB("Z%d" % i) for i in range(2)]
        YB = [CB("Y%d" % i) for i in range(2)]
        ob = [A.bf16(NT) for _ in range(2)]
        OB = [CB("cob%d" % i) for i in range(2)]
        P.phase_begin(XIN + ZB + YB + OB)
        allu = range(NU)
        for c in range(8):
            i = c % 2
            x3 = xin[i].rearrange("p (k t) -> p k t", k=3)
            P.dma(SP, lambda e, x3=x3, c=c: [e.dma_start(out=x3[:, 0, :], in_=pT[C_CX + c]),
                                             e.dma_start(out=x3[:, 1, :], in_=pT[C_CC + c]),
                                             e.dma_start(out=x3[:, 2, :], in_=pT[C_CB + c])],
                  reads=[PT[C_CX + c][u] for u in allu] + [PT[C_CC + c][u] for u in allu] + [PT[C_CB + c][u] for u in allu],
                  writes=[XIN[i]], owner=XIN[i], ndma=3)
            z, y = Z[i], Y[i]
            P.op(DVE, lambda e, z=z, x3=x3: e.tensor_tensor(out=z, in0=x3[:, 1, :], in1=x3[:, 0, :], op=ALU.mult),
                 reads=[XIN[i]], writes=[ZB[i]])
            P.op(ACT, lambda e, z=z, y=y, c=c: e.activation(out=y, in_=z, func=AF.Identity,
                                                            scale=vcol(l, V_CW + 8 + c), bias=vcol(l, V_CB + c)),
                 reads=[ZB[i], CONST], writes=[YB[i]])

            fns = []
            for (a, b) in segs():
                fns.append(lambda e, z=z, y=y, c=c, a=a, b=b: e.scalar_tensor_tensor(
                    out=y[:, a + 1:b], in0=z[:, a:b - 1], scalar=vcol(l, V_CW + c), in1=y[:, a + 1:b],
                    op0=ALU.mult, op1=ALU.add))
                fns.append(lambda e, z=z, y=y, c=c, a=a, b=b: e.scalar_tensor_tensor(
                    out=y[:, a:b - 1], in0=z[:, a + 1:b], scalar=vcol(l, V_CW + 16 + c), in1=y[:, a:b - 1],
                    op0=ALU.mult, op1=ALU.add))
            P.seq(DVE, fns, reads=[ZB[i], YB[i], CONST], writes=[YB[i]])
            P.op(DVE, lambda e, y=y, x3=x3, o=ob[i]: e.tensor_tensor(out=o, in0=x3[:, 2, :], in1=y, op=ALU.mult),
                 reads=[YB[i], XIN[i]], writes=[OB[i]])
            P.dma(POOL, lambda e, o=ob[i], c=c: [e.dma_start(out=yT[8 + c], in_=o)],
                  reads=[OB[i]], writes=[YT[8 + c][u] for u in allu], owner=OB[i])

    def phase_rnn(l):
        A.reset()
        xg = [A.bf16(2 * NT) for _ in range(2)]
        XG = [CB("xg%d" % i) for i in range(2)]
        rg = A.f32(32 * 128)
        RGW = CB("rgw")
        XR, AA, GG, TT, HF = [A.f32(NT) for _ in range(5)]
        BXR, BA, BG, BT, BH = [CB(n) for n in ("XR", "AA", "GG", "TT", "HF")]
        ob0 = A.bf16(NT)
        ob = [ob0, ob0]
        OB0 = CB("rob")
        OB = [OB0, OB0]
        P.phase_begin(XG + [RGW, BXR, BA, BG, BT, BH, OB0])
        rg3 = rg.rearrange("p (i k) -> p i k", i=32)
        P.dma(SP, lambda e: [e.dma_start(out=rg, in_=rgw[l])], reads=[], writes=[RGW], owner=RGW)
        allu = range(NU)
        tiles = p1_tiles()
        for n in range(8):
            i = n % 2
            x3 = xg[i].rearrange("p (k t) -> p k t", k=2)
            P.dma(SP, lambda e, x3=x3, n=n: [e.dma_start(out=x3[:, 0, :], in_=pT[C_RG + n]),
                                             e.dma_start(out=x3[:, 1, :], in_=pT[C_RX + n])],
                  reads=[PT[C_RG + n][u] for u in allu] + [PT[C_RX + n][u] for u in allu],
                  writes=[XG[i]], owner=XG[i], ndma=2)
            xx = x3[:, 1, :]
            gt = x3[:, 0, :]
            P.op(ACT, lambda e, xx=xx, n=n: e.activation(out=XR, in_=xx, func=AF.Identity,
                                                         scale=vcol(l, V_RW + 16 + n), bias=vcol(l, V_RB + n)),
                 reads=[XG[i], CONST], writes=[BXR])

            fns = []
            for (a, b) in segs():
                fns.append(lambda e, xx=xx, n=n, a=a, b=b: e.scalar_tensor_tensor(
                    out=XR[:, a + 2:b], in0=xx[:, a:b - 2], scalar=vcol(l, V_RW + n), in1=XR[:, a + 2:b],
                    op0=ALU.mult, op1=ALU.add))
                fns.append(lambda e, xx=xx, n=n, a=a, b=b: e.scalar_tensor_tensor(
                    out=XR[:, a + 1:b], in0=xx[:, a:b - 1], scalar=vcol(l, V_RW + 8 + n), in1=XR[:, a + 1:b],
                    op0=ALU.mult, op1=ALU.add))
                fns.append(lambda e, xx=xx, n=n, a=a, b=b: e.scalar_tensor_tensor(
                    out=XR[:, a:b - 1], in0=xx[:, a + 1:b], scalar=vcol(l, V_RW + 24 + n), in1=XR[:, a:b - 1],
                    op0=ALU.mult, op1=ALU.add))
            P.seq(DVE, fns, reads=[XG[i], BXR, CONST], writes=[BXR])
            for d in range(2):
                for (t0, w) in tiles:
                    for gi, (dst, DSTB, voff) in enumerate(((AA, BA, V_BA), (GG, BG, V_BX))):
                        b = next_bank()
                        widx = gi * 16 + d * 8 + n
                        P.op(PE, lambda e, b=b, widx=widx, t0=t0, w=w: e.matmul(
                            ps[b][:, :w], lhsT=rg3[:, widx, :], rhs=XR[:, t0:t0 + w], start=True, stop=True),
                            reads=[RGW, BXR], writes=[PSB[b]])
                        P.op(ACT, lambda e, b=b, dst=dst, t0=t0, w=w, voff=voff, d=d, n=n: e.activation(
                            out=dst[:, t0:t0 + w], in_=ps[b][:, :w], func=AF.Sigmoid, bias=vcol(l, voff + d * 8 + n),
                            scale=1.0), reads=[PSB[b], CONST], writes=[DSTB])
                cp = cpv[:, l * 16 + d * 8 + n: l * 16 + d * 8 + n + 1]
                cp2 = cp2v[:, l * 16 + d * 8 + n: l * 16 + d * 8 + n + 1]
                P.op(ACT, lambda e, cp2=cp2: e.activation(out=TT, in_=AA, func=AF.Exp, scale=cp2),
                     reads=[BA, MODV], writes=[BT])
                P.op(ACT, lambda e, cp=cp: e.activation(out=AA, in_=AA, func=AF.Exp, scale=cp),
                     reads=[BA, MODV], writes=[BA])
                P.op(DVE, lambda e: e.tensor_scalar(out=TT, in0=TT, scalar1=-1.0, scalar2=1.0, op0=ALU.mult, op1=ALU.add),
                     reads=[BT], writes=[BT])
                P.op(ACT, lambda e: e.activation(out=TT, in_=TT, func=AF.Sqrt, bias=1e-30, scale=1.0),
                     reads=[BT], writes=[BT])
                P.op(DVE, lambda e: e.tensor_tensor(out=GG, in0=GG, in1=XR, op=ALU.mult), reads=[BG, BXR], writes=[BG])
                P.op(DVE, lambda e: e.tensor_tensor(out=GG, in0=GG, in1=TT, op=ALU.mult), reads=[BG, BT], writes=[BG])
                if d == 0:
                    fns = []
                    if CTX:
                        fns.append(lambda e: e.tensor_tensor_scan(out=HF[:, 0:CTX], data0=AA[:, 0:CTX], data1=GG[:, 0:CTX],
                                                                  initial=0.0, op0=ALU.mult, op1=ALU.add))
                    init = HF[:, CTX - 1:CTX] if CTX else 0.0
                    fns.append(lambda e, init=init: e.tensor_tensor_scan(
                        out=HF[:, CTX:NT], data0=AA[:, CTX:NT], data1=GG[:, CTX:NT], initial=init,
                        op0=ALU.mult, op1=ALU.add))
                    P.seq(DVE, fns, reads=[BA, BG, BH], writes=[BH])
                else:
                    fns = []
                    if CTX:
                        fns.append(lambda e: e.tensor_tensor_scan(
                            out=TT[:, 0:CTX][:, ::-1], data0=AA[:, 0:CTX][:, ::-1], data1=GG[:, 0:CTX][:, ::-1],
                            initial=0.0, op0=ALU.mult, op1=ALU.add))
                    init = TT[:, 0:1] if CTX else 0.0
                    fns.append(lambda e, init=init: e.tensor_tensor_scan(
                        out=TT[:, CTX:NT][:, ::-1], data0=AA[:, CTX:NT][:, ::-1], data1=GG[:, CTX:NT][:, ::-1],
                        initial=init, op0=ALU.mult, op1=ALU.add))
                    P.seq(DVE, fns, reads=[BA, BG, BT], writes=[BT])
            P.op(DVE, lambda e: e.tensor_tensor(out=HF, in0=HF, in1=TT, op=ALU.add), reads=[BH, BT], writes=[BH])
            P.op(ACT, lambda e, gt=gt: e.activation(out=XR, in_=gt, func=AF.Gelu_apprx_tanh),
                 reads=[XG[i], BXR], writes=[BXR])
            P.op(DVE, lambda e, o=ob[i]: e.tensor_tensor(out=o, in0=XR, in1=HF, op=ALU.mult),
                 reads=[BXR, BH], writes=[OB[i]])
            P.dma(POOL, lambda e, o=ob[i], n=n: [e.dma_start(out=yT[16 + n], in_=o)],
                  reads=[OB[i]], writes=[YT[16 + n][u] for u in allu], owner=OB[i])
            ada_step(l + 1, 6)

    def phase_attn(l, need_ctx):
        A.reset()
        kq = [A.bf16(3 * NT) for _ in range(2)]
        KQ = [CB("kqv%d" % i) for i in range(2)]
        ya = [A.bf16(NT) for _ in range(2)]
        YA = [CB("ya%d" % i) for i in range(2)]
        pe = [A.bf16(512) for _ in range(4)]
        PEX = [CB("pex%d" % i) for i in range(4)]
        rc = [A.f32(512) for _ in range(2)]
        oo = [A.f32(512) for _ in range(2)]
        osum, osq, orr = A.f32(512), A.f32(512), A.f32(512)
        EP = CB("ep")
        P.phase_begin(KQ + YA + PEX + [EP])
        allu = range(NU)
        neglam = lamv[:, l * 4:l * 4 + 1]
        gs = lamv[:, l * 4 + 1:l * 4 + 2]
        qtiles = []
        if CTX and need_ctx:
            qtiles.append((0, CTX, CTX // 128))
        for t0 in range(CTX, NT, 512):
            qtiles.append((t0, min(512, NT - t0), NCH))
        pe_i = [0]
        pending = [None]

        def head_body(h):
            i = h % 2
            k3 = kq[i].rearrange("p (k t) -> p k t", k=3)
            kS, qS = k3[:, 0, :], k3[:, 1, :]
            vv = k3[:, 2, :].rearrange("p (c e) -> p c e", e=128)
            P.dma(SP, lambda e, kS=kS, qS=qS, k3=k3, h=h: [
                e.dma_start(out=kS, in_=pT[C_K + h]), e.dma_start(out=qS, in_=pT[C_Q + h]),
                e.dma_start(out=k3[:, 2, :], in_=vS[h].rearrange("p c e -> p (c e)"))],
                reads=[PT[C_K + h][u] for u in allu] + [PT[C_Q + h][u] for u in allu] + [VS[h][u] for u in allu],
                writes=[KQ[i]], owner=KQ[i], ndma=3)
            def qtile_body(t0, w, nk):
                def S_op(j, c, t0=t0, w=w):
                    b = (2 * j + c) % 4
                    P.op(PE, lambda e: e.matmul(ps[b][:, :w], lhsT=kS[c * 64:(c + 1) * 64, j * 128:(j + 1) * 128],
                                                rhs=qS[c * 64:(c + 1) * 64, t0:t0 + w], start=True, stop=True),
                         reads=[KQ[i]], writes=[PSB[b]])
                    pi = pe_i[0] % 4
                    pe_i[0] += 1
                    P.op(ACT, lambda e: e.activation(out=pe[pi][:, :w], in_=ps[b][:, :w], func=AF.Exp, scale=0.125),
                         reads=[PSB[b]], writes=[PEX[pi]])
                    return pi

                def PV_op(j, c, pi, nk=nk, w=w):
                    def fn(e):
                        e.matmul(ps[4 + c][:, :w], lhsT=vv[:, j, :], rhs=pe[pi][:, :w], start=(j == 0), stop=(j == nk - 1))
                        return e.matmul(ps[6 + c][:, :w], lhsT=ones16, rhs=pe[pi][:, :w], start=(j == 0),
                                        stop=(j == nk - 1))
                    P.op(PE, fn, reads=[KQ[i], PEX[pi], CONST], writes=[PSB[4 + c], PSB[6 + c]])
                cur = [S_op(0, 0), S_op(0, 1)]
                for j in range(nk):
                    nxt = None
                    if j + 1 < nk:
                        nxt = [S_op(j + 1, 0), S_op(j + 1, 1)]
                    PV_op(j, 0, cur[0])
                    PV_op(j, 1, cur[1])
                    cur = nxt
                    if j == min(2, nk - 1) and pending[0] is not None:
                        pending[0]()
                        pending[0] = None
                for c in range(2):
                    P.op(DVE, lambda e, c=c, w=w: e.reciprocal(out=rc[c][:, :w], in_=ps[6 + c][:, :w]),
                         reads=[PSB[6 + c]], writes=[EP])
                    P.op(DVE, lambda e, c=c, w=w: e.tensor_tensor(out=oo[c][:, :w], in0=ps[4 + c][:, :w],
                                                                  in1=rc[c][:, :w], op=ALU.mult),
                         reads=[PSB[4 + c], EP], writes=[EP])
                P.op(DVE, lambda e, w=w: e.scalar_tensor_tensor(out=osum[:, :w], in0=oo[1][:, :w], scalar=neglam,
                                                                in1=oo[0][:, :w], op0=ALU.mult, op1=ALU.add),
                     reads=[EP, MODV], writes=[EP])
                P.op(POOL, lambda e, w=w: e.tensor_tensor(out=osq[:, :w], in0=osum[:, :w], in1=osum[:, :w], op=ALU.mult),
                     reads=[EP], writes=[EP])

                def part_b(w=w, t0=t0, i=i):
                    P.op(PE, lambda e: e.matmul(ps[0][:, :w], lhsT=ones32, rhs=osq[:, :w], start=True, stop=True),
                         reads=[EP, CONST], writes=[PSB[0]])
                    rstd_from(ps, 0, w, orr, EP, 128)
                    P.op(DVE, lambda e: e.scalar_tensor_tensor(
                        out=ya[i][:, t0:t0 + w], in0=osum[:, :w], scalar=gs, in1=orr[:, :w], op0=ALU.mult, op1=ALU.mult),
                        reads=[EP, MODV], writes=[YA[i]])
                pending[0] = part_b
            for (t0, w, nk) in qtiles:
                qtile_body(t0, w, nk)
            if pending[0] is not None:
                pending[0]()
                pending[0] = None
            q0 = 0 if (need_ctx or not CTX) else CTX
            P.dma(POOL, lambda e, i=i, h=h, q0=q0: [e.dma_start(out=yT[h][:, q0:NT], in_=ya[i][:, q0:NT])],
                  reads=[YA[i]], writes=[YT[h][u] for u in units(q0, NT - q0)], owner=YA[i])

        for h in range(HEADS):
            head_body(h)

    def phase_mf(l, need_ctx, last):
        A.reset()
        W = 512
        aT = A.bf16(NJ * W)
        AT = [CB("aT%d" % c) for c in range(NJ)]
        gS = [A.bf16(3 * W) for _ in range(3)]
        GS = [CB("gS%d" % i) for i in range(3)]
        hc = [A.f32(W) for _ in range(4)]
        HC = [CB("hc%d" % i) for i in range(4)]
        mT = A.bf16(NCK * W)
        MT = [CB("mT%d" % c) for c in range(NCK)]
        X = A.f32(NCK * W)
        XB = [CB("X_%d" % c) for c in range(NCK)]
        rs = A.f32(W)
        RS = CB("mrs")
        sq = [A.f32(W) for _ in range(2)]
        SQ = [CB("msq%d" % i) for i in range(2)]
        NG = 4
        gsig = [A.f32(W) for _ in range(NG)]
        GSG = [CB("gsig%d" % i) for i in range(NG)]
        tt = [A.f32(W) for _ in range(6)]
        TTB = [CB("tt%d" % i) for i in range(6)]
        sgb = [A.f32(W) for _ in range(2)]
        SGB = [CB("sg%d" % i) for i in range(2)]
        ost = [A.f32(D)] if last else []
        OST = [CB("ost0")] if last else []
        P.phase_begin(AT + GS + HC + MT + XB + [RS] + SQ + GSG + TTB + SGB + OST)
        mT3 = mT.rearrange("p (c t) -> p c t", c=NCK)
        X3 = X.rearrange("p (c t) -> p c t", c=NCK)
        aT3 = aT.rearrange("p (c t) -> p c t", c=NJ)
        tiles = []
        if CTX and need_ctx:
            for t0 in range(0, CTX, W):
                tiles.append((t0, min(W, CTX - t0)))
        for t0 in range(CTX, NT, W):
            tiles.append((t0, min(W, NT - t0)))
        cnt = {"g": 0, "gs": 0, "tt": 0, "sq": 0, "sg": 0, "hc": 0}

        def sumsq_chain(w):
            for c in range(NCK):
                k = cnt["sq"] % 2
                cnt["sq"] += 1
                P.op(POOL, lambda e, c=c, k=k: e.tensor_tensor(out=sq[k][:, :w], in0=X3[:, c, :w], in1=X3[:, c, :w],
                                                               op=ALU.mult), reads=[XB[c]], writes=[SQ[k]])
                P.op(PE, lambda e, c=c, k=k: e.matmul(ps[7][:, :w], lhsT=ones32, rhs=sq[k][:, :w], start=(c == 0),
                                                      stop=(c == NCK - 1)), reads=[SQ[k], CONST], writes=[PSB[7]])
            rstd_from(ps, 7, w, rs, RS, D)

        def residual(kind, j, t0, w, us):
            for n in range(NCK):
                hk = cnt["hc"] % 4
                cnt["hc"] += 1
                P.dma(SP, lambda e, hk=hk, n=n: [e.dma_start(out=hc[hk][:, :w], in_=hT[n, :, t0:t0 + w])],
                      reads=[HT[u] for u in us], writes=[HC[hk]], owner=HC[hk])
                k = cnt["tt"] % 6
                cnt["tt"] += 1
                P.op(DVE, lambda e, n=n, k=k: e.scalar_tensor_tensor(
                    out=tt[k][:, :w], in0=X3[:, n, :w], scalar=mod(l, kind, j, n), in1=rs[:, :w],
                    op0=ALU.mult, op1=ALU.mult), reads=[XB[n], RS, MODV], writes=[TTB[k]])
                P.op(POOL, lambda e, n=n, k=k, hk=hk: e.tensor_tensor(out=X3[:, n, :w], in0=hc[hk][:, :w],
                                                                      in1=tt[k][:, :w], op=ALU.add),
                     reads=[HC[hk], TTB[k]], writes=[XB[n]])

        def store_h(t0, w, us):
            P.dma(POOL, lambda e: [e.dma_start(out=hT[:, :, t0:t0 + w].rearrange("c p t -> p c t"),
                                               in_=X3[:, :, 0:w])],
                  reads=XB, writes=[HT[u] for u in us], owner=XB[0])

        def mf_tile(ti, t0, w):
            j = 1 if t0 < CTX else 0
            us = list(units(t0, w))
            P.dma(SP, lambda e: [e.dma_start(out=aT3[:, 0:24, 0:w],
                                             in_=yT[:, :, t0:t0 + w].rearrange("c p t -> p c t"))],
                  reads=[YT[c][u] for c in range(24) for u in us], writes=AT[0:24], owner=AT[0])
            for q in range(8):
                s = load_w(l, OFF_BR + q * 786432, 6144)
                w4 = wslots[s][:, 0:6144].rearrange("p (r k n) -> p r k n", r=3, k=8)
                for i2 in range(2):
                    n = q * 2 + i2
                    gi = cnt["g"] % 3
                    cnt["g"] += 1
                    g3 = gS[gi].rearrange("p (r t) -> p r t", r=3)
                    P.dma(SP, lambda e, g3=g3, n=n: [e.dma_start(
                        out=g3[:, :, 0:w],
                        in_=pT[C_MG + n:C_MG + n + 33:16, :, t0:t0 + w].rearrange("r p t -> p r t"))],
                        reads=[PT[C_MG + r * 16 + n][u] for r in range(3) for u in us], writes=[GS[gi]], owner=GS[gi])
                    tks = []
                    for r in range(3):
                        b = next_bank()

                        def fn(e, r=r, b=b, i2=i2, w4=w4):
                            for kc in range(8):
                                x = e.matmul(ps[b][:, :w], lhsT=w4[:, r, kc, i2 * 128:(i2 + 1) * 128],
                                             rhs=aT3[:, r * 8 + kc, :w], start=(kc == 0), stop=(kc == 7))
                            return x
                        P.op(PE, fn, reads=[WS[s]] + AT[r * 8:(r + 1) * 8], writes=[PSB[b]])
                        k = cnt["gs"] % NG
                        cnt["gs"] += 1
                        P.op(ACT, lambda e, k=k, g3=g3, r=r, n=n: e.activation(
                            out=gsig[k][:, :w], in_=g3[:, r, :w], func=AF.Sigmoid,
                            bias=vcol(l, V_BM + r * 16 + n), scale=1.0), reads=[GS[gi], CONST], writes=[GSG[k]])
                        tk = cnt["tt"] % 6
                        cnt["tt"] += 1
                        P.op(DVE, lambda e, tk=tk, k=k, b=b: e.tensor_tensor(out=tt[tk][:, :w], in0=ps[b][:, :w],
                                                                             in1=gsig[k][:, :w], op=ALU.mult),
                             reads=[PSB[b], GSG[k]], writes=[TTB[tk]])
                        tks.append(tk)
                    P.op(POOL, lambda e, tks=tks: e.tensor_tensor(out=tt[tks[0]][:, :w], in0=tt[tks[0]][:, :w],
                                                                 in1=tt[tks[1]][:, :w], op=ALU.add),
                         reads=[TTB[tks[0]], TTB[tks[1]]], writes=[TTB[tks[0]]])
                    P.op(POOL, lambda e, tks=tks, n=n: e.tensor_tensor(out=mT3[:, n, :w], in0=tt[tks[0]][:, :w],
                                                                      in1=tt[tks[2]][:, :w], op=ALU.add),
                         reads=[TTB[tks[0]], TTB[tks[2]]], writes=[MT[n]])
            for g in range(4):
                s = load_w(l, OFF_WO + g * 1048576, 8192)
                w3 = wslots[s].rearrange("p (k n) -> p k n", k=NCK)
                for i4 in range(4):
                    n = g * 4 + i4
                    b = next_bank()

                    def fn(e, b=b, i4=i4, w3=w3):
                        for kc in range(NCK):
                            x = e.matmul(ps[b][:, :w], lhsT=w3[:, kc, i4 * 128:(i4 + 1) * 128], rhs=mT3[:, kc, :w],
                                         start=(kc == 0), stop=(kc == NCK - 1))
                        return x
                    P.op(PE, fn, reads=[WS[s]] + MT, writes=[PSB[b]])
                    P.op(ACT, lambda e, b=b, n=n: e.activation(out=X3[:, n, :w], in_=ps[b][:, :w], func=AF.Copy),
                         reads=[PSB[b]], writes=[XB[n]])
            sumsq_chain(w)
            residual(K_G2, j, t0, w, us)
            store_h(t0, w, us)
            sumsq_chain(w)
            for c in range(NCK):
                k = cnt["tt"] % 6
                cnt["tt"] += 1
                P.op(DVE, lambda e, c=c, k=k: e.tensor_tensor(out=tt[k][:, :w], in0=X3[:, c, :w], in1=rs[:, :w],
                                                              op=ALU.mult), reads=[XB[c], RS], writes=[TTB[k]])
                P.op(ACT, lambda e, c=c, k=k: e.activation(out=mT3[:, c, :w], in_=tt[k][:, :w], func=AF.Identity,
                                                           scale=mod(l, K_A3, j, c), bias=mod(l, K_B3, j, c)),
                     reads=[TTB[k], MODV], writes=[MT[c]])
            for jb in range(22):
                s = load_w(l, OFF_GU + jb * 1048576, 8192)
                w5 = wslots[s].rearrange("p (a g k n) -> p a g k n", a=2, g=2, k=NCK)
                for jj in range(2):
                    jx = jb * 2 + jj
                    bg = next_bank()
                    bu = next_bank()
                    for (bb, gu) in ((bg, 0), (bu, 1)):
                        def fn(e, bb=bb, gu=gu, jj=jj, w5=w5):
                            for kc in range(NCK):
                                x = e.matmul(ps[bb][:, :w], lhsT=w5[:, jj, gu, kc, :], rhs=mT3[:, kc, :w],
                                             start=(kc == 0), stop=(kc == NCK - 1))
                            return x
                        P.op(PE, fn, reads=[WS[s]] + MT, writes=[PSB[bb]])
                    k = cnt["sg"] % 2
                    cnt["sg"] += 1
                    P.op(ACT, lambda e, k=k, bg=bg: e.activation(out=sgb[k][:, :w], in_=ps[bg][:, :w], func=AF.Silu),
                         reads=[PSB[bg]], writes=[SGB[k]])
                    P.op(DVE, lambda e, k=k, bu=bu, jx=jx: e.tensor_tensor(out=aT3[:, jx, :w], in0=ps[bu][:, :w],
                                                                           in1=sgb[k][:, :w], op=ALU.mult),
                         reads=[PSB[bu], SGB[k]], writes=[AT[jx]])
            for n in range(NCK):
                s = load_w(l, OFF_WD + n * 720896, 5632)
                w3 = wslots[s][:, 0:5632].rearrange("p (k n) -> p k n", k=NJ)
                b = next_bank()

                def fn(e, b=b, w3=w3):
                    for kc in range(NJ):
                        x = e.matmul(ps[b][:, :w], lhsT=w3[:, kc, :], rhs=aT3[:, kc, :w], start=(kc == 0),
                                     stop=(kc == NJ - 1))
                    return x
                P.op(PE, fn, reads=[WS[s]] + AT, writes=[PSB[b]])
                P.op(ACT, lambda e, b=b, n=n: e.activation(out=X3[:, n, :w], in_=ps[b][:, :w], func=AF.Copy),
                     reads=[PSB[b]], writes=[XB[n]])
            sumsq_chain(w)
            residual(K_G5, j, t0, w, us)
            if not last:
                store_h(t0, w, us)
            else:
                for tb in range(w // 128):
                    for cg in range(4):
                        b = next_bank()

                        def tr(e, b=b, cg=cg, tb=tb):
                            for i4 in range(4):
                                c = cg * 4 + i4
                                x = e.transpose(ps[b][:, i4 * 128:(i4 + 1) * 128], X3[:, c, tb * 128:(tb + 1) * 128],
                                                ident)
                            return x
                        P.op(PE, tr, reads=[XB[cg * 4 + i4] for i4 in range(4)] + [CONST], writes=[PSB[b]])
                        if cg % 2 == 0:
                            P.op(ACT, lambda e, b=b, cg=cg: e.activation(
                                out=ost[0][:, cg * 512:(cg + 1) * 512], in_=ps[b][:, :], func=AF.Copy),
                                reads=[PSB[b]], writes=[OST[0]])
                        else:
                            P.op(DVE, lambda e, b=b, cg=cg: e.tensor_copy(
                                out=ost[0][:, cg * 512:(cg + 1) * 512], in_=ps[b][:, :]),
                                reads=[PSB[b]], writes=[OST[0]])
                    r0 = t0 - CTX + tb * 128
                    P.dma(POOL, lambda e, r0=r0: [e.dma_start(out=out[r0:r0 + 128, :], in_=ost[0])],
                          reads=[OST[0]], writes=[OUTB], owner=OST[0])

        for ti, (t0, w) in enumerate(tiles):
            mf_tile(ti, t0, w)

    prologue()
    cast_weights(0)
    phase0()
    ada_setup()
    for l in range(L):
        ada_gens[l] = ada_layer_gen(l)
    ada_finish(0)
    for l in range(L):
        last = (l == L - 1)
        need_ctx = not last
        phase_p1(l)
        if l + 1 < L:
            cast_weights(l + 1)
        phase_conv(l)
        phase_rnn(l)
        ada_finish(l + 1)
        phase_attn(l, need_ctx)
        phase_mf(l, need_ctx, last)
    snap = P.snapshot()
    fin = P.op(SP, None)
    fin.deps = snap
    P.emit(nc)
    return nc, P


def _vec16(v):
    return np.ascontiguousarray(v.reshape(-1, 128).T)


def pack_layer_weights(w_in, wa, wb, wc, wo, wg, wu, wd):
    out = np.empty(WTOT, np.float32)
    idx = np.arange(1024).reshape(8, 2, 2, 2, 16)
    sw = idx[:, :, :, ::-1, :].reshape(-1)

    def blk(cols):
        m = w_in[:, cols]
        return m.reshape(16, 128, 512).transpose(1, 0, 2).reshape(-1)
    o = OFF_IN
    blocks = []
    for base in (0, 1024):
        for g in range(2):
            c = np.arange(g * 512, (g + 1) * 512)
            blocks.append(base + c)
            blocks.append(base + sw[c])
    for vb in range(2):
        blocks.append(2048 + np.arange(vb * 512, (vb + 1) * 512))
    for pb in range(22):
        blocks.append(3072 + np.arange(pb * 512, (pb + 1) * 512))
    assert len(blocks) == 32
    for cols in blocks:
        out[o:o + 1048576] = blk(cols)
        o += 1048576
    assert o == OFF_BR
    for q in range(8):
        parts = []
        for wr in (wa, wb, wc):
            m = wr[:, q * 256:(q + 1) * 256].reshape(8, 128, 256).transpose(1, 0, 2)
            parts.append(m)
        out[o:o + 786432] = np.stack(parts, axis=1).reshape(-1)
        o += 786432
    assert o == OFF_WO
    for g in range(4):
        out[o:o + 1048576] = wo[:, g * 512:(g + 1) * 512].reshape(16, 128, 512).transpose(1, 0, 2).reshape(-1)
        o += 1048576
    assert o == OFF_GU
    for jb in range(22):
        parts = []
        for jj in range(2):
            jx = jb * 2 + jj
            gg = wg[:, jx * 128:(jx + 1) * 128].reshape(16, 128, 128).transpose(1, 0, 2)
            uu = wu[:, jx * 128:(jx + 1) * 128].reshape(16, 128, 128).transpose(1, 0, 2)
            parts.append(np.stack([gg, uu], axis=1))
        out[o:o + 1048576] = np.stack(parts, axis=1).reshape(-1)
        o += 1048576
    assert o == OFF_WD
    for n in range(16):
        out[o:o + 720896] = wd[:, n * 128:(n + 1) * 128].reshape(44, 128, 128).transpose(1, 0, 2).reshape(-1)
        o += 720896
    assert o == WTOT
    return out.reshape(WTOT // 2048, 2048)


def pack_vecs(l, inp):
    v = np.zeros((128, NV), np.float32)
    v[:, V_BADA:V_BADA + 96] = np.concatenate([_vec16(inp["b_ada"][l, k]) for k in range(6)], axis=1)
    v[:, V_GPM:V_GPM + 16] = _vec16(inp["g_pre_mix"][l])
    v[:, V_GQM:V_GQM + 16] = _vec16(inp["g_post_mix"][l])
    v[:, V_GPF:V_GPF + 16] = _vec16(inp["g_pre_ffn"][l])
    v[:, V_GQF:V_GQF + 16] = _vec16(inp["g_post_ffn"][l])
    v[:, V_CW:V_CW + 24] = np.concatenate([_vec16(inp["conv_w"][l, k]) for k in range(3)], axis=1)
    v[:, V_CB:V_CB + 8] = _vec16(inp["conv_b"][l])
    v[:, V_RW:V_RW + 32] = np.concatenate([_vec16(inp["rnn_conv_w"][l, k]) for k in range(4)], axis=1)
    v[:, V_RB:V_RB + 8] = _vec16(inp["rnn_conv_b"][l])
    v[:, V_BA:V_BA + 16] = np.concatenate([_vec16(inp["rg_ba"][l, d]) for d in range(2)], axis=1)
    v[:, V_BX:V_BX + 16] = np.concatenate([_vec16(inp["rg_bx"][l, d]) for d in range(2)], axis=1)
    v[:, V_LAM:V_LAM + 16] = np.concatenate([_vec16(inp["rg_lambda"][l, d]) for d in range(2)], axis=1)
    v[:, V_BM:V_BM + 48] = np.concatenate([_vec16(inp["b_merge"][l, r]) for r in range(3)], axis=1)
    v[:, V_SUB] = inp["diff_subln"][l]
    return v


def rope_tables(S):
    t = np.arange(S)
    row = (t // 64).astype(np.float32)
    col = (t % 64).astype(np.float32)
    inv = (np.float32(10000.0) ** (-np.arange(16, dtype=np.float32) / np.float32(16))).astype(np.float32)
    tab = np.zeros((2, 128, S), np.float32)
    for p in range(128):
        r = p % 64
        axis, half, f = r // 32, (r % 32) // 16, r % 16
        pos = row if axis == 0 else col
        ang = (pos * inv[f]).astype(np.float32)
        tab[0, p] = np.cos(ang.astype(np.float64))
        tab[1, p] = np.sin(ang.astype(np.float64)) * (-1.0 if half == 0 else 1.0)
    return tab


def prepare_shared(inp, L):
    sh = {}
    sh["wada"] = np.ascontiguousarray(
        inp["w_ada"][:L].reshape(L, 16, 128, 48, 256).transpose(0, 3, 2, 1, 4)).reshape(L, 48, 128, 16 * 256)
    sh["wpack"] = np.stack([pack_layer_weights(inp["w_in"][l], inp["w_branch_a"][l], inp["w_branch_b"][l],
                                               inp["w_branch_c"][l], inp["w_o"][l], inp["w_ffn_gate"][l],
                                               inp["w_ffn_up"][l], inp["w_ffn_down"][l]) for l in range(L)])
    rg = np.stack([inp["rg_wa"][:L], inp["rg_wx"][:L]], axis=1)
    sh["rgw"] = np.ascontiguousarray(rg.transpose(0, 4, 1, 2, 3, 5)).reshape(L, 128, 32 * 128)
    sh["vecs"] = np.ascontiguousarray(np.stack([pack_vecs(l, inp) for l in range(L)], axis=1)).reshape(128, L * NV)
    sh["dlam"] = np.ascontiguousarray(inp["diff_lambda"][:L]).reshape(-1)
    sh["ident"] = np.eye(128, dtype=np.float32)
    return sh


_CACHE = {}


def run(inp, S, CTX, L, B, trace=False):
    inp = {k: np.asarray(v) for k, v in inp.items()}
    key = (S, CTX, L)
    if key not in _CACHE:
        _CACHE[key] = build_program(S, CTX, L)
    nc, _ = _CACHE[key]
    sh = prepare_shared(inp, L)
    sh["rope"] = rope_tables(S)
    in_maps = []
    for b in range(B):
        m = dict(sh)
        m["xb"] = np.ascontiguousarray(np.concatenate([inp["ctx"][b], inp["x"][b]], axis=0))
        m["cvec"] = np.ascontiguousarray(np.concatenate([_vec16(inp["c"][b]), _vec16(inp["c_ctx"])], axis=1))
        in_maps.append(m)
    res = run_bass_kernel_spmd(nc, in_maps, core_ids=list(range(B)), trace=trace)
    outp = np.stack([np.asarray(r["out"]) for r in res.results], axis=0)
    return outp.astype(np.float32), res


def kernel(**inputs):
    outp, _ = run(inputs, 4096, 256, 4, 8)
    return outp
```
